# Optimizing a Trainium2 kernel written in Bass

```python
import math
import jax, jax.numpy as jnp
from jax import lax
import numpy as np

D_MODEL = 1024
BATCH = 8
SEQ = 2048
DEPTH = 2
DEC_BATCH = 128
DEC_SEQ = 4
PAST_LEN = 16384
PAGE_SIZE = 128

D_MIX = D_MODEL
D_RNN = D_MIX // 2
RNN_HEADS = 8
RNN_HEAD_DIM = D_RNN // RNN_HEADS
RG_CONV_W = 4
LRU_C = 8.0
D_CONF = D_MIX - D_RNN
CF_CONV_W = 31
D_IN = 2 * D_RNN + 2 * D_CONF
N_MEM = 256
MEM_HEADS = 4
MEM_HEAD_DIM = D_MODEL // MEM_HEADS
D_FF = ((8 * D_MODEL // 3 + 255) // 256) * 256
EPS = 1e-6

kernel_name = "hymba_rglru_conformer_memxattn_step"


def _rmsnorm(x, g):
    xf = x.astype(jnp.float32)
    y = xf * lax.rsqrt(jnp.mean(xf * xf, axis=-1, keepdims=True) + EPS)
    return (y * g.astype(jnp.float32)).astype(x.dtype)


def _layernorm(x, g, b):
    xf = x.astype(jnp.float32)
    mu = jnp.mean(xf, axis=-1, keepdims=True)
    xc = xf - mu
    var = jnp.mean(xc * xc, axis=-1, keepdims=True)
    return (xc * lax.rsqrt(var + EPS) * g.astype(jnp.float32) + b.astype(jnp.float32)).astype(x.dtype)


def _causal_dwconv(x, buf, w, b):
    k = w.shape[0]
    xp = jnp.concatenate([buf.astype(x.dtype), x], axis=1)
    y = lax.conv_general_dilated(
        xp, w[:, None, :].astype(x.dtype), window_strides=(1,), padding='VALID',
        dimension_numbers=('NWC', 'WIO', 'NWC'), feature_group_count=x.shape[-1])
    return y + b.astype(x.dtype), xp[:, -(k - 1):]


def _rglru(x, h0, wa, ba, wx, bx, lam):
    bsz, t = x.shape[0], x.shape[1]
    f32 = jnp.float32
    xf = x.astype(f32)
    xh = xf.reshape(bsz, t, RNN_HEADS, RNN_HEAD_DIM)
    r = jax.nn.sigmoid(jnp.einsum('bthi,hij->bthj', xh, wa.astype(f32)).reshape(bsz, t, D_RNN) + ba.astype(f32))
    i = jax.nn.sigmoid(jnp.einsum('bthi,hij->bthj', xh, wx.astype(f32)).reshape(bsz, t, D_RNN) + bx.astype(f32))
    log_a = -LRU_C * r * jax.nn.softplus(-lam.astype(f32))
    a = jnp.exp(log_a)
    mult = jnp.sqrt(-jnp.expm1(2.0 * log_a))
    bterm = mult * (i * xf)
    bterm = bterm.at[:, 0].add(a[:, 0] * h0.astype(f32))

    def combine(left, right):
        a_l, b_l = left
        a_r, b_r = right
        return a_l * a_r, a_r * b_l + b_r

    _, h = lax.associative_scan(combine, (a, bterm), axis=1)
    return h.astype(x.dtype), h[:, -1].astype(h0.dtype)


def _mem_kv(mem, g, wk, wv):
    bsz = mem.shape[0]
    m = _rmsnorm(mem, g)
    k = (m @ wk).reshape(bsz, N_MEM, MEM_HEADS, MEM_HEAD_DIM)
    v = (m @ wv).reshape(bsz, N_MEM, MEM_HEADS, MEM_HEAD_DIM)
    return k, v


def _cross_attn(x, k, v, wq, wo):
    bsz, t = x.shape[0], x.shape[1]
    q = (x @ wq).reshape(bsz, t, MEM_HEADS, MEM_HEAD_DIM).astype(jnp.float32)
    s = jnp.einsum('bthd,bmhd->bhtm', q, k.astype(jnp.float32)) * (MEM_HEAD_DIM ** -0.5)
    p = jax.nn.softmax(s, axis=-1)
    o = jnp.einsum('bhtm,bmhd->bthd', p, v.astype(jnp.float32)).astype(x.dtype)
    return o.reshape(bsz, t, MEM_HEADS * MEM_HEAD_DIM) @ wo


def _layer(x, mem_k, mem_v, h0, rg_buf, cf_buf, lw):
    u = _rmsnorm(x, lw['norm_mix_g']) @ lw['w_in']
    xr = u[..., :D_RNN]
    gr = u[..., D_RNN:2 * D_RNN]
    cv = u[..., 2 * D_RNN:2 * D_RNN + D_CONF]
    cg = u[..., 2 * D_RNN + D_CONF:]
    xr_c, rg_buf_new = _causal_dwconv(xr, rg_buf, lw['rg_conv_w'], lw['rg_conv_b'])
    h, h_last = _rglru(xr_c, h0, lw['rg_wa'], lw['rg_ba'], lw['rg_wx'], lw['rg_bx'], lw['rg_lambda'])
    y_r = h * jax.nn.gelu(gr)
    c = cv * jax.nn.sigmoid(cg)
    c_c, cf_buf_new = _causal_dwconv(c, cf_buf, lw['cf_conv_w'], lw['cf_conv_b'])
    y_c = jax.nn.silu(_layernorm(c_c, lw['cf_ln_g'], lw['cf_ln_b']))
    x = x + jnp.concatenate([y_r, y_c], axis=-1) @ lw['w_out']
    x = x + _cross_attn(_rmsnorm(x, lw['norm_attn_g']), mem_k, mem_v, lw['w_q'], lw['w_o'])
    z = _rmsnorm(x, lw['norm_ffn_g'])
    x = x + (jax.nn.silu(z @ lw['w_gate']) * (z @ lw['w_up'])) @ lw['w_down']
    return x, h_last, rg_buf_new, cf_buf_new


def setup_inputs(seed: int = 0) -> dict:
    key = jax.random.key(seed)
    ks = iter(jax.random.split(key, 40))
    f32 = jnp.float32

    def nrm(shape, scale):
        return jax.random.normal(next(ks), shape, f32) * scale

    def gain(shape):
        return 1.0 + nrm(shape, 0.02)

    u = jax.random.uniform(next(ks), (DEPTH, D_RNN), f32, 0.9, 0.999)
    s = u ** (1.0 / LRU_C)
    rg_lambda = jnp.log(s) - jnp.log1p(-s)
    return {
        'x_prompt': nrm((BATCH, SEQ, D_MODEL), 1.0),
        'x_sample': nrm((DEC_BATCH, DEC_SEQ, D_MODEL), 1.0),
        'state_rglru_h': nrm((DEPTH, DEC_BATCH, D_RNN), 0.5),
        'state_rglru_conv': nrm((DEPTH, DEC_BATCH, RG_CONV_W - 1, D_RNN), 1.0),
        'state_conf_conv': nrm((DEPTH, DEC_BATCH, CF_CONV_W - 1, D_CONF), 1.0),
        'cache_mem_k': nrm((DEPTH, DEC_BATCH, N_MEM, MEM_HEADS, MEM_HEAD_DIM), 1.0),
        'cache_mem_v': nrm((DEPTH, DEC_BATCH, N_MEM, MEM_HEADS, MEM_HEAD_DIM), 1.0),
        'mem_prompt': nrm((BATCH, N_MEM, D_MODEL), 1.0),
        'norm_mix_g': gain((DEPTH, D_MODEL)),
        'w_in': nrm((DEPTH, D_MODEL, D_IN), D_MODEL ** -0.5),
        'rg_conv_w': nrm((DEPTH, RG_CONV_W, D_RNN), RG_CONV_W ** -0.5),
        'rg_conv_b': nrm((DEPTH, D_RNN), 0.02),
        'rg_wa': nrm((DEPTH, RNN_HEADS, RNN_HEAD_DIM, RNN_HEAD_DIM), RNN_HEAD_DIM ** -0.5),
        'rg_ba': nrm((DEPTH, D_RNN), 0.02),
        'rg_wx': nrm((DEPTH, RNN_HEADS, RNN_HEAD_DIM, RNN_HEAD_DIM), RNN_HEAD_DIM ** -0.5),
        'rg_bx': nrm((DEPTH, D_RNN), 0.02),
        'rg_lambda': rg_lambda,
        'cf_conv_w': nrm((DEPTH, CF_CONV_W, D_CONF), CF_CONV_W ** -0.5),
        'cf_conv_b': nrm((DEPTH, D_CONF), 0.02),
        'cf_ln_g': gain((DEPTH, D_CONF)),
        'cf_ln_b': nrm((DEPTH, D_CONF), 0.02),
        'w_out': nrm((DEPTH, D_MIX, D_MODEL), D_MIX ** -0.5),
        'norm_attn_g': gain((DEPTH, D_MODEL)),
        'norm_mem_g': gain((DEPTH, D_MODEL)),
        'w_q': nrm((DEPTH, D_MODEL, MEM_HEADS * MEM_HEAD_DIM), D_MODEL ** -0.5),
        'w_k': nrm((DEPTH, D_MODEL, MEM_HEADS * MEM_HEAD_DIM), D_MODEL ** -0.5),
        'w_v': nrm((DEPTH, D_MODEL, MEM_HEADS * MEM_HEAD_DIM), D_MODEL ** -0.5),
        'w_o': nrm((DEPTH, MEM_HEADS * MEM_HEAD_DIM, D_MODEL), (MEM_HEADS * MEM_HEAD_DIM) ** -0.5),
        'norm_ffn_g': gain((DEPTH, D_MODEL)),
        'w_gate': nrm((DEPTH, D_MODEL, D_FF), D_MODEL ** -0.5),
        'w_up': nrm((DEPTH, D_MODEL, D_FF), D_MODEL ** -0.5),
        'w_down': nrm((DEPTH, D_FF, D_MODEL), D_FF ** -0.5),
        'norm_final_g': gain((D_MODEL,)),
    }


def reference(x_prompt, x_sample, state_rglru_h, state_rglru_conv, state_conf_conv,
              cache_mem_k, cache_mem_v, mem_prompt,
              norm_mix_g, w_in, rg_conv_w, rg_conv_b, rg_wa, rg_ba, rg_wx, rg_bx, rg_lambda,
              cf_conv_w, cf_conv_b, cf_ln_g, cf_ln_b, w_out,
              norm_attn_g, norm_mem_g, w_q, w_k, w_v, w_o,
              norm_ffn_g, w_gate, w_up, w_down, norm_final_g):
    bp = x_prompt.shape[0]
    xp = x_prompt
    xs = x_sample
    p_h, p_rg, p_cf, p_mk, p_mv = [], [], [], [], []
    s_h, s_rg, s_cf = [], [], []
    for l in range(DEPTH):
        lw = {
            'norm_mix_g': norm_mix_g[l], 'w_in': w_in[l],
            'rg_conv_w': rg_conv_w[l], 'rg_conv_b': rg_conv_b[l],
            'rg_wa': rg_wa[l], 'rg_ba': rg_ba[l], 'rg_wx': rg_wx[l], 'rg_bx': rg_bx[l],
            'rg_lambda': rg_lambda[l],
            'cf_conv_w': cf_conv_w[l], 'cf_conv_b': cf_conv_b[l],
            'cf_ln_g': cf_ln_g[l], 'cf_ln_b': cf_ln_b[l], 'w_out': w_out[l],
            'norm_attn_g': norm_attn_g[l], 'w_q': w_q[l], 'w_o': w_o[l],
            'norm_ffn_g': norm_ffn_g[l], 'w_gate': w_gate[l], 'w_up': w_up[l], 'w_down': w_down[l],
        }
        mk, mv = _mem_kv(mem_prompt, norm_mem_g[l], w_k[l], w_v[l])
        h0 = jnp.zeros((bp, D_RNN), state_rglru_h.dtype)
        rb0 = jnp.zeros((bp, RG_CONV_W - 1, D_RNN), state_rglru_conv.dtype)
        cb0 = jnp.zeros((bp, CF_CONV_W - 1, D_CONF), state_conf_conv.dtype)
        xp, hp, rbp, cbp = _layer(xp, mk, mv, h0, rb0, cb0, lw)
        p_h.append(hp); p_rg.append(rbp); p_cf.append(cbp); p_mk.append(mk); p_mv.append(mv)
        xs, hs, rbs, cbs = _layer(xs, cache_mem_k[l], cache_mem_v[l], state_rglru_h[l],
                                  state_rglru_conv[l], state_conf_conv[l], lw)
        s_h.append(hs); s_rg.append(rbs); s_cf.append(cbs)
    y_prompt = _rmsnorm(xp, norm_final_g)
    y_sample = _rmsnorm(xs, norm_final_g)
    return (y_prompt, y_sample,
            jnp.stack(p_h), jnp.stack(p_rg), jnp.stack(p_cf), jnp.stack(p_mk), jnp.stack(p_mv),
            jnp.stack(s_h), jnp.stack(s_rg), jnp.stack(s_cf))
```

```python
import itertools
import numpy as np
import concourse.bass as bass
import concourse.mybir as mybir
from concourse.bass_utils import run_bass_kernel_spmd

F32 = mybir.dt.float32
BF16 = mybir.dt.bfloat16
U8 = mybir.dt.uint8
AF = mybir.ActivationFunctionType
ALU = mybir.AluOpType

NCORES = 8
L = 2
D = 1024
KC = 8
NPR = 2048
NSM = 64
T = NPR + NSM
DFF = 2816
EPS = 1e-6
GRAN = 128
PSUM_BASE = 1 << 24
RING_SLOTS = 6
SLOT_BYTES = 8192

PV_GMIX, PV_GATT, PV_GFFN, PV_GMEM, PV_GFIN = 0, 16, 32, 48, 64
PV_RGW, PV_RGB, PV_BA, PV_BX, PV_LAM = 72, 104, 112, 120, 128
PV_CFW, PV_CFB, PV_LNG, PV_LNB = 136, 384, 392, 400
PV_N = 408


def _esz(dt):
    return 4 if dt == F32 else (2 if dt == BF16 else 1)


class Trk:
    def __init__(self, nc):
        self.nc = nc
        self.E = {'pe': nc.tensor, 'act': nc.scalar, 'dve': nc.vector, 'pool': nc.gpsimd, 'sp': nc.sync}
        self.sems = {}
        self.val = {}
        self.waited = {e: {} for e in self.E}
        self.lastw = {}
        self.rd = {}
        self.base = {}

    def reg(self, handle, addr):
        self.base[handle.name] = addr

    def sem(self, key):
        if key not in self.sems:
            self.sems[key] = self.nc.alloc_semaphore('s_' + key)
            self.val[key] = 0
        return self.sems[key]

    def grans(self, aps):
        out = set()
        for ap in aps:
            nm = ap.tensor.name
            if nm not in self.base:
                continue
            base = self.base[nm]
            es = _esz(ap.dtype)
            pat = ap.ap
            rowstride = pat[0][0]
            off = int(ap.offset)
            if rowstride > 0:
                off = off % rowstride
            free = pat[1:]
            if len(free) == 0:
                rngs = [(off, off + 1)]
            else:
                ist, icnt = free[-1]
                ilen = (icnt - 1) * abs(ist) + 1
                outer = free[:-1]
                rngs = []
                for idx in itertools.product(*[range(c) for (_, c) in outer]):
                    st = off + sum(i * s for i, (s, _) in zip(idx, outer))
                    rngs.append((st, st + ilen))
            for lo, hi in rngs:
                blo = base + lo * es
                bhi = base + hi * es
                for g in range(blo // GRAN, (bhi - 1) // GRAN + 1):
                    out.add(g)
        return out

    def _deps(self, rg, wg):
        need = {}
        for g in rg:
            w = self.lastw.get(g)
            if w is not None and need.get(w[0], 0) < w[1]:
                need[w[0]] = w[1]
        for g in wg:
            w = self.lastw.get(g)
            if w is not None and need.get(w[0], 0) < w[1]:
                need[w[0]] = w[1]
            r = self.rd.get(g)
            if r:
                for k, v in r.items():
                    if need.get(k, 0) < v:
                        need[k] = v
        return need

    def _commit(self, rg, wg, key, val):
        for g in rg:
            d = self.rd.get(g)
            if d is None:
                self.rd[g] = {key: val}
            else:
                d[key] = val
        for g in wg:
            self.lastw[g] = (key, val)
            self.rd[g] = None

    def _wait(self, e, need):
        eng = self.E[e]
        wd = self.waited[e]
        for k, v in need.items():
            if k not in self.E:
                v = self.val[k]
            if wd.get(k, 0) < v:
                eng.wait_ge(self.sems[k], v)
                wd[k] = v

    def op(self, e, reads, writes, fn):
        rg = self.grans(reads)
        wg = self.grans(writes)
        self._wait(e, self._deps(rg, wg))
        ins = fn()
        s = self.sem(e)
        self.val[e] += 1
        ins.then_inc(s, 1)
        self._commit(rg, wg, e, self.val[e])

    def dma(self, q, out, in_, key):
        reads = [in_] if in_.tensor.name in self.base else []
        writes = [out] if out.tensor.name in self.base else []
        rg = self.grans(reads)
        wg = self.grans(writes)
        self._wait(q, self._deps(rg, wg))
        s = self.sem(key)
        ins = self.E[q].dma_start(out=out, in_=in_)
        self.val[key] += 16
        ins.then_inc(s, 16)
        self._commit(rg, wg, key, self.val[key])


class Region:
    def __init__(self, nc, trk, base, size):
        self.nc, self.trk, self.base, self.size = nc, trk, base, size
        self.ptr = 0

    def reset(self):
        self.ptr = 0

    def alloc(self, name, shape, dt):
        n = 1
        for s in shape[1:]:
            n *= s
        nbytes = n * _esz(dt)
        nbytes = (nbytes + GRAN - 1) // GRAN * GRAN
        assert self.ptr + nbytes <= self.size, (name, self.ptr, nbytes, self.size)
        addr = self.base + self.ptr
        t = self.nc.alloc_sbuf_tensor_at(name, list(shape), dt, offset=addr)
        self.trk.reg(t, addr)
        self.ptr += nbytes
        return t


class _Stop(Exception):
    pass


_STOP = None


class Builder:
    def ck(self, name):
        if _STOP == name:
            raise _Stop()

    def __init__(self):
        self.nc = bass.Bass("TRN2", target_bir_lowering=False)
        self.trk = Trk(self.nc)
        self.pb = 0

    def bank(self):
        self.pb = (self.pb + 1) % 8
        return self.ps[self.pb]

    def mm(self, out, pairs, extra_reads=()):
        reads = []
        for a, b in pairs:
            reads.append(a)
            reads.append(b)
        n = len(pairs)

        def fn():
            ins = None
            for i, (a, b) in enumerate(pairs):
                ins = self.nc.tensor.matmul(out, lhsT=a, rhs=b, start=(i == 0), stop=(i == n - 1))
            return ins
        self.trk.op('pe', reads, [out], fn)

    def mm_groups(self, groups):
        reads, writes = [], []
        for out, pairs in groups:
            writes.append(out)
            for a, b in pairs:
                reads.append(a)
                reads.append(b)

        def fn():
            ins = None
            for out, pairs in groups:
                n = len(pairs)
                for i, (a, b) in enumerate(pairs):
                    ins = self.nc.tensor.matmul(out, lhsT=a, rhs=b, start=(i == 0), stop=(i == n - 1))
            return ins
        self.trk.op('pe', reads, writes, fn)

    def act(self, out, in_, func, bias=None, scale=None):
        reads = [in_]
        kw = {}
        if bias is not None:
            kw['bias'] = bias
            if not isinstance(bias, (int, float)):
                reads.append(bias)
        if scale is not None:
            kw['scale'] = scale
            if not isinstance(scale, (int, float)):
                reads.append(scale)
        self.trk.op('act', reads, [out], lambda: self.nc.scalar.activation(out=out, in_=in_, func=func, **kw))

    def tt(self, out, in0, in1, op, eng='dve'):
        e = self.trk.E[eng]
        self.trk.op(eng, [in0, in1], [out], lambda: e.tensor_tensor(out=out, in0=in0, in1=in1, op=op))

    def ts(self, out, in0, s1, s2, op0, op1=None, eng='dve'):
        e = self.trk.E[eng]
        reads = [in0]
        if not isinstance(s1, (int, float)):
            reads.append(s1)
        if s2 is not None and not isinstance(s2, (int, float)):
            reads.append(s2)
        if op1 is None:
            self.trk.op(eng, reads, [out], lambda: e.tensor_scalar(out=out, in0=in0, scalar1=s1, scalar2=None, op0=op0))
        else:
            self.trk.op(eng, reads, [out], lambda: e.tensor_scalar(out=out, in0=in0, scalar1=s1, scalar2=s2, op0=op0, op1=op1))

    def stt(self, out, in0, scalar, in1, op0, op1):
        reads = [in0, in1]
        if not isinstance(scalar, (int, float)):
            reads.append(scalar)
        self.trk.op('dve', reads, [out], lambda: self.nc.vector.scalar_tensor_tensor(
            out=out, in0=in0, scalar=scalar, in1=in1, op0=op0, op1=op1))

    def scan(self, out, d0, d1, initial):
        reads = [d0, d1]
        if not isinstance(initial, (int, float)):
            reads.append(initial)
        self.trk.op('dve', reads, [out], lambda: self.nc.vector.tensor_tensor_scan(
            out=out, data0=d0, data1=d1, initial=initial, op0=ALU.mult, op1=ALU.add))

    def recip(self, out, in_):
        self.trk.op('dve', [in_], [out], lambda: self.nc.vector.reciprocal(out=out, in_=in_))

    def copy(self, out, in_, eng='dve'):
        e = self.trk.E[eng]
        self.trk.op(eng, [in_], [out], lambda: e.tensor_copy(out=out, in_=in_))

    def memset(self, out, v, eng='dve'):
        e = self.trk.E[eng]
        self.trk.op(eng, [], [out], lambda: e.memset(out, v))

    def ring_plan(self, blocks):
        self.blocks = blocks
        self.blk_view = [None] * len(blocks)
        self.blk_emitted = 0
        self.blk_released = [False] * len(blocks)

    def _ring_emit(self, i):
        ap = self.blocks[i]
        a, b = ap.shape[1], ap.shape[2]
        slot = i % RING_SLOTS
        v = self.ring[slot][:, 0:a * b].rearrange("p (a b) -> p a b", a=a)
        self.trk.dma('pool', v, ap, 'ring%d' % slot)
        self.blk_view[i] = v

    def _ring_pump(self, upto=None):
        while self.blk_emitted < len(self.blocks):
            j = self.blk_emitted
            if upto is not None and j <= upto:
                pass
            elif j >= RING_SLOTS and not self.blk_released[j - RING_SLOTS]:
                break
            if j >= RING_SLOTS:
                assert self.blk_released[j - RING_SLOTS], "ring overflow: block %d" % j
            self._ring_emit(j)
            self.blk_emitted += 1

    def wget(self, i):
        self._ring_pump(upto=i)
        return self.blk_view[i]

    def wrel(self, i):
        self.blk_released[i] = True
        self._ring_pump()

    def norm(self, xin, gvec, out_fn, N, sq, rstd):
        for kc in range(KC):
            self.act(sq[:, kc, 0:N], xin(kc), AF.Square)
        bk = self.bank()
        self.mm(bk[:, 0:N], [(self.c1024[:], sq[:, kc, 0:N]) for kc in range(KC)])
        self.act(rstd[:, 0:N], bk[:, 0:N], AF.Ln, bias=EPS)
        self.act(rstd[:, 0:N], rstd[:, 0:N], AF.Exp, scale=-0.5)
        for kc in range(KC):
            self.stt(out_fn(kc), xin(kc), gvec(kc), rstd[:, 0:N], ALU.mult, ALU.mult)

    def build(self):
        nc = self.nc
        trk = self.trk
        dr = {}

        def din(name, shape):
            dr[name] = nc.dram_tensor(name, list(shape), F32, kind="ExternalInput").ap()
            return dr[name]

        def dout(name, shape):
            dr[name] = nc.dram_tensor(name, list(shape), F32, kind="ExternalOutput").ap()
            return dr[name]

        xT = din("xT", [D, T])
        memT = din("memT", [D, 256])
        KcT = din("KcT", [L, 16, D, 256])
        Vc = din("Vc", [L, 16, 256, D])
        h0 = din("h0", [L, 128, 4, 16])
        rgs = din("rgs", [L, 128, 4, 16, 3])
        cfs = din("cfs", [L, 128, 4, 16, 30])
        pvd = din("pv", [128, PV_N])
        rg_wa = din("rg_wa", [L, 8, 64, 64])
        rg_wx = din("rg_wx", [L, 8, 64, 64])
        w_in = din("w_in", [L, D, 2048])
        w_out = din("w_out", [L, D, D])
        w_q = din("w_q", [L, D, D])
        w_k = din("w_k", [L, D, D])
        w_v = din("w_v", [L, D, D])
        w_o = din("w_o", [L, D, D])
        w_gate = din("w_gate", [L, D, DFF])
        w_up = din("w_up", [L, D, DFF])
        w_down = din("w_down", [L, DFF, D])

        yT = dout("yT", [D, T])
        o_ph = dout("o_ph", [L, 128, 4])
        o_prg = dout("o_prg", [L, 128, 4, 3])
        o_pcf = dout("o_pcf", [L, 128, 4, 30])
        o_pk = dout("o_pk", [L, 128, 8, 256])
        o_pv = dout("o_pv", [L, 256, D])
        o_sh = dout("o_sh", [L, 128, 4, 16])
        o_srg = dout("o_srg", [L, 128, 4, 16, 3])
        o_scfh = dout("o_scfh", [L, 128, 4, 16, 26])
        o_scfn = dout("o_scfn", [L, 128, 4, 16, 4])

        ARENA = 212800
        arena = nc.alloc_sbuf_tensor("arena", [128, ARENA], U8)
        abase = nc.lookup_mloc(arena).addr
        P = Region(nc, trk, abase, ARENA)
        X = P.alloc("X", [128, KC, T], F32)
        identf = P.alloc("identf", [128, 128], F32)
        ident = P.alloc("ident", [128, 128], BF16)
        self.c1024 = P.alloc("c1024", [128, 128], BF16)
        c512 = P.alloc("c512", [128, 128], BF16)
        ones = P.alloc("ones", [128, 128], BF16)
        pv = P.alloc("pvs", [128, PV_N], F32)
        cl = P.alloc("cl", [128, L * 4], F32)
        cl2 = P.alloc("cl2", [128, L * 4], F32)
        wabd = P.alloc("wabd", [128, L * 2 * 4, 128], BF16)
        hstate = P.alloc("hstate", [128, 4], F32)
        self.ring = [P.alloc("ring%d" % i, [128, SLOT_BYTES // 2], BF16) for i in range(RING_SLOTS)]
        rbase = abase + P.ptr
        R = Region(nc, trk, rbase, ARENA - P.ptr)
        self.ps = []
        for i in range(8):
            t = nc.alloc_psum_tensor("ps%d" % i, [128, 512], F32)
            trk.reg(t, PSUM_BASE + i * 2048)
            self.ps.append(t)

        blocks = []
        bidx = {}

        def wv(w, l):
            return w[l].rearrange("(kc p) n -> p kc n", p=128)

        for l in range(L):
            for c in range(4):
                bidx[(l, 'in', c)] = len(blocks)
                blocks.append(wv(w_in, l)[:, :, c * 512:(c + 1) * 512])
            for c in range(2):
                bidx[(l, 'out', c)] = len(blocks)
                blocks.append(wv(w_out, l)[:, :, c * 512:(c + 1) * 512])
            for nm, w in (('k', w_k), ('v', w_v), ('q', w_q), ('o', w_o)):
                for c in range(2):
                    bidx[(l, nm, c)] = len(blocks)
                    blocks.append(wv(w, l)[:, :, c * 512:(c + 1) * 512])
            for g in range(3):
                ncols = 1024 if g < 2 else 768
                c0 = g * 1024
                for hb in range(2):
                    lo = c0 + hb * 512
                    hi = min(c0 + ncols, lo + 512)
                    bidx[(l, 'gate', g, hb)] = len(blocks)
                    blocks.append(wv(w_gate, l)[:, :, lo:hi])
                    bidx[(l, 'up', g, hb)] = len(blocks)
                    blocks.append(wv(w_up, l)[:, :, lo:hi])
                nch = ncols // 128
                for hb in range(2):
                    k0 = hb * 4
                    k1 = min(nch, k0 + 4)
                    bidx[(l, 'down', g, hb)] = len(blocks)
                    blocks.append(w_down[l, c0 + k0 * 128:c0 + k1 * 128, :].rearrange("(kc p) n -> p kc n", p=128))
        self.ring_plan(blocks)

        A = self.act
        xv = xT.rearrange("(kc p) n -> p kc n", p=128)
        yv = yT.rearrange("(kc p) n -> p kc n", p=128)

        def pvc(off, n=1):
            return pv[:, off:off + n]

        with nc.Block():
            trk.dma('sp', pv[:], pvd[:, :], 'ld_pv')
            for ti, c0 in enumerate(range(0, T, 512)):
                n = min(512, T - c0)
                trk.dma('sp', X[:, :, c0:c0 + n], xv[:, :, c0:c0 + n], 'ld_x%d' % ti)
            for l in range(L):
                for j in range(4):
                    trk.dma('sp', o_scfh[l, :, j], cfs[l, :, j, :, 4:30], 'st_h')
            self.memset(identf[:], 0.0, eng='dve')
            trk.op('pool', [identf[:]], [identf[:]], lambda: nc.gpsimd.affine_select(
                out=identf[:], in_=identf[:], pattern=[[-1, 128]], compare_op=ALU.not_equal, fill=1.0,
                base=0, channel_multiplier=1))
            self.copy(ident[:], identf[:])
            self.memset(self.c1024[:], 1.0 / 1024.0)
            self.memset(c512[:], 1.0 / 512.0)
            self.memset(ones[:], 1.0)
            R.reset()
            wst = R.alloc("wst", [128, L * 2 * 4, 128], F32)
            self.memset(wst[:], 0.0)
            for l in range(L):
                for gi, wsrc in enumerate((rg_wa, rg_wx)):
                    for hh in range(8):
                        j, e = hh // 2, hh % 2
                        trk.dma('act', wst[e * 64:(e + 1) * 64, (l * 2 + gi) * 4 + j, e * 64:(e + 1) * 64],
                                wsrc[l, hh, :, :], 'ld_w%d' % ((l * 2 + gi) % 2))
            self.copy(wabd[:], wst[:])
            spt = R.alloc("spt", [128, L * 4], F32)
            A(spt[:], pvc(PV_LAM, 8), AF.Exp, scale=-1.0)
            A(spt[:], spt[:], AF.Ln, bias=1.0)
            self.ts(cl[:], spt[:], -8.0, None, ALU.mult)
            self.ts(cl2[:], spt[:], -16.0, None, ALU.mult)

            try:
                self.ck('P')
                for l in range(L):
                    self.layer(l, locals())
                if not getattr(self, 'final_done', False):
                    self.final_norm(locals())
            except _Stop:
                pass
            for k, v in trk.val.items():
                if k not in trk.E and v > 0:
                    nc.sync.wait_ge(trk.sems[k], v)
        return nc

    def layer(self, l, env):
        nc, trk = self.nc, self.trk
        R = env['R']; X = env['X']; pv = env['pv']; bidx = env['bidx']
        ident, identf, c512, ones = env['ident'], env['identf'], env['c512'], env['ones']
        cl, cl2, wabd, hstate = env['cl'], env['cl2'], env['wabd'], env['hstate']
        dr_ = env['dr']
        A = self.act
        MUL, ADD, SUB = ALU.mult, ALU.add, ALU.subtract

        def pvc(off, n=1):
            return pv[:, off:off + n]

        R.reset()
        cfD = R.alloc("cfD", [128, 4 * 31, 128], BF16)
        rgD = R.alloc("rgD", [128, 4 * 4, 128], BF16)
        NA = 256
        ycat = R.alloc("a_ycat", [128, KC, NA], BF16)
        sq = R.alloc("a_sq", [128, KC, NA], BF16)
        rstd = R.alloc("a_rstd", [128, NA], F32)
        xn = R.alloc("a_xn", [128, KC, NA], BF16)
        o_x = R.ptr
        xrb0 = R.alloc("a_xrb0", [128, 4, 3 + NA], BF16)
        cb0 = R.alloc("a_cb0", [128, 4, 30 + NA], BF16)
        o_end = R.ptr
        R.ptr = o_x
        csb = R.alloc("a_csb", [128, 4, 16, 34], BF16)
        assert R.ptr <= o_end
        R.ptr = o_end
        xrb1 = R.alloc("a_xrb1", [128, 4, 3 + NA], BF16)
        cb1 = R.alloc("a_cb1", [128, 4, 30 + NA], BF16)
        xrb2 = [xrb0, xrb1]
        cb2 = [cb0, cb1]
        gg2 = [R.alloc("a_gg%d" % i, [128, 4, NA], BF16) for i in range(2)]
        sg = R.alloc("a_sg", [128, NA], F32)
        xc2 = [R.alloc("a_xc%d" % i, [128, NA], F32) for i in range(2)]
        xcb = R.alloc("a_xcb", [128, NA], BF16)
        r_ = R.alloc("a_r", [128, NA], F32)
        i_ = R.alloc("a_i", [128, NA], F32)
        a_ = R.alloc("a_a", [128, NA], F32)
        m_ = R.alloc("a_m", [128, NA], F32)
        b_ = R.alloc("a_b", [128, NA], F32)
        h_ = R.alloc("a_h", [128, NA], F32)
        ccf = R.alloc("a_ccf", [128, 4, NA], F32)
        ccb = R.alloc("a_ccb", [128, 4, NA], BF16)
        sqb = R.alloc("a_sqb", [128, 4, NA], BF16)
        mean = R.alloc("a_mean", [128, NA], F32)
        var = R.alloc("a_var", [128, NA], F32)
        rstc = R.alloc("a_rstc", [128, NA], F32)
        o_dd = R.ptr
        dd = R.alloc("a_dd", [128, NA], F32)
        dd2 = [dd, sg]
        prg = R.alloc("a_prg", [128, 4, 3], F32)
        pcf = R.alloc("a_pcf", [128, 4, 30], F32)
        xrs = R.alloc("a_xrs", [128, 4, 16, 7], BF16)
        o_keep = R.ptr
        R.ptr = o_dd
        xrsf = R.alloc("a_xrsf", [128, 4, 16, 3], F32)
        R.ptr = o_keep
        srg = R.alloc("a_srg", [128, 4, 16, 3], F32)
        cs4 = R.alloc("a_cs4", [128, 4, 16, 4], F32)
        h0s = R.alloc("a_h0s", [128, 4, 16], F32)
        shl = R.alloc("a_shl", [128, 4, 16], F32)
        tmp16 = R.alloc("a_tmp16", [128, 16], F32)
        nb2 = R.alloc("a_nb2", [128, 8], F32)

        for j in range(4):
            for k in range(4):
                self.ts(rgD[:, j * 4 + k, :], identf[:], pvc(PV_RGW + (l * 4 + j) * 4 + k), None, MUL)
            for k in range(31):
                if k % 3 == 2:
                    A(cfD[:, j * 31 + k, :], identf[:], AF.Copy, scale=pvc(PV_CFW + (l * 4 + j) * 31 + k))
                else:
                    self.ts(cfD[:, j * 31 + k, :], identf[:], pvc(PV_CFW + (l * 4 + j) * 31 + k), None, MUL)
        self.ts(nb2[:, 0:4], pvc(PV_BA + l * 4, 4), -1.0, None, MUL)
        self.ts(nb2[:, 4:8], pvc(PV_BX + l * 4, 4), -1.0, None, MUL)
        self.memset(xrb0[:, :, 0:3], 0.0)
        self.memset(cb0[:, :, 0:30], 0.0)
        self.memset(hstate[:], 0.0)

        win = [self.wget(bidx[(l, 'in', c)]) for c in range(4)]
        wout = [self.wget(bidx[(l, 'out', c)]) for c in range(2)]
        tiles = [(c0, NA, False) for c0 in range(0, NPR, NA)] + [(NPR, NSM, True)]
        NT = len(tiles)

        def load_sample_state():
            trk.dma('sp', xrsf[:], dr_['rgs'][l], 'ld_st0')
            trk.dma('sp', h0s[:], dr_['h0'][l], 'ld_st1')
            for j in range(4):
                trk.dma('pool', csb[:, j, :, 0:30], dr_['cfs'][l, :, j], 'ld_st%d' % (2 + j))
            self.copy(xrs[:, :, :, 0:3], xrsf[:])

        def v3(ap):
            return ap.rearrange("p (b t) -> p b t", t=4)

        def front(ti):
            c0, N, smp = tiles[ti]
            sset = ti % 2
            xrb, cb, gg = xrb2[sset], cb2[sset], gg2[sset]
            last_p = (not smp) and (c0 + N == NPR)
            if smp:
                load_sample_state()
            elif ti >= 1:
                self.copy(xrb[:, :, 0:3], xrb2[1 - sset][:, :, NA:NA + 3])
                self.copy(cb[:, :, 0:30], cb2[1 - sset][:, :, NA:NA + 30])
            self.norm(lambda kc: X[:, kc, c0:c0 + N], lambda kc: pvc(PV_GMIX + l * 8 + kc),
                      lambda kc: xn[:, kc, 0:N], N, sq, rstd)
            yield

            def proj(blk, j):
                bk = self.bank()
                self.mm(bk[:, 0:N], [(win[blk][:, kc, j * 128:(j + 1) * 128], xn[:, kc, 0:N]) for kc in range(KC)])
                return bk
            for j in range(4):
                bk = proj(0, j)
                if smp:
                    A(xrs[:, j, :, 3:7], v3(bk[:, 0:N]), AF.Copy)
                    A(srg[:, j, :, :], v3(bk[:, 0:N])[:, :, 1:4], AF.Copy)
                else:
                    A(xrb[:, j, 3:3 + N], bk[:, 0:N], AF.Copy)
                    if last_p:
                        A(prg[:, j, :], bk[:, N - 3:N], AF.Copy)
            yield
            yield
            yield
            yield
            for j in range(4):
                bk = proj(1, j)
                A(gg[:, j, 0:N], bk[:, 0:N], AF.Gelu_apprx_tanh)
            for half in range(1):
                for j in range(4):
                    bv = proj(2, j)
                    bg = proj(3, j)
                    A(sg[:, 0:N], bg[:, 0:N], AF.Sigmoid)
                    if smp:
                        self.tt(cs4[:, j, :, :], v3(bv[:, 0:N]), v3(sg[:, 0:N]), MUL)
                        self.copy(csb[:, j, :, 30:34], cs4[:, j, :, :])
                    else:
                        self.tt(cb[:, j, 30:30 + N], bv[:, 0:N], sg[:, 0:N], MUL)
                        if last_p:
                            self.tt(pcf[:, j, :], bv[:, N - 30:N], sg[:, N - 30:N], MUL)
                yield

        def back(ti):
            c0, N, smp = tiles[ti]
            sset = ti % 2
            xrb, cb, gg = xrb2[sset], cb2[sset], gg2[sset]

            def tail(j):
                xc = xc2[j % 2]
                self.tt(b_[:, 0:N], i_[:, 0:N], xc[:, 0:N], MUL)
                self.tt(b_[:, 0:N], b_[:, 0:N], m_[:, 0:N], MUL)
                if smp:
                    av = v3(a_[:, 0:N])
                    bvw = v3(b_[:, 0:N])
                    self.tt(tmp16[:], av[:, :, 0], h0s[:, j, :], MUL)
                    self.tt(bvw[:, :, 0], bvw[:, :, 0], tmp16[:], ADD)
                    self.memset(av[:, :, 0], 0.0)
                    self.scan(h_[:, 0:N], a_[:, 0:N], b_[:, 0:N], 0.0)
                    self.copy(shl[:, j, :], v3(h_[:, 0:N])[:, :, 3])
                else:
                    self.scan(h_[:, 0:N], a_[:, 0:N], b_[:, 0:N], hstate[:, j:j + 1])
                    self.copy(hstate[:, j:j + 1], h_[:, N - 1:N])
                self.tt(ycat[:, j, 0:N], h_[:, 0:N], gg[:, j, 0:N], MUL)

            for j in range(4):
                xc = xc2[j % 2]
                bk = self.bank()
                if smp:
                    self.mm(bk[:, 0:N], [(rgD[:, j * 4 + k, :], xrs[:, j, :, k:k + 4]) for k in range(4)])
                else:
                    self.mm(bk[:, 0:N], [(rgD[:, j * 4 + k, :], xrb[:, j, k:k + N]) for k in range(4)])
                bias = pvc(PV_RGB + l * 4 + j)
                self.ts(xcb[:, 0:N], bk[:, 0:N], bias, None, ADD)
                self.ts(xc[:, 0:N], bk[:, 0:N], bias, None, ADD)
                bc = self.bank()
                if smp:
                    self.mm(bc[:, 0:N], [(cfD[:, j * 31 + k, :], csb[:, j, :, k:k + 4]) for k in range(31)])
                else:
                    self.mm(bc[:, 0:N], [(cfD[:, j * 31 + k, :], cb[:, j, k:k + N]) for k in range(31)])
                ba = self.bank()
                self.mm(ba[:, 0:N], [(wabd[:, (l * 2 + 0) * 4 + j, :], xcb[:, 0:N])])
                bx = self.bank()
                self.mm(bx[:, 0:N], [(wabd[:, (l * 2 + 1) * 4 + j, :], xcb[:, 0:N])])
                cbias = pvc(PV_CFB + l * 4 + j)
                self.ts(ccf[:, j, 0:N], bc[:, 0:N], cbias, None, ADD)
                self.ts(ccb[:, j, 0:N], bc[:, 0:N], cbias, None, ADD)
                self.stt(sqb[:, j, 0:N], bc[:, 0:N], cbias, ccf[:, j, 0:N], ADD, MUL)
                if j >= 1:
                    tail(j - 1)
                A(r_[:, 0:N], ba[:, 0:N], AF.Exp, scale=-1.0, bias=nb2[:, j:j + 1])
                A(i_[:, 0:N], bx[:, 0:N], AF.Exp, scale=-1.0, bias=nb2[:, 4 + j:5 + j])
                A(r_[:, 0:N], r_[:, 0:N], AF.Ln, bias=1.0)
                A(i_[:, 0:N], i_[:, 0:N], AF.Ln, bias=1.0)
                A(r_[:, 0:N], r_[:, 0:N], AF.Exp, scale=-1.0)
                A(i_[:, 0:N], i_[:, 0:N], AF.Exp, scale=-1.0)
                A(a_[:, 0:N], r_[:, 0:N], AF.Exp, scale=cl[:, l * 4 + j:l * 4 + j + 1])
                A(m_[:, 0:N], r_[:, 0:N], AF.Exp, scale=cl2[:, l * 4 + j:l * 4 + j + 1])
                A(m_[:, 0:N], m_[:, 0:N], AF.Ln, scale=-1.0, bias=1.0000001)
                A(m_[:, 0:N], m_[:, 0:N], AF.Exp, scale=0.5)
                yield
            tail(3)
            bm = self.bank()
            self.mm(bm[:, 0:N], [(c512[:], ccb[:, j, 0:N]) for j in range(4)])
            bq = self.bank()
            self.mm(bq[:, 0:N], [(c512[:], sqb[:, j, 0:N]) for j in range(4)])
            A(mean[:, 0:N], bm[:, 0:N], AF.Copy)
            self.tt(var[:, 0:N], mean[:, 0:N], mean[:, 0:N], MUL)
            self.tt(var[:, 0:N], bq[:, 0:N], var[:, 0:N], SUB)
            A(rstc[:, 0:N], var[:, 0:N], AF.Ln, bias=EPS)
            A(rstc[:, 0:N], rstc[:, 0:N], AF.Exp, scale=-0.5)
            for j in range(4):
                dd = dd2[j % 2]
                self.tt(dd[:, 0:N], ccf[:, j, 0:N], mean[:, 0:N], SUB)
                self.tt(dd[:, 0:N], dd[:, 0:N], rstc[:, 0:N], MUL)
                A(ycat[:, 4 + j, 0:N], dd[:, 0:N], AF.Silu, bias=pvc(PV_LNB + l * 4 + j),
                  scale=pvc(PV_LNG + l * 4 + j))
            yield
            for oc in range(KC):
                bk = self.bank()
                self.mm(bk[:, 0:N], [(wout[oc // 4][:, kc, (oc % 4) * 128:(oc % 4 + 1) * 128], ycat[:, kc, 0:N])
                                     for kc in range(KC)])
                self.tt(X[:, oc, c0:c0 + N], X[:, oc, c0:c0 + N], bk[:, 0:N], ADD)
            yield

        def drive(gens):
            gens = [g for g in gens if g is not None]
            while gens:
                alive = []
                for g in gens:
                    try:
                        next(g)
                        alive.append(g)
                    except StopIteration:
                        pass
                gens = alive

        drive([front(0)])
        for ti in range(NT):
            drive([front(ti + 1) if ti + 1 < NT else None, back(ti)])
            if ti == NT - 2:
                for c in range(4):
                    self.wrel(bidx[(l, 'in', c)])

        trk.dma('sp', dr_['o_ph'][l], hstate[:], 'st_a0')
        trk.dma('sp', dr_['o_prg'][l], prg[:], 'st_a1')
        trk.dma('sp', dr_['o_pcf'][l], pcf[:], 'st_a2')
        trk.dma('sp', dr_['o_sh'][l], shl[:], 'st_a3')
        trk.dma('sp', dr_['o_srg'][l], srg[:], 'st_a4')
        trk.dma('sp', dr_['o_scfn'][l], cs4[:], 'st_a5')
        for c in range(2):
            self.wrel(bidx[(l, 'out', c)])

        self.ck('A%d' % l)
        R.reset()
        NB = 512
        KpT = R.alloc("b_KpT", [128, KC, 256], BF16)
        Vp = R.alloc("b_Vp", [128, 2, D], BF16)
        Ks = [R.alloc("b_Ks%d" % i, [128, KC, 256], BF16) for i in range(2)]
        Vs = [R.alloc("b_Vs%d" % i, [128, 2, D], BF16) for i in range(2)]
        mark = R.ptr
        memf = R.alloc("b_memf", [128, KC, 256], F32)
        memn = R.alloc("b_memn", [128, KC, 256], BF16)
        msq = R.alloc("b_msq", [128, KC, 256], BF16)
        mrs = R.alloc("b_mrs", [128, 256], F32)
        kst = R.alloc("b_kst", [128, KC, 256], F32)
        vst = R.alloc("b_vst", [128, 2, D], F32)
        trk.dma('sp', memf[:], dr_['memT'].rearrange("(kc p) n -> p kc n", p=128), 'ld_mem')
        self.norm(lambda kc: memf[:, kc, :], lambda kc: pvc(PV_GMEM + l * 8 + kc),
                  lambda kc: memn[:, kc, :], 256, msq, mrs)
        self.ck('Bn%d' % l)
        wk = [self.wget(bidx[(l, 'k', c)]) for c in range(2)]
        self.ck('Bw%d' % l)
        for dc in range(KC):
            bk = self.bank()
            self.mm(bk[:, 0:256], [(wk[dc // 4][:, kc, (dc % 4) * 128:(dc % 4 + 1) * 128], memn[:, kc, :])
                                   for kc in range(KC)])
            A(KpT[:, dc, :], bk[:, 0:256], AF.Copy)
            A(kst[:, dc, :], bk[:, 0:256], AF.Copy)
        self.ck('Bk%d' % l)
        trk.dma('sp', dr_['o_pk'][l], kst[:], 'st_bk')
        for c in range(2):
            self.wrel(bidx[(l, 'k', c)])
        wvv = [self.wget(bidx[(l, 'v', c)]) for c in range(2)]
        for mt in range(2):
            for cbk in range(2):
                bk = self.bank()
                self.mm(bk[:, :], [(memn[:, kc, mt * 128:(mt + 1) * 128], wvv[cbk][:, kc, :]) for kc in range(KC)])
                A(Vp[:, mt, cbk * 512:(cbk + 1) * 512], bk[:, :], AF.Copy)
                A(vst[:, mt, cbk * 512:(cbk + 1) * 512], bk[:, :], AF.Copy)
        trk.dma('sp', dr_['o_pv'][l].rearrange("(j p) d -> p j d", p=128), vst[:], 'st_bv')
        for c in range(2):
            self.wrel(bidx[(l, 'v', c)])
        self.ck('Ba%d' % l)
        R.ptr = mark
        sq = R.alloc("b_sq", [128, KC, NB], BF16)
        rstd = R.alloc("b_rstd", [128, NB], F32)
        xn = R.alloc("b_xn", [128, KC, NB], BF16)
        qT2 = [R.alloc("b_qT%d" % i, [128, KC, NB], BF16) for i in range(2)]
        eT2 = [R.alloc("b_eT%d" % i, [128, 2, NB], BF16) for i in range(2)]
        rs2 = [R.alloc("b_rs%d" % i, [128, NB], F32) for i in range(2)]
        oT = R.alloc("b_oT", [128, KC, NB], BF16)
        qTs = R.alloc("b_qTs", [128, KC, NSM], BF16)
        oTs = R.alloc("b_oTs", [128, KC, NSM], BF16)
        esb = [R.alloc("b_esb%d" % i, [128, 32], BF16) for i in range(2)]
        ssb = R.alloc("b_ssb", [128, 32], F32)
        rsb = R.alloc("b_rsb", [128, 16], F32)
        wq = [self.wget(bidx[(l, 'q', c)]) for c in range(2)]
        wo = [self.wget(bidx[(l, 'o', c)]) for c in range(2)]

        def qproj(c0, N, qdst):
            self.norm(lambda kc: X[:, kc, c0:c0 + N], lambda kc: pvc(PV_GATT + l * 8 + kc),
                      lambda kc: xn[:, kc, 0:N], N, sq, rstd)
            yield
            for oc in range(KC):
                bk = self.bank()
                self.mm(bk[:, 0:N], [(wq[oc // 4][:, kc, (oc % 4) * 128:(oc % 4 + 1) * 128], xn[:, kc, 0:N])
                                     for kc in range(KC)])
                A(qdst[:, oc, 0:N], bk[:, 0:N], AF.Copy, scale=1.0 / 16.0)
                if oc == 3:
                    yield
            yield

        def oproj(c0, N, osrc):
            for oc in range(KC):
                bk = self.bank()
                self.mm(bk[:, 0:N], [(wo[oc // 4][:, kc, (oc % 4) * 128:(oc % 4 + 1) * 128], osrc[:, kc, 0:N])
                                     for kc in range(KC)])
                self.tt(X[:, oc, c0:c0 + N], X[:, oc, c0:c0 + N], bk[:, 0:N], ADD)

        def kv_load(b):
            trk.dma('pool', Ks[b % 2][:], dr_['KcT'][l, b].rearrange("(kc p) m -> p kc m", p=128), 'ld_k%d' % (b % 2))
            trk.dma('pool', Vs[b % 2][:], dr_['Vc'][l, b].rearrange("(j p) d -> p j d", p=128), 'ld_v%d' % (b % 2))

        def sample_batch(b):
            if b + 1 < 16:
                kv_load(b + 1)
            kb, vb, es = Ks[b % 2], Vs[b % 2], esb[b % 2]
            bsc = self.bank()
            groups = []
            for hh in range(4):
                for jm in range(2):
                    col = hh * 8 + jm * 4
                    groups.append((bsc[:, col:col + 4],
                                   [(kb[:, 2 * hh + dc, jm * 128:(jm + 1) * 128],
                                     qTs[:, 2 * hh + dc, b * 4:(b + 1) * 4]) for dc in range(2)]))
            self.mm_groups(groups)
            A(es[:, :], bsc[:, 0:32], AF.Exp)
            bs = self.bank()
            self.mm(bs[:, 0:32], [(ones[:], es[:, :])])
            A(ssb[:, :], bs[:, 0:32], AF.Copy)
            sv = ssb[:, :].rearrange("p (h j t) -> p h j t", j=2, t=4)
            self.tt(rsb[:, :].rearrange("p (h t) -> p h t", t=4), sv[:, :, 0, :], sv[:, :, 1, :], ADD)
            self.recip(rsb[:, :], rsb[:, :])
            bo = self.bank()
            groups = []
            for hh in range(4):
                for dc in range(2):
                    col = (hh * 2 + dc) * 4
                    groups.append((bo[:, col:col + 4],
                                   [(vb[:, jm, hh * 256 + dc * 128:hh * 256 + (dc + 1) * 128],
                                     es[:, hh * 8 + jm * 4:hh * 8 + jm * 4 + 4]) for jm in range(2)]))
            self.mm_groups(groups)
            for dc in range(2):
                ov = oTs[:, :, b * 4:(b + 1) * 4].rearrange("p (h dc) t -> p dc h t", dc=2)[:, dc]
                iv = bo[:, 0:32].rearrange("p (h dc t) -> p dc h t", dc=2, t=4)[:, dc]
                rv = rsb[:, :].rearrange("p (h t) -> p h t", t=4)
                self.tt(ov, iv, rv, MUL)

        def scores(hh, N, qT):
            eT = eT2[hh % 2]
            for jm in range(2):
                bk = self.bank()
                self.mm(bk[:, 0:N], [(KpT[:, 2 * hh + dc, jm * 128:(jm + 1) * 128], qT[:, 2 * hh + dc, 0:N])
                                     for dc in range(2)])
                A(eT[:, jm, 0:N], bk[:, 0:N], AF.Exp)

        def sum_pv(hh, N):
            eT, rs = eT2[hh % 2], rs2[hh % 2]
            bs = self.bank()
            self.mm(bs[:, 0:N], [(ones[:], eT[:, jm, 0:N]) for jm in range(2)])
            A(rs[:, 0:N], bs[:, 0:N], AF.Ln)
            A(rs[:, 0:N], rs[:, 0:N], AF.Exp, scale=-1.0)
            for dc in range(2):
                bo = self.bank()
                self.mm(bo[:, 0:N], [(Vp[:, jm, hh * 256 + dc * 128:hh * 256 + (dc + 1) * 128], eT[:, jm, 0:N])
                                     for jm in range(2)])
                self.tt(oT[:, 2 * hh + dc, 0:N], bo[:, 0:N], rs[:, 0:N], MUL)

        def drive(gens):
            gens = [g for g in gens if g is not None]
            while gens:
                alive = []
                for g in gens:
                    try:
                        next(g)
                        alive.append(g)
                    except StopIteration:
                        pass
                gens = alive

        ptiles = list(range(0, NPR, NB))

        def backB(ti):
            qT = qT2[ti % 2]
            scores(0, NB, qT)
            for hh in range(4):
                if hh + 1 < 4:
                    scores(hh + 1, NB, qT)
                sum_pv(hh, NB)
                sample_batch(ti * 4 + hh)
                yield
            oproj(ptiles[ti], NB, oT)
            yield

        kv_load(0)
        drive([qproj(NPR, NSM, qTs)])
        drive([qproj(ptiles[0], NB, qT2[0])])
        for ti in range(len(ptiles)):
            nxt = qproj(ptiles[ti + 1], NB, qT2[(ti + 1) % 2]) if ti + 1 < len(ptiles) else None
            drive([nxt, backB(ti)])
        oproj(NPR, NSM, oTs)
        for c in range(2):
            self.wrel(bidx[(l, 'q', c)])
        for c in range(2):
            self.wrel(bidx[(l, 'o', c)])

        self.ck('B%d' % l)
        R.reset()
        NC_ = 512
        o_xnf = R.ptr
        xnf = R.alloc("c_xn", [128, KC, T], BF16)
        hb = R.alloc("c_h", [128, 8, T], BF16)
        sq = R.alloc("c_sq", [128, KC, NC_], BF16)
        rstd = R.alloc("c_rstd", [128, NC_], F32)
        sgt = [R.alloc("c_sg%d" % i, [128, NC_], F32) for i in range(2)]
        tiles = [(c0, min(NC_, T - c0)) for c0 in range(0, T, NC_)]
        for (c0, N) in tiles:
            self.norm(lambda kc: X[:, kc, c0:c0 + N], lambda kc: pvc(PV_GFFN + l * 8 + kc),
                      lambda kc: xnf[:, kc, c0:c0 + N], N, sq, rstd)
        cnt = 0
        for g in range(3):
            nch = 8 if g < 2 else 6
            for hbk in range(2):
                k0 = hbk * 4
                k1 = min(nch, k0 + 4)
                if k1 <= k0:
                    continue
                wg = self.wget(bidx[(l, 'gate', g, hbk)])
                wu = self.wget(bidx[(l, 'up', g, hbk)])
                for fcl in range(k0, k1):
                    cc = (fcl - k0) * 128
                    for (c0, N) in tiles:
                        bg = self.bank()
                        self.mm(bg[:, 0:N], [(wg[:, kc, cc:cc + 128], xnf[:, kc, c0:c0 + N]) for kc in range(KC)])
                        bu = self.bank()
                        self.mm(bu[:, 0:N], [(wu[:, kc, cc:cc + 128], xnf[:, kc, c0:c0 + N]) for kc in range(KC)])
                        st = sgt[cnt % 2]
                        cnt += 1
                        A(st[:, 0:N], bg[:, 0:N], AF.Silu)
                        self.tt(hb[:, fcl, c0:c0 + N], bu[:, 0:N], st[:, 0:N], MUL)
                self.wrel(bidx[(l, 'gate', g, hbk)])
                self.wrel(bidx[(l, 'up', g, hbk)])
            wd = [self.wget(bidx[(l, 'down', g, hbk)]) for hbk in range(2)]
            if l == L - 1 and g == 2:
                o_keep = R.ptr
                R.ptr = o_xnf
                ys = [R.alloc("f_y%d" % i, [128, KC, NC_], F32) for i in range(2)]
                R.ptr = o_keep
                yv = env['yv']
                for ti, (c0, N) in enumerate(tiles):
                    for oc in range(KC):
                        bk = self.bank()
                        self.mm(bk[:, 0:N], [(wd[kcl // 4][:, kcl % 4, oc * 128:(oc + 1) * 128],
                                              hb[:, kcl, c0:c0 + N]) for kcl in range(nch)])
                        self.tt(X[:, oc, c0:c0 + N], X[:, oc, c0:c0 + N], bk[:, 0:N], ADD)
                    y = ys[ti % 2]
                    self.norm(lambda kc: X[:, kc, c0:c0 + N], lambda kc: pvc(PV_GFIN + kc),
                              lambda kc: y[:, kc, 0:N], N, sq, rstd)
                    trk.dma('sp', yv[:, :, c0:c0 + N], y[:, :, 0:N], 'st_y%d' % (ti % 2))
                self.final_done = True
            else:
              for oc in range(KC):
                for (c0, N) in tiles:
                    bk = self.bank()
                    self.mm(bk[:, 0:N], [(wd[kcl // 4][:, kcl % 4, oc * 128:(oc + 1) * 128], hb[:, kcl, c0:c0 + N])
                                         for kcl in range(nch)])
                    self.tt(X[:, oc, c0:c0 + N], X[:, oc, c0:c0 + N], bk[:, 0:N], ADD)
            for hbk in range(2):
                self.wrel(bidx[(l, 'down', g, hbk)])

        self.ck('C%d' % l)

    def final_norm(self, env):
        trk = self.trk
        R = env['R']; X = env['X']; pv = env['pv']; yv = env['yv']
        R.reset()
        NF = 512
        sqs = [R.alloc("f_sq%d" % i, [128, KC, NF], BF16) for i in range(2)]
        rstds = [R.alloc("f_rstd%d" % i, [128, NF], F32) for i in range(2)]
        ys = [R.alloc("f_y%d" % i, [128, KC, NF], F32) for i in range(2)]
        for ti, c0 in enumerate(range(0, T, NF)):
            N = min(NF, T - c0)
            y = ys[ti % 2]
            self.norm(lambda kc: X[:, kc, c0:c0 + N], lambda kc: pv[:, PV_GFIN + kc:PV_GFIN + kc + 1],
                      lambda kc: y[:, kc, 0:N], N, sqs[ti % 2], rstds[ti % 2])
            trk.dma('sp', yv[:, :, c0:c0 + N], y[:, :, 0:N], 'st_y%d' % (ti % 2))


_CACHE = {}


def _get_nc():
    if 'nc' not in _CACHE:
        _CACHE['nc'] = Builder().build()
    return _CACHE['nc']


def _pack_pv(inp):
    pv = np.zeros((128, PV_N), np.float32)

    def feat(v):
        return np.ascontiguousarray(v.reshape(L, KC, 128).transpose(2, 0, 1)).reshape(128, L * KC)

    def chan(v):
        return np.ascontiguousarray(v.reshape(L, 4, 128).transpose(2, 0, 1)).reshape(128, L * 4)

    pv[:, PV_GMIX:PV_GMIX + 16] = feat(inp['norm_mix_g'])
    pv[:, PV_GATT:PV_GATT + 16] = feat(inp['norm_attn_g'])
    pv[:, PV_GFFN:PV_GFFN + 16] = feat(inp['norm_ffn_g'])
    pv[:, PV_GMEM:PV_GMEM + 16] = feat(inp['norm_mem_g'])
    pv[:, PV_GFIN:PV_GFIN + 8] = inp['norm_final_g'].reshape(KC, 128).T
    rgw = inp['rg_conv_w'].reshape(L, 4, 4, 128).transpose(3, 0, 2, 1)
    pv[:, PV_RGW:PV_RGW + 32] = np.ascontiguousarray(rgw).reshape(128, 32)
    pv[:, PV_RGB:PV_RGB + 8] = chan(inp['rg_conv_b'])
    pv[:, PV_BA:PV_BA + 8] = chan(inp['rg_ba'])
    pv[:, PV_BX:PV_BX + 8] = chan(inp['rg_bx'])
    pv[:, PV_LAM:PV_LAM + 8] = chan(inp['rg_lambda'])
    cfw = inp['cf_conv_w'].reshape(L, 31, 4, 128).transpose(3, 0, 2, 1)
    pv[:, PV_CFW:PV_CFW + 248] = np.ascontiguousarray(cfw).reshape(128, 248)
    pv[:, PV_CFB:PV_CFB + 8] = chan(inp['cf_conv_b'])
    pv[:, PV_LNG:PV_LNG + 8] = chan(inp['cf_ln_g'])
    pv[:, PV_LNB:PV_LNB + 8] = chan(inp['cf_ln_b'])
    return pv


def _prep(inp):
    inp = {k: np.asarray(v) for k, v in inp.items()}
    f32 = np.float32
    pv = _pack_pv(inp)
    shared = {
        'pv': pv,
        'rg_wa': np.ascontiguousarray(inp['rg_wa'], f32), 'rg_wx': np.ascontiguousarray(inp['rg_wx'], f32),
        'w_in': np.ascontiguousarray(inp['w_in'], f32), 'w_out': np.ascontiguousarray(inp['w_out'], f32),
        'w_q': np.ascontiguousarray(inp['w_q'], f32), 'w_k': np.ascontiguousarray(inp['w_k'], f32),
        'w_v': np.ascontiguousarray(inp['w_v'], f32), 'w_o': np.ascontiguousarray(inp['w_o'], f32),
        'w_gate': np.ascontiguousarray(inp['w_gate'], f32), 'w_up': np.ascontiguousarray(inp['w_up'], f32),
        'w_down': np.ascontiguousarray(inp['w_down'], f32),
    }
    in_maps = []
    for c in range(NCORES):
        sl = slice(16 * c, 16 * (c + 1))
        xs = inp['x_sample'][sl].reshape(NSM, D)
        xT = np.ascontiguousarray(np.concatenate([inp['x_prompt'][c].T, xs.T], axis=1), f32)
        memT = np.ascontiguousarray(inp['mem_prompt'][c].T, f32)
        KcT = np.ascontiguousarray(inp['cache_mem_k'][:, sl].reshape(L, 16, 256, D).transpose(0, 1, 3, 2), f32)
        Vc = np.ascontiguousarray(inp['cache_mem_v'][:, sl].reshape(L, 16, 256, D), f32)
        h0 = np.ascontiguousarray(inp['state_rglru_h'][:, sl].reshape(L, 16, 4, 128).transpose(0, 3, 2, 1), f32)
        rgs = np.ascontiguousarray(
            inp['state_rglru_conv'][:, sl].reshape(L, 16, 3, 4, 128).transpose(0, 4, 3, 1, 2), f32)
        cfs = np.ascontiguousarray(
            inp['state_conf_conv'][:, sl].reshape(L, 16, 30, 4, 128).transpose(0, 4, 3, 1, 2), f32)
        m = dict(shared)
        m.update({'xT': xT, 'memT': memT, 'KcT': KcT, 'Vc': Vc, 'h0': h0, 'rgs': rgs, 'cfs': cfs})
        in_maps.append(m)
    return in_maps


def kernel(**inp):
    nc = _get_nc()
    in_maps = _prep(inp)
    res = run_bass_kernel_spmd(nc, in_maps, core_ids=list(range(NCORES)))
    return _post(res.results)


def _post(rs):
    f32 = np.float32
    B = NCORES
    y_prompt = np.empty((B, NPR, D), f32)
    y_sample = np.empty((128, 4, D), f32)
    p_h = np.empty((L, B, 512), f32)
    p_rg = np.empty((L, B, 3, 512), f32)
    p_cf = np.empty((L, B, 30, 512), f32)
    p_mk = np.empty((L, B, 256, 4, 256), f32)
    p_mv = np.empty((L, B, 256, 4, 256), f32)
    s_h = np.empty((L, 128, 512), f32)
    s_rg = np.empty((L, 128, 3, 512), f32)
    s_cf = np.empty((L, 128, 30, 512), f32)
    for c in range(NCORES):
        r = rs[c]
        sl = slice(16 * c, 16 * (c + 1))
        yT = r['yT']
        y_prompt[c] = yT[:, :NPR].T
        y_sample[sl] = yT[:, NPR:].T.reshape(16, 4, D)
        p_h[:, c] = r['o_ph'].transpose(0, 2, 1).reshape(L, 512)
        p_rg[:, c] = r['o_prg'].transpose(0, 3, 2, 1).reshape(L, 3, 512)
        p_cf[:, c] = r['o_pcf'].transpose(0, 3, 2, 1).reshape(L, 30, 512)
        p_mk[:, c] = r['o_pk'].transpose(0, 3, 2, 1).reshape(L, 256, 4, 256)
        p_mv[:, c] = r['o_pv'].reshape(L, 256, 4, 256)
        s_h[:, sl] = r['o_sh'].transpose(0, 3, 2, 1).reshape(L, 16, 512)
        s_rg[:, sl] = r['o_srg'].transpose(0, 3, 4, 2, 1).reshape(L, 16, 3, 512)
        scf = np.concatenate([r['o_scfh'], r['o_scfn']], axis=4)
        s_cf[:, sl] = scf.transpose(0, 3, 4, 2, 1).reshape(L, 16, 30, 512)
    return (y_prompt, y_sample, p_h, p_rg, p_cf, p_mk, p_mv, s_h, s_rg, s_cf)
```

```python
import itertools
import numpy as np
import concourse.bass as bass
import concourse.mybir as mybir
from concourse.bass_utils import run_bass_kernel_spmd

F32 = mybir.dt.float32
BF16 = mybir.dt.bfloat16
U8 = mybir.dt.uint8
AF = mybir.ActivationFunctionType
ALU = mybir.AluOpType

NCORES = 8
L = 2
D = 1024
KC = 8
NPR = 2048
NSM = 64
T = NPR + NSM
DFF = 2816
EPS = 1e-6
GRAN = 128
PSUM_BASE = 1 << 24
RING_SLOTS = 6
SLOT_BYTES = 8192

PV_GMIX, PV_GATT, PV_GFFN, PV_GMEM, PV_GFIN = 0, 16, 32, 48, 64
PV_RGW, PV_RGB, PV_BA, PV_BX, PV_LAM = 72, 104, 112, 120, 128
PV_CFW, PV_CFB, PV_LNG, PV_LNB = 136, 384, 392, 400
PV_N = 408


def _esz(dt):
    return 4 if dt == F32 else (2 if dt == BF16 else 1)


class Trk:
    def __init__(self, nc):
        self.nc = nc
        self.E = {'pe': nc.tensor, 'act': nc.scalar, 'dve': nc.vector, 'pool': nc.gpsimd, 'sp': nc.sync}
        self.sems = {}
        self.val = {}
        self.waited = {e: {} for e in self.E}
        self.lastw = {}
        self.rd = {}
        self.base = {}

    def reg(self, handle, addr):
        self.base[handle.name] = addr

    def sem(self, key):
        if key not in self.sems:
            self.sems[key] = self.nc.alloc_semaphore('s_' + key)
            self.val[key] = 0
        return self.sems[key]

    def grans(self, aps):
        out = set()
        for ap in aps:
            nm = ap.tensor.name
            if nm not in self.base:
                continue
            base = self.base[nm]
            es = _esz(ap.dtype)
            pat = ap.ap
            rowstride = pat[0][0]
            off = int(ap.offset)
            if rowstride > 0:
                off = off % rowstride
            free = pat[1:]
            if len(free) == 0:
                rngs = [(off, off + 1)]
            else:
                ist, icnt = free[-1]
                ilen = (icnt - 1) * abs(ist) + 1
                outer = free[:-1]
                rngs = []
                for idx in itertools.product(*[range(c) for (_, c) in outer]):
                    st = off + sum(i * s for i, (s, _) in zip(idx, outer))
                    rngs.append((st, st + ilen))
            for lo, hi in rngs:
                blo = base + lo * es
                bhi = base + hi * es
                for g in range(blo // GRAN, (bhi - 1) // GRAN + 1):
                    out.add(g)
        return out

    def _deps(self, rg, wg):
        need = {}
        for g in rg:
            w = self.lastw.get(g)
            if w is not None and need.get(w[0], 0) < w[1]:
                need[w[0]] = w[1]
        for g in wg:
            w = self.lastw.get(g)
            if w is not None and need.get(w[0], 0) < w[1]:
                need[w[0]] = w[1]
            r = self.rd.get(g)
            if r:
                for k, v in r.items():
                    if need.get(k, 0) < v:
                        need[k] = v
        return need

    def _commit(self, rg, wg, key, val):
        for g in rg:
            d = self.rd.get(g)
            if d is None:
                self.rd[g] = {key: val}
            else:
                d[key] = val
        for g in wg:
            self.lastw[g] = (key, val)
            self.rd[g] = None

    def _wait(self, e, need):
        eng = self.E[e]
        wd = self.waited[e]
        for k, v in need.items():
            if k not in self.E:
                v = self.val[k]
            if wd.get(k, 0) < v:
                eng.wait_ge(self.sems[k], v)
                wd[k] = v

    def op(self, e, reads, writes, fn):
        rg = self.grans(reads)
        wg = self.grans(writes)
        self._wait(e, self._deps(rg, wg))
        ins = fn()
        s = self.sem(e)
        self.val[e] += 1
        ins.then_inc(s, 1)
        self._commit(rg, wg, e, self.val[e])

    def dma(self, q, out, in_, key):
        reads = [in_] if in_.tensor.name in self.base else []
        writes = [out] if out.tensor.name in self.base else []
        rg = self.grans(reads)
        wg = self.grans(writes)
        self._wait(q, self._deps(rg, wg))
        s = self.sem(key)
        ins = self.E[q].dma_start(out=out, in_=in_)
        self.val[key] += 16
        ins.then_inc(s, 16)
        self._commit(rg, wg, key, self.val[key])


class Region:
    def __init__(self, nc, trk, base, size):
        self.nc, self.trk, self.base, self.size = nc, trk, base, size
        self.ptr = 0

    def reset(self):
        self.ptr = 0

    def alloc(self, name, shape, dt):
        n = 1
        for s in shape[1:]:
            n *= s
        nbytes = n * _esz(dt)
        nbytes = (nbytes + GRAN - 1) // GRAN * GRAN
        assert self.ptr + nbytes <= self.size, (name, self.ptr, nbytes, self.size)
        addr = self.base + self.ptr
        t = self.nc.alloc_sbuf_tensor_at(name, list(shape), dt, offset=addr)
        self.trk.reg(t, addr)
        self.ptr += nbytes
        return t


class _Stop(Exception):
    pass


_STOP = None


class Builder:
    def ck(self, name):
        if _STOP == name:
            raise _Stop()

    def __init__(self):
        self.nc = bass.Bass("TRN2", target_bir_lowering=False)
        self.trk = Trk(self.nc)
        self.pb = 0
        self.cf_prebuilt = set()
        self.cf_handle = {}

    def bank(self):
        self.pb = (self.pb + 1) % 8
        return self.ps[self.pb]

    def mm(self, out, pairs, extra_reads=()):
        reads = []
        for a, b in pairs:
            reads.append(a)
            reads.append(b)
        n = len(pairs)

        def fn():
            ins = None
            for i, (a, b) in enumerate(pairs):
                ins = self.nc.tensor.matmul(out, lhsT=a, rhs=b, start=(i == 0), stop=(i == n - 1))
            return ins
        self.trk.op('pe', reads, [out], fn)

    def mm_groups(self, groups):
        reads, writes = [], []
        for out, pairs in groups:
            writes.append(out)
            for a, b in pairs:
                reads.append(a)
                reads.append(b)

        def fn():
            ins = None
            for out, pairs in groups:
                n = len(pairs)
                for i, (a, b) in enumerate(pairs):
                    ins = self.nc.tensor.matmul(out, lhsT=a, rhs=b, start=(i == 0), stop=(i == n - 1))
            return ins
        self.trk.op('pe', reads, writes, fn)

    def act(self, out, in_, func, bias=None, scale=None):
        reads = [in_]
        kw = {}
        if bias is not None:
            kw['bias'] = bias
            if not isinstance(bias, (int, float)):
                reads.append(bias)
        if scale is not None:
            kw['scale'] = scale
            if not isinstance(scale, (int, float)):
                reads.append(scale)
        self.trk.op('act', reads, [out], lambda: self.nc.scalar.activation(out=out, in_=in_, func=func, **kw))

    def tt(self, out, in0, in1, op, eng='dve'):
        e = self.trk.E[eng]
        self.trk.op(eng, [in0, in1], [out], lambda: e.tensor_tensor(out=out, in0=in0, in1=in1, op=op))

    def ts(self, out, in0, s1, s2, op0, op1=None, eng='dve'):
        e = self.trk.E[eng]
        reads = [in0]
        if not isinstance(s1, (int, float)):
            reads.append(s1)
        if s2 is not None and not isinstance(s2, (int, float)):
            reads.append(s2)
        if op1 is None:
            self.trk.op(eng, reads, [out], lambda: e.tensor_scalar(out=out, in0=in0, scalar1=s1, scalar2=None, op0=op0))
        else:
            self.trk.op(eng, reads, [out], lambda: e.tensor_scalar(out=out, in0=in0, scalar1=s1, scalar2=s2, op0=op0, op1=op1))

    def stt(self, out, in0, scalar, in1, op0, op1):
        reads = [in0, in1]
        if not isinstance(scalar, (int, float)):
            reads.append(scalar)
        self.trk.op('dve', reads, [out], lambda: self.nc.vector.scalar_tensor_tensor(
            out=out, in0=in0, scalar=scalar, in1=in1, op0=op0, op1=op1))

    def scan(self, out, d0, d1, initial):
        reads = [d0, d1]
        if not isinstance(initial, (int, float)):
            reads.append(initial)
        self.trk.op('dve', reads, [out], lambda: self.nc.vector.tensor_tensor_scan(
            out=out, data0=d0, data1=d1, initial=initial, op0=ALU.mult, op1=ALU.add))

    def recip(self, out, in_):
        self.trk.op('dve', [in_], [out], lambda: self.nc.vector.reciprocal(out=out, in_=in_))

    def copy(self, out, in_, eng='dve'):
        e = self.trk.E[eng]
        self.trk.op(eng, [in_], [out], lambda: e.tensor_copy(out=out, in_=in_))

    def memset(self, out, v, eng='dve'):
        e = self.trk.E[eng]
        self.trk.op(eng, [], [out], lambda: e.memset(out, v))

    def ring_plan(self, blocks):
        self.blocks = blocks
        self.blk_view = [None] * len(blocks)
        self.blk_emitted = 0
        self.blk_released = [False] * len(blocks)

    def _ring_emit(self, i):
        ap = self.blocks[i]
        a, b = ap.shape[1], ap.shape[2]
        slot = i % RING_SLOTS
        v = self.ring[slot][:, 0:a * b].rearrange("p (a b) -> p a b", a=a)
        self.trk.dma('pool', v, ap, 'ring%d' % slot)
        self.blk_view[i] = v

    def _ring_pump(self, upto=None):
        while self.blk_emitted < len(self.blocks):
            j = self.blk_emitted
            if upto is not None and j <= upto:
                pass
            elif j >= RING_SLOTS and not self.blk_released[j - RING_SLOTS]:
                break
            if j >= RING_SLOTS:
                assert self.blk_released[j - RING_SLOTS], "ring overflow: block %d" % j
            self._ring_emit(j)
            self.blk_emitted += 1

    def wget(self, i):
        self._ring_pump(upto=i)
        return self.blk_view[i]

    def wrel(self, i):
        self.blk_released[i] = True
        self._ring_pump()

    def norm(self, xin, gvec, out_fn, N, sq, rstd):
        for kc in range(KC):
            self.act(sq[:, kc, 0:N], xin(kc), AF.Square)
        bk = self.bank()
        self.mm(bk[:, 0:N], [(self.c1024[:], sq[:, kc, 0:N]) for kc in range(KC)])
        self.act(rstd[:, 0:N], bk[:, 0:N], AF.Ln, bias=EPS)
        self.act(rstd[:, 0:N], rstd[:, 0:N], AF.Exp, scale=-0.5)
        for kc in range(KC):
            self.stt(out_fn(kc), xin(kc), gvec(kc), rstd[:, 0:N], ALU.mult, ALU.mult)

    def build(self):
        nc = self.nc
        trk = self.trk
        dr = {}

        def din(name, shape):
            dr[name] = nc.dram_tensor(name, list(shape), F32, kind="ExternalInput").ap()
            return dr[name]

        def dout(name, shape):
            dr[name] = nc.dram_tensor(name, list(shape), F32, kind="ExternalOutput").ap()
            return dr[name]

        xT = din("xT", [D, T])
        memT = din("memT", [D, 256])
        KcT = din("KcT", [L, 16, D, 256])
        Vc = din("Vc", [L, 16, 256, D])
        h0 = din("h0", [L, 128, 4, 16])
        rgs = din("rgs", [L, 128, 4, 16, 3])
        cfs = din("cfs", [L, 128, 4, 16, 30])
        pvd = din("pv", [128, PV_N])
        rg_wa = din("rg_wa", [L, 8, 64, 64])
        rg_wx = din("rg_wx", [L, 8, 64, 64])
        w_in = din("w_in", [L, D, 2048])
        w_out = din("w_out", [L, D, D])
        w_q = din("w_q", [L, D, D])
        w_k = din("w_k", [L, D, D])
        w_v = din("w_v", [L, D, D])
        w_o = din("w_o", [L, D, D])
        w_gate = din("w_gate", [L, D, DFF])
        w_up = din("w_up", [L, D, DFF])
        w_down = din("w_down", [L, DFF, D])

        yT = dout("yT", [D, T])
        o_ph = dout("o_ph", [L, 128, 4])
        o_prg = dout("o_prg", [L, 128, 4, 3])
        o_pcf = dout("o_pcf", [L, 128, 4, 30])
        o_pk = dout("o_pk", [L, 128, 8, 256])
        o_pv = dout("o_pv", [L, 256, D])
        o_sh = dout("o_sh", [L, 128, 4, 16])
        o_srg = dout("o_srg", [L, 128, 4, 16, 3])
        o_scfh = dout("o_scfh", [L, 128, 4, 16, 26])
        o_scfn = dout("o_scfn", [L, 128, 4, 16, 4])

        ARENA = 212800
        arena = nc.alloc_sbuf_tensor("arena", [128, ARENA], U8)
        abase = nc.lookup_mloc(arena).addr
        P = Region(nc, trk, abase, ARENA)
        X = P.alloc("X", [128, KC, T], F32)
        identf = P.alloc("identf", [128, 128], F32)
        ident = P.alloc("ident", [128, 128], BF16)
        self.c1024 = P.alloc("c1024", [128, 128], BF16)
        c512 = P.alloc("c512", [128, 128], BF16)
        ones = P.alloc("ones", [128, 128], BF16)
        pv = P.alloc("pvs", [128, PV_N], F32)
        cl = P.alloc("cl", [128, L * 4], F32)
        cl2 = P.alloc("cl2", [128, L * 4], F32)
        wabd = P.alloc("wabd", [128, L * 2 * 4, 128], BF16)
        hstate = P.alloc("hstate", [128, 4], F32)
        self.ring = [P.alloc("ring%d" % i, [128, SLOT_BYTES // 2], BF16) for i in range(RING_SLOTS)]
        rbase = abase + P.ptr
        R = Region(nc, trk, rbase, ARENA - P.ptr)
        self.ps = []
        for i in range(8):
            t = nc.alloc_psum_tensor("ps%d" % i, [128, 512], F32)
            trk.reg(t, PSUM_BASE + i * 2048)
            self.ps.append(t)

        blocks = []
        bidx = {}

        def wv(w, l):
            return w[l].rearrange("(kc p) n -> p kc n", p=128)

        for l in range(L):
            for c in range(4):
                bidx[(l, 'in', c)] = len(blocks)
                blocks.append(wv(w_in, l)[:, :, c * 512:(c + 1) * 512])
            for c in range(2):
                bidx[(l, 'out', c)] = len(blocks)
                blocks.append(wv(w_out, l)[:, :, c * 512:(c + 1) * 512])
            for nm, w in (('k', w_k), ('v', w_v), ('q', w_q), ('o', w_o)):
                for c in range(2):
                    bidx[(l, nm, c)] = len(blocks)
                    blocks.append(wv(w, l)[:, :, c * 512:(c + 1) * 512])
            for g in range(3):
                ncols = 1024 if g < 2 else 768
                c0 = g * 1024
                for hb in range(2):
                    lo = c0 + hb * 512
                    hi = min(c0 + ncols, lo + 512)
                    bidx[(l, 'gate', g, hb)] = len(blocks)
                    blocks.append(wv(w_gate, l)[:, :, lo:hi])
                    bidx[(l, 'up', g, hb)] = len(blocks)
                    blocks.append(wv(w_up, l)[:, :, lo:hi])
                nch = ncols // 128
                for hb in range(2):
                    k0 = hb * 4
                    k1 = min(nch, k0 + 4)
                    bidx[(l, 'down', g, hb)] = len(blocks)
                    blocks.append(w_down[l, c0 + k0 * 128:c0 + k1 * 128, :].rearrange("(kc p) n -> p kc n", p=128))
        self.ring_plan(blocks)

        A = self.act
        xv = xT.rearrange("(kc p) n -> p kc n", p=128)
        yv = yT.rearrange("(kc p) n -> p kc n", p=128)

        def pvc(off, n=1):
            return pv[:, off:off + n]

        with nc.Block():
            trk.dma('sp', pv[:], pvd[:, :], 'ld_pv')
            for ti, c0 in enumerate(range(0, T, 512)):
                n = min(512, T - c0)
                trk.dma('sp', X[:, :, c0:c0 + n], xv[:, :, c0:c0 + n], 'ld_x%d' % ti)
            for l in range(L):
                for j in range(4):
                    trk.dma('sp', o_scfh[l, :, j], cfs[l, :, j, :, 4:30], 'st_h')
            self.memset(identf[:], 0.0, eng='dve')
            trk.op('pool', [identf[:]], [identf[:]], lambda: nc.gpsimd.affine_select(
                out=identf[:], in_=identf[:], pattern=[[-1, 128]], compare_op=ALU.not_equal, fill=1.0,
                base=0, channel_multiplier=1))
            self.copy(ident[:], identf[:])
            self.memset(self.c1024[:], 1.0 / 1024.0)
            self.memset(c512[:], 1.0 / 512.0)
            self.memset(ones[:], 1.0)
            R.reset()
            wst = R.alloc("wst", [128, L * 2 * 4, 128], F32)
            self.memset(wst[:], 0.0)
            for l in range(L):
                for gi, wsrc in enumerate((rg_wa, rg_wx)):
                    for hh in range(8):
                        j, e = hh // 2, hh % 2
                        trk.dma('act', wst[e * 64:(e + 1) * 64, (l * 2 + gi) * 4 + j, e * 64:(e + 1) * 64],
                                wsrc[l, hh, :, :], 'ld_w%d' % ((l * 2 + gi) % 2))
            self.copy(wabd[:], wst[:])
            spt = R.alloc("spt", [128, L * 4], F32)
            A(spt[:], pvc(PV_LAM, 8), AF.Exp, scale=-1.0)
            A(spt[:], spt[:], AF.Ln, bias=1.0)
            self.ts(cl[:], spt[:], -8.0, None, ALU.mult)
            self.ts(cl2[:], spt[:], -16.0, None, ALU.mult)

            try:
                self.ck('P')
                for l in range(L):
                    self.layer(l, locals())
                if not getattr(self, 'final_done', False):
                    self.final_norm(locals())
            except _Stop:
                pass
            for k, v in trk.val.items():
                if k not in trk.E and v > 0:
                    nc.sync.wait_ge(trk.sems[k], v)
        return nc

    def build_diag(self, l, cfD, rgD, identf, pvc, do_cf=True, do_rg=True):
        MUL = ALU.mult
        for j in range(4):
            if do_rg:
                for k in range(4):
                    self.ts(rgD[:, j * 4 + k, :], identf[:], pvc(PV_RGW + (l * 4 + j) * 4 + k), None, MUL)
            if do_cf:
                for k in range(31):
                    if k % 3 == 2:
                        self.act(cfD[:, j * 31 + k, :], identf[:], AF.Copy,
                                 scale=pvc(PV_CFW + (l * 4 + j) * 31 + k))
                    else:
                        self.ts(cfD[:, j * 31 + k, :], identf[:], pvc(PV_CFW + (l * 4 + j) * 31 + k), None, MUL)

    def layer(self, l, env):
        nc, trk = self.nc, self.trk
        R = env['R']; X = env['X']; pv = env['pv']; bidx = env['bidx']
        ident, identf, c512, ones = env['ident'], env['identf'], env['c512'], env['ones']
        cl, cl2, wabd, hstate = env['cl'], env['cl2'], env['wabd'], env['hstate']
        dr_ = env['dr']
        A = self.act
        MUL, ADD, SUB = ALU.mult, ALU.add, ALU.subtract

        def pvc(off, n=1):
            return pv[:, off:off + n]

        R.reset()
        cfD = R.alloc("cfD", [128, 4 * 31, 128], BF16)
        if l in self.cf_handle:
            cfD = self.cf_handle[l]
        rgD = R.alloc("rgD", [128, 4 * 4, 128], BF16)
        NA = 256
        ycat = R.alloc("a_ycat", [128, KC, NA], BF16)
        sq = R.alloc("a_sq", [128, KC, NA], BF16)
        rstd = R.alloc("a_rstd", [128, NA], F32)
        xn = R.alloc("a_xn", [128, KC, NA], BF16)
        o_x = R.ptr
        xrb0 = R.alloc("a_xrb0", [128, 4, 3 + NA], BF16)
        cb0 = R.alloc("a_cb0", [128, 4, 30 + NA], BF16)
        o_end = R.ptr
        R.ptr = o_x
        csb = R.alloc("a_csb", [128, 4, 16, 34], BF16)
        assert R.ptr <= o_end
        R.ptr = o_end
        xrb1 = R.alloc("a_xrb1", [128, 4, 3 + NA], BF16)
        cb1 = R.alloc("a_cb1", [128, 4, 30 + NA], BF16)
        xrb2 = [xrb0, xrb1]
        cb2 = [cb0, cb1]
        gg2 = [R.alloc("a_gg%d" % i, [128, 4, NA], BF16) for i in range(2)]
        sg = R.alloc("a_sg", [128, NA], F32)
        xc2 = [R.alloc("a_xc%d" % i, [128, NA], F32) for i in range(2)]
        xcb = R.alloc("a_xcb", [128, NA], BF16)
        r_ = R.alloc("a_r", [128, NA], F32)
        i_ = R.alloc("a_i", [128, NA], F32)
        a_ = R.alloc("a_a", [128, NA], F32)
        m_ = R.alloc("a_m", [128, NA], F32)
        b_ = R.alloc("a_b", [128, NA], F32)
        h_ = R.alloc("a_h", [128, NA], F32)
        ccf = R.alloc("a_ccf", [128, 4, NA], F32)
        ccb = R.alloc("a_ccb", [128, 4, NA], BF16)
        sqb = R.alloc("a_sqb", [128, 4, NA], BF16)
        mean = R.alloc("a_mean", [128, NA], F32)
        var = R.alloc("a_var", [128, NA], F32)
        rstc = R.alloc("a_rstc", [128, NA], F32)
        o_dd = R.ptr
        dd = R.alloc("a_dd", [128, NA], F32)
        dd2 = [dd, sg]
        prg = R.alloc("a_prg", [128, 4, 3], F32)
        pcf = R.alloc("a_pcf", [128, 4, 30], F32)
        xrs = R.alloc("a_xrs", [128, 4, 16, 7], BF16)
        o_keep = R.ptr
        R.ptr = o_dd
        xrsf = R.alloc("a_xrsf", [128, 4, 16, 3], F32)
        R.ptr = o_keep
        srg = R.alloc("a_srg", [128, 4, 16, 3], F32)
        cs4 = R.alloc("a_cs4", [128, 4, 16, 4], F32)
        h0s = R.alloc("a_h0s", [128, 4, 16], F32)
        shl = R.alloc("a_shl", [128, 4, 16], F32)
        tmp16 = R.alloc("a_tmp16", [128, 16], F32)
        nb2 = R.alloc("a_nb2", [128, 8], F32)

        self.build_diag(l, cfD, rgD, identf, pvc, do_cf=(l not in self.cf_prebuilt))
        self.ts(nb2[:, 0:4], pvc(PV_BA + l * 4, 4), -1.0, None, MUL)
        self.ts(nb2[:, 4:8], pvc(PV_BX + l * 4, 4), -1.0, None, MUL)
        self.memset(xrb0[:, :, 0:3], 0.0)
        self.memset(cb0[:, :, 0:30], 0.0)
        self.memset(hstate[:], 0.0)

        win = [self.wget(bidx[(l, 'in', c)]) for c in range(4)]
        wout = [self.wget(bidx[(l, 'out', c)]) for c in range(2)]
        tiles = [(c0, NA, False) for c0 in range(0, NPR, NA)] + [(NPR, NSM, True)]
        NT = len(tiles)

        def load_sample_state():
            trk.dma('sp', xrsf[:], dr_['rgs'][l], 'ld_st0')
            trk.dma('sp', h0s[:], dr_['h0'][l], 'ld_st1')
            for j in range(4):
                trk.dma('pool', csb[:, j, :, 0:30], dr_['cfs'][l, :, j], 'ld_st%d' % (2 + j))
            self.copy(xrs[:, :, :, 0:3], xrsf[:])

        def v3(ap):
            return ap.rearrange("p (b t) -> p b t", t=4)

        def front(ti):
            c0, N, smp = tiles[ti]
            sset = ti % 2
            xrb, cb, gg = xrb2[sset], cb2[sset], gg2[sset]
            last_p = (not smp) and (c0 + N == NPR)
            if smp:
                load_sample_state()
            elif ti >= 1:
                self.copy(xrb[:, :, 0:3], xrb2[1 - sset][:, :, NA:NA + 3])
                self.copy(cb[:, :, 0:30], cb2[1 - sset][:, :, NA:NA + 30])
            self.norm(lambda kc: X[:, kc, c0:c0 + N], lambda kc: pvc(PV_GMIX + l * 8 + kc),
                      lambda kc: xn[:, kc, 0:N], N, sq, rstd)
            yield

            def proj(blk, j):
                bk = self.bank()
                self.mm(bk[:, 0:N], [(win[blk][:, kc, j * 128:(j + 1) * 128], xn[:, kc, 0:N]) for kc in range(KC)])
                return bk
            for j in range(4):
                bk = proj(0, j)
                if smp:
                    A(xrs[:, j, :, 3:7], v3(bk[:, 0:N]), AF.Copy)
                    A(srg[:, j, :, :], v3(bk[:, 0:N])[:, :, 1:4], AF.Copy)
                else:
                    A(xrb[:, j, 3:3 + N], bk[:, 0:N], AF.Copy)
                    if last_p:
                        A(prg[:, j, :], bk[:, N - 3:N], AF.Copy)
            yield
            yield
            yield
            yield
            for j in range(4):
                bk = proj(1, j)
                A(gg[:, j, 0:N], bk[:, 0:N], AF.Gelu_apprx_tanh)
            yield
            for half in range(1):
                for j in range(4):
                    bv = proj(2, j)
                    bg = proj(3, j)
                    A(sg[:, 0:N], bg[:, 0:N], AF.Sigmoid)
                    if smp:
                        self.tt(cs4[:, j, :, :], v3(bv[:, 0:N]), v3(sg[:, 0:N]), MUL)
                        self.copy(csb[:, j, :, 30:34], cs4[:, j, :, :])
                    else:
                        self.tt(cb[:, j, 30:30 + N], bv[:, 0:N], sg[:, 0:N], MUL)
                        if last_p:
                            self.tt(pcf[:, j, :], bv[:, N - 30:N], sg[:, N - 30:N], MUL)
                yield

        def back(ti):
            c0, N, smp = tiles[ti]
            sset = ti % 2
            xrb, cb, gg = xrb2[sset], cb2[sset], gg2[sset]

            def tail(j):
                xc = xc2[j % 2]
                self.tt(b_[:, 0:N], i_[:, 0:N], xc[:, 0:N], MUL)
                self.tt(b_[:, 0:N], b_[:, 0:N], m_[:, 0:N], MUL)
                if smp:
                    av = v3(a_[:, 0:N])
                    bvw = v3(b_[:, 0:N])
                    self.tt(tmp16[:], av[:, :, 0], h0s[:, j, :], MUL)
                    self.tt(bvw[:, :, 0], bvw[:, :, 0], tmp16[:], ADD)
                    self.memset(av[:, :, 0], 0.0)
                    self.scan(h_[:, 0:N], a_[:, 0:N], b_[:, 0:N], 0.0)
                    self.copy(shl[:, j, :], v3(h_[:, 0:N])[:, :, 3])
                else:
                    self.scan(h_[:, 0:N], a_[:, 0:N], b_[:, 0:N], hstate[:, j:j + 1])
                    self.copy(hstate[:, j:j + 1], h_[:, N - 1:N])
                self.tt(ycat[:, j, 0:N], h_[:, 0:N], gg[:, j, 0:N], MUL)

            for j in range(4):
                xc = xc2[j % 2]
                bk = self.bank()
                if smp:
                    self.mm(bk[:, 0:N], [(rgD[:, j * 4 + k, :], xrs[:, j, :, k:k + 4]) for k in range(4)])
                else:
                    self.mm(bk[:, 0:N], [(rgD[:, j * 4 + k, :], xrb[:, j, k:k + N]) for k in range(4)])
                bias = pvc(PV_RGB + l * 4 + j)
                self.ts(xcb[:, 0:N], bk[:, 0:N], bias, None, ADD)
                self.ts(xc[:, 0:N], bk[:, 0:N], bias, None, ADD)
                bc = self.bank()
                if smp:
                    self.mm(bc[:, 0:N], [(cfD[:, j * 31 + k, :], csb[:, j, :, k:k + 4]) for k in range(31)])
                else:
                    self.mm(bc[:, 0:N], [(cfD[:, j * 31 + k, :], cb[:, j, k:k + N]) for k in range(31)])
                ba = self.bank()
                self.mm(ba[:, 0:N], [(wabd[:, (l * 2 + 0) * 4 + j, :], xcb[:, 0:N])])
                bx = self.bank()
                self.mm(bx[:, 0:N], [(wabd[:, (l * 2 + 1) * 4 + j, :], xcb[:, 0:N])])
                cbias = pvc(PV_CFB + l * 4 + j)
                self.ts(ccf[:, j, 0:N], bc[:, 0:N], cbias, None, ADD)
                self.ts(ccb[:, j, 0:N], bc[:, 0:N], cbias, None, ADD)
                self.stt(sqb[:, j, 0:N], bc[:, 0:N], cbias, ccf[:, j, 0:N], ADD, MUL)
                if j >= 1:
                    tail(j - 1)
                A(r_[:, 0:N], ba[:, 0:N], AF.Exp, scale=-1.0, bias=nb2[:, j:j + 1])
                A(i_[:, 0:N], bx[:, 0:N], AF.Exp, scale=-1.0, bias=nb2[:, 4 + j:5 + j])
                A(r_[:, 0:N], r_[:, 0:N], AF.Ln, bias=1.0)
                A(i_[:, 0:N], i_[:, 0:N], AF.Ln, bias=1.0)
                A(r_[:, 0:N], r_[:, 0:N], AF.Exp, scale=-1.0)
                A(i_[:, 0:N], i_[:, 0:N], AF.Exp, scale=-1.0)
                A(a_[:, 0:N], r_[:, 0:N], AF.Exp, scale=cl[:, l * 4 + j:l * 4 + j + 1])
                A(m_[:, 0:N], r_[:, 0:N], AF.Exp, scale=cl2[:, l * 4 + j:l * 4 + j + 1])
                A(m_[:, 0:N], m_[:, 0:N], AF.Ln, scale=-1.0, bias=1.0000001)
                A(m_[:, 0:N], m_[:, 0:N], AF.Exp, scale=0.5)
                yield
            tail(3)
            bm = self.bank()
            self.mm(bm[:, 0:N], [(c512[:], ccb[:, j, 0:N]) for j in range(4)])
            bq = self.bank()
            self.mm(bq[:, 0:N], [(c512[:], sqb[:, j, 0:N]) for j in range(4)])
            A(mean[:, 0:N], bm[:, 0:N], AF.Copy)
            self.tt(var[:, 0:N], mean[:, 0:N], mean[:, 0:N], MUL)
            self.tt(var[:, 0:N], bq[:, 0:N], var[:, 0:N], SUB)
            A(rstc[:, 0:N], var[:, 0:N], AF.Ln, bias=EPS)
            A(rstc[:, 0:N], rstc[:, 0:N], AF.Exp, scale=-0.5)
            yield
            for j in range(4):
                dd = dd2[j % 2]
                self.tt(dd[:, 0:N], ccf[:, j, 0:N], mean[:, 0:N], SUB)
                self.tt(dd[:, 0:N], dd[:, 0:N], rstc[:, 0:N], MUL)
                A(ycat[:, 4 + j, 0:N], dd[:, 0:N], AF.Silu, bias=pvc(PV_LNB + l * 4 + j),
                  scale=pvc(PV_LNG + l * 4 + j))
            yield
            for oc in range(KC):
                bk = self.bank()
                self.mm(bk[:, 0:N], [(wout[oc // 4][:, kc, (oc % 4) * 128:(oc % 4 + 1) * 128], ycat[:, kc, 0:N])
                                     for kc in range(KC)])
                self.tt(X[:, oc, c0:c0 + N], X[:, oc, c0:c0 + N], bk[:, 0:N], ADD)
            yield

        def drive(gens):
            gens = [g for g in gens if g is not None]
            while gens:
                alive = []
                for g in gens:
                    try:
                        next(g)
                        alive.append(g)
                    except StopIteration:
                        pass
                gens = alive

        drive([front(0)])
        for ti in range(NT):
            drive([front(ti + 1) if ti + 1 < NT else None, back(ti)])
            if ti == NT - 2:
                for c in range(4):
                    self.wrel(bidx[(l, 'in', c)])

        trk.dma('sp', dr_['o_ph'][l], hstate[:], 'st_a0')
        trk.dma('sp', dr_['o_prg'][l], prg[:], 'st_a1')
        trk.dma('sp', dr_['o_pcf'][l], pcf[:], 'st_a2')
        trk.dma('sp', dr_['o_sh'][l], shl[:], 'st_a3')
        trk.dma('sp', dr_['o_srg'][l], srg[:], 'st_a4')
        trk.dma('sp', dr_['o_scfn'][l], cs4[:], 'st_a5')
        for c in range(2):
            self.wrel(bidx[(l, 'out', c)])

        self.ck('A%d' % l)
        R.reset()
        NB = 512
        KpT = R.alloc("b_KpT", [128, KC, 256], BF16)
        Vp = R.alloc("b_Vp", [128, 2, D], BF16)
        Ks = [R.alloc("b_Ks%d" % i, [128, KC, 256], BF16) for i in range(2)]
        Vs = [R.alloc("b_Vs%d" % i, [128, 2, D], BF16) for i in range(2)]
        mark = R.ptr
        memf = R.alloc("b_memf", [128, KC, 256], F32)
        memn = R.alloc("b_memn", [128, KC, 256], BF16)
        msq = R.alloc("b_msq", [128, KC, 256], BF16)
        mrs = R.alloc("b_mrs", [128, 256], F32)
        kst = R.alloc("b_kst", [128, KC, 256], F32)
        vst = R.alloc("b_vst", [128, 2, D], F32)
        trk.dma('sp', memf[:], dr_['memT'].rearrange("(kc p) n -> p kc n", p=128), 'ld_mem')
        self.norm(lambda kc: memf[:, kc, :], lambda kc: pvc(PV_GMEM + l * 8 + kc),
                  lambda kc: memn[:, kc, :], 256, msq, mrs)
        self.ck('Bn%d' % l)
        wk = [self.wget(bidx[(l, 'k', c)]) for c in range(2)]
        self.ck('Bw%d' % l)
        for dc in range(KC):
            bk = self.bank()
            self.mm(bk[:, 0:256], [(wk[dc // 4][:, kc, (dc % 4) * 128:(dc % 4 + 1) * 128], memn[:, kc, :])
                                   for kc in range(KC)])
            A(KpT[:, dc, :], bk[:, 0:256], AF.Copy)
            A(kst[:, dc, :], bk[:, 0:256], AF.Copy)
        self.ck('Bk%d' % l)
        trk.dma('sp', dr_['o_pk'][l], kst[:], 'st_bk')
        for c in range(2):
            self.wrel(bidx[(l, 'k', c)])
        wvv = [self.wget(bidx[(l, 'v', c)]) for c in range(2)]
        for mt in range(2):
            for cbk in range(2):
                bk = self.bank()
                self.mm(bk[:, :], [(memn[:, kc, mt * 128:(mt + 1) * 128], wvv[cbk][:, kc, :]) for kc in range(KC)])
                A(Vp[:, mt, cbk * 512:(cbk + 1) * 512], bk[:, :], AF.Copy)
                A(vst[:, mt, cbk * 512:(cbk + 1) * 512], bk[:, :], AF.Copy)
        trk.dma('sp', dr_['o_pv'][l].rearrange("(j p) d -> p j d", p=128), vst[:], 'st_bv')
        for c in range(2):
            self.wrel(bidx[(l, 'v', c)])
        self.ck('Ba%d' % l)
        R.ptr = mark
        sq = R.alloc("b_sq", [128, KC, NB], BF16)
        rstd = R.alloc("b_rstd", [128, NB], F32)
        xn = R.alloc("b_xn", [128, KC, NB], BF16)
        qT2 = [R.alloc("b_qT%d" % i, [128, KC, NB], BF16) for i in range(2)]
        eT2 = [R.alloc("b_eT%d" % i, [128, 2, NB], BF16) for i in range(2)]
        rs2 = [R.alloc("b_rs%d" % i, [128, NB], F32) for i in range(2)]
        oT = R.alloc("b_oT", [128, KC, NB], BF16)
        qTs = R.alloc("b_qTs", [128, KC, NSM], BF16)
        oTs = R.alloc("b_oTs", [128, KC, NSM], BF16)
        esb = [R.alloc("b_esb%d" % i, [128, 32], BF16) for i in range(2)]
        ssb = R.alloc("b_ssb", [128, 32], F32)
        rsb = R.alloc("b_rsb", [128, 16], F32)
        wq = [self.wget(bidx[(l, 'q', c)]) for c in range(2)]
        wo = [self.wget(bidx[(l, 'o', c)]) for c in range(2)]

        def qproj(c0, N, qdst):
            self.norm(lambda kc: X[:, kc, c0:c0 + N], lambda kc: pvc(PV_GATT + l * 8 + kc),
                      lambda kc: xn[:, kc, 0:N], N, sq, rstd)
            yield
            for oc in range(KC):
                bk = self.bank()
                self.mm(bk[:, 0:N], [(wq[oc // 4][:, kc, (oc % 4) * 128:(oc % 4 + 1) * 128], xn[:, kc, 0:N])
                                     for kc in range(KC)])
                A(qdst[:, oc, 0:N], bk[:, 0:N], AF.Copy, scale=1.0 / 16.0)
                if oc == 3:
                    yield
            yield

        def oproj(c0, N, osrc):
            for oc in range(KC):
                bk = self.bank()
                self.mm(bk[:, 0:N], [(wo[oc // 4][:, kc, (oc % 4) * 128:(oc % 4 + 1) * 128], osrc[:, kc, 0:N])
                                     for kc in range(KC)])
                self.tt(X[:, oc, c0:c0 + N], X[:, oc, c0:c0 + N], bk[:, 0:N], ADD)

        def kv_load(b):
            trk.dma('pool', Ks[b % 2][:], dr_['KcT'][l, b].rearrange("(kc p) m -> p kc m", p=128), 'ld_k%d' % (b % 2))
            trk.dma('pool', Vs[b % 2][:], dr_['Vc'][l, b].rearrange("(j p) d -> p j d", p=128), 'ld_v%d' % (b % 2))

        def sample_batch(b):
            if b + 1 < 16:
                kv_load(b + 1)
            kb, vb, es = Ks[b % 2], Vs[b % 2], esb[b % 2]
            bsc = self.bank()
            groups = []
            for hh in range(4):
                for jm in range(2):
                    col = hh * 8 + jm * 4
                    groups.append((bsc[:, col:col + 4],
                                   [(kb[:, 2 * hh + dc, jm * 128:(jm + 1) * 128],
                                     qTs[:, 2 * hh + dc, b * 4:(b + 1) * 4]) for dc in range(2)]))
            self.mm_groups(groups)
            A(es[:, :], bsc[:, 0:32], AF.Exp)
            bs = self.bank()
            self.mm(bs[:, 0:32], [(ones[:], es[:, :])])
            A(ssb[:, :], bs[:, 0:32], AF.Copy)
            sv = ssb[:, :].rearrange("p (h j t) -> p h j t", j=2, t=4)
            self.tt(rsb[:, :].rearrange("p (h t) -> p h t", t=4), sv[:, :, 0, :], sv[:, :, 1, :], ADD)
            self.recip(rsb[:, :], rsb[:, :])
            bo = self.bank()
            groups = []
            for hh in range(4):
                for dc in range(2):
                    col = (hh * 2 + dc) * 4
                    groups.append((bo[:, col:col + 4],
                                   [(vb[:, jm, hh * 256 + dc * 128:hh * 256 + (dc + 1) * 128],
                                     es[:, hh * 8 + jm * 4:hh * 8 + jm * 4 + 4]) for jm in range(2)]))
            self.mm_groups(groups)
            for dc in range(2):
                ov = oTs[:, :, b * 4:(b + 1) * 4].rearrange("p (h dc) t -> p dc h t", dc=2)[:, dc]
                iv = bo[:, 0:32].rearrange("p (h dc t) -> p dc h t", dc=2, t=4)[:, dc]
                rv = rsb[:, :].rearrange("p (h t) -> p h t", t=4)
                self.tt(ov, iv, rv, MUL)

        def scores(hh, N, qT):
            eT = eT2[hh % 2]
            for jm in range(2):
                bk = self.bank()
                self.mm(bk[:, 0:N], [(KpT[:, 2 * hh + dc, jm * 128:(jm + 1) * 128], qT[:, 2 * hh + dc, 0:N])
                                     for dc in range(2)])
                A(eT[:, jm, 0:N], bk[:, 0:N], AF.Exp)

        def sum_pv(hh, N):
            eT, rs = eT2[hh % 2], rs2[hh % 2]
            bs = self.bank()
            self.mm(bs[:, 0:N], [(ones[:], eT[:, jm, 0:N]) for jm in range(2)])
            A(rs[:, 0:N], bs[:, 0:N], AF.Ln)
            A(rs[:, 0:N], rs[:, 0:N], AF.Exp, scale=-1.0)
            for dc in range(2):
                bo = self.bank()
                self.mm(bo[:, 0:N], [(Vp[:, jm, hh * 256 + dc * 128:hh * 256 + (dc + 1) * 128], eT[:, jm, 0:N])
                                     for jm in range(2)])
                self.tt(oT[:, 2 * hh + dc, 0:N], bo[:, 0:N], rs[:, 0:N], MUL)

        def drive(gens):
            gens = [g for g in gens if g is not None]
            while gens:
                alive = []
                for g in gens:
                    try:
                        next(g)
                        alive.append(g)
                    except StopIteration:
                        pass
                gens = alive

        ptiles = list(range(0, NPR, NB))

        def backB(ti):
            qT = qT2[ti % 2]
            scores(0, NB, qT)
            for hh in range(4):
                if hh + 1 < 4:
                    scores(hh + 1, NB, qT)
                sum_pv(hh, NB)
                sample_batch(ti * 4 + hh)
                yield
            oproj(ptiles[ti], NB, oT)
            yield

        kv_load(0)
        drive([qproj(NPR, NSM, qTs)])
        drive([qproj(ptiles[0], NB, qT2[0])])
        for ti in range(len(ptiles)):
            nxt = qproj(ptiles[ti + 1], NB, qT2[(ti + 1) % 2]) if ti + 1 < len(ptiles) else None
            drive([nxt, backB(ti)])
        oproj(NPR, NSM, oTs)
        for c in range(2):
            self.wrel(bidx[(l, 'q', c)])
        for c in range(2):
            self.wrel(bidx[(l, 'o', c)])

        self.ck('B%d' % l)
        R.reset()
        NC_ = 512
        o_xnf = R.ptr
        xnf = R.alloc("c_xn", [128, KC, T], BF16)
        hb = R.alloc("c_h", [128, 8, T], BF16)
        sq = R.alloc("c_sq", [128, KC, NC_], BF16)
        rstd = R.alloc("c_rstd", [128, NC_], F32)
        sgt = [R.alloc("c_sg%d" % i, [128, NC_], F32) for i in range(2)]
        tiles = [(c0, min(NC_, T - c0)) for c0 in range(0, T, NC_)]
        def cnorm(ti):
            c0, N = tiles[ti]
            self.norm(lambda kc: X[:, kc, c0:c0 + N], lambda kc: pvc(PV_GFFN + l * 8 + kc),
                      lambda kc: xnf[:, kc, c0:c0 + N], N, sq, rstd)
        cnorm(0)
        cnorm(1)
        normed = 2
        cnt = 0
        for g in range(3):
            nch = 8 if g < 2 else 6
            for hbk in range(2):
                k0 = hbk * 4
                k1 = min(nch, k0 + 4)
                if k1 <= k0:
                    continue
                wg = self.wget(bidx[(l, 'gate', g, hbk)])
                wu = self.wget(bidx[(l, 'up', g, hbk)])
                for fcl in range(k0, k1):
                    cc = (fcl - k0) * 128
                    for ti_, (c0, N) in enumerate(tiles):
                        if normed < len(tiles) and ti_ + 2 >= normed:
                            cnorm(normed)
                            normed += 1
                        bg = self.bank()
                        self.mm(bg[:, 0:N], [(wg[:, kc, cc:cc + 128], xnf[:, kc, c0:c0 + N]) for kc in range(KC)])
                        bu = self.bank()
                        self.mm(bu[:, 0:N], [(wu[:, kc, cc:cc + 128], xnf[:, kc, c0:c0 + N]) for kc in range(KC)])
                        st = sgt[cnt % 2]
                        cnt += 1
                        A(st[:, 0:N], bg[:, 0:N], AF.Silu)
                        self.tt(hb[:, fcl, c0:c0 + N], bu[:, 0:N], st[:, 0:N], MUL)
                self.wrel(bidx[(l, 'gate', g, hbk)])
                self.wrel(bidx[(l, 'up', g, hbk)])
            wd = [self.wget(bidx[(l, 'down', g, hbk)]) for hbk in range(2)]
            if g == 2 and l + 1 < L:
                o_keep = R.ptr
                R.ptr = 0
                cfD_n = R.alloc("cfD_n", [128, 4 * 31, 128], BF16)
                R.ptr = o_keep
                assert o_xnf == 0
                self.build_diag(l + 1, cfD_n, None, identf, pvc, do_cf=True, do_rg=False)
                self.cf_prebuilt.add(l + 1)
                self.cf_handle[l + 1] = cfD_n
            if l == L - 1 and g == 2:
                o_keep = R.ptr
                R.ptr = o_xnf
                ys = [R.alloc("f_y%d" % i, [128, KC, NC_], F32) for i in range(2)]
                R.ptr = o_keep
                yv = env['yv']
                for ti, (c0, N) in enumerate(tiles):
                    for oc in range(KC):
                        bk = self.bank()
                        self.mm(bk[:, 0:N], [(wd[kcl // 4][:, kcl % 4, oc * 128:(oc + 1) * 128],
                                              hb[:, kcl, c0:c0 + N]) for kcl in range(nch)])
                        self.tt(X[:, oc, c0:c0 + N], X[:, oc, c0:c0 + N], bk[:, 0:N], ADD)
                    y = ys[ti % 2]
                    self.norm(lambda kc: X[:, kc, c0:c0 + N], lambda kc: pvc(PV_GFIN + kc),
                              lambda kc: y[:, kc, 0:N], N, sq, rstd)
                    trk.dma('sp', yv[:, :, c0:c0 + N], y[:, :, 0:N], 'st_y%d' % (ti % 2))
                self.final_done = True
            else:
              for oc in range(KC):
                for (c0, N) in tiles:
                    bk = self.bank()
                    self.mm(bk[:, 0:N], [(wd[kcl // 4][:, kcl % 4, oc * 128:(oc + 1) * 128], hb[:, kcl, c0:c0 + N])
                                         for kcl in range(nch)])
                    self.tt(X[:, oc, c0:c0 + N], X[:, oc, c0:c0 + N], bk[:, 0:N], ADD)
            for hbk in range(2):
                self.wrel(bidx[(l, 'down', g, hbk)])

        self.ck('C%d' % l)

    def final_norm(self, env):
        trk = self.trk
        R = env['R']; X = env['X']; pv = env['pv']; yv = env['yv']
        R.reset()
        NF = 512
        sqs = [R.alloc("f_sq%d" % i, [128, KC, NF], BF16) for i in range(2)]
        rstds = [R.alloc("f_rstd%d" % i, [128, NF], F32) for i in range(2)]
        ys = [R.alloc("f_y%d" % i, [128, KC, NF], F32) for i in range(2)]
        for ti, c0 in enumerate(range(0, T, NF)):
            N = min(NF, T - c0)
            y = ys[ti % 2]
            self.norm(lambda kc: X[:, kc, c0:c0 + N], lambda kc: pv[:, PV_GFIN + kc:PV_GFIN + kc + 1],
                      lambda kc: y[:, kc, 0:N], N, sqs[ti % 2], rstds[ti % 2])
            trk.dma('sp', yv[:, :, c0:c0 + N], y[:, :, 0:N], 'st_y%d' % (ti % 2))


_CACHE = {}


def _get_nc():
    if 'nc' not in _CACHE:
        _CACHE['nc'] = Builder().build()
    return _CACHE['nc']


def _pack_pv(inp):
    pv = np.zeros((128, PV_N), np.float32)

    def feat(v):
        return np.ascontiguousarray(v.reshape(L, KC, 128).transpose(2, 0, 1)).reshape(128, L * KC)

    def chan(v):
        return np.ascontiguousarray(v.reshape(L, 4, 128).transpose(2, 0, 1)).reshape(128, L * 4)

    pv[:, PV_GMIX:PV_GMIX + 16] = feat(inp['norm_mix_g'])
    pv[:, PV_GATT:PV_GATT + 16] = feat(inp['norm_attn_g'])
    pv[:, PV_GFFN:PV_GFFN + 16] = feat(inp['norm_ffn_g'])
    pv[:, PV_GMEM:PV_GMEM + 16] = feat(inp['norm_mem_g'])
    pv[:, PV_GFIN:PV_GFIN + 8] = inp['norm_final_g'].reshape(KC, 128).T
    rgw = inp['rg_conv_w'].reshape(L, 4, 4, 128).transpose(3, 0, 2, 1)
    pv[:, PV_RGW:PV_RGW + 32] = np.ascontiguousarray(rgw).reshape(128, 32)
    pv[:, PV_RGB:PV_RGB + 8] = chan(inp['rg_conv_b'])
    pv[:, PV_BA:PV_BA + 8] = chan(inp['rg_ba'])
    pv[:, PV_BX:PV_BX + 8] = chan(inp['rg_bx'])
    pv[:, PV_LAM:PV_LAM + 8] = chan(inp['rg_lambda'])
    cfw = inp['cf_conv_w'].reshape(L, 31, 4, 128).transpose(3, 0, 2, 1)
    pv[:, PV_CFW:PV_CFW + 248] = np.ascontiguousarray(cfw).reshape(128, 248)
    pv[:, PV_CFB:PV_CFB + 8] = chan(inp['cf_conv_b'])
    pv[:, PV_LNG:PV_LNG + 8] = chan(inp['cf_ln_g'])
    pv[:, PV_LNB:PV_LNB + 8] = chan(inp['cf_ln_b'])
    return pv


def _prep(inp):
    inp = {k: np.asarray(v) for k, v in inp.items()}
    f32 = np.float32
    pv = _pack_pv(inp)
    shared = {
        'pv': pv,
        'rg_wa': np.ascontiguousarray(inp['rg_wa'], f32), 'rg_wx': np.ascontiguousarray(inp['rg_wx'], f32),
        'w_in': np.ascontiguousarray(inp['w_in'], f32), 'w_out': np.ascontiguousarray(inp['w_out'], f32),
        'w_q': np.ascontiguousarray(inp['w_q'], f32), 'w_k': np.ascontiguousarray(inp['w_k'], f32),
        'w_v': np.ascontiguousarray(inp['w_v'], f32), 'w_o': np.ascontiguousarray(inp['w_o'], f32),
        'w_gate': np.ascontiguousarray(inp['w_gate'], f32), 'w_up': np.ascontiguousarray(inp['w_up'], f32),
        'w_down': np.ascontiguousarray(inp['w_down'], f32),
    }
    in_maps = []
    for c in range(NCORES):
        sl = slice(16 * c, 16 * (c + 1))
        xs = inp['x_sample'][sl].reshape(NSM, D)
        xT = np.ascontiguousarray(np.concatenate([inp['x_prompt'][c].T, xs.T], axis=1), f32)
        memT = np.ascontiguousarray(inp['mem_prompt'][c].T, f32)
        KcT = np.ascontiguousarray(inp['cache_mem_k'][:, sl].reshape(L, 16, 256, D).transpose(0, 1, 3, 2), f32)
        Vc = np.ascontiguousarray(inp['cache_mem_v'][:, sl].reshape(L, 16, 256, D), f32)
        h0 = np.ascontiguousarray(inp['state_rglru_h'][:, sl].reshape(L, 16, 4, 128).transpose(0, 3, 2, 1), f32)
        rgs = np.ascontiguousarray(
            inp['state_rglru_conv'][:, sl].reshape(L, 16, 3, 4, 128).transpose(0, 4, 3, 1, 2), f32)
        cfs = np.ascontiguousarray(
            inp['state_conf_conv'][:, sl].reshape(L, 16, 30, 4, 128).transpose(0, 4, 3, 1, 2), f32)
        m = dict(shared)
        m.update({'xT': xT, 'memT': memT, 'KcT': KcT, 'Vc': Vc, 'h0': h0, 'rgs': rgs, 'cfs': cfs})
        in_maps.append(m)
    return in_maps


def kernel(**inp):
    nc = _get_nc()
    in_maps = _prep(inp)
    res = run_bass_kernel_spmd(nc, in_maps, core_ids=list(range(NCORES)))
    return _post(res.results)


def _post(rs):
    f32 = np.float32
    B = NCORES
    y_prompt = np.empty((B, NPR, D), f32)
    y_sample = np.empty((128, 4, D), f32)
    p_h = np.empty((L, B, 512), f32)
    p_rg = np.empty((L, B, 3, 512), f32)
    p_cf = np.empty((L, B, 30, 512), f32)
    p_mk = np.empty((L, B, 256, 4, 256), f32)
    p_mv = np.empty((L, B, 256, 4, 256), f32)
    s_h = np.empty((L, 128, 512), f32)
    s_rg = np.empty((L, 128, 3, 512), f32)
    s_cf = np.empty((L, 128, 30, 512), f32)
    for c in range(NCORES):
        r = rs[c]
        sl = slice(16 * c, 16 * (c + 1))
        yT = r['yT']
        y_prompt[c] = yT[:, :NPR].T
        y_sample[sl] = yT[:, NPR:].T.reshape(16, 4, D)
        p_h[:, c] = r['o_ph'].transpose(0, 2, 1).reshape(L, 512)
        p_rg[:, c] = r['o_prg'].transpose(0, 3, 2, 1).reshape(L, 3, 512)
        p_cf[:, c] = r['o_pcf'].transpose(0, 3, 2, 1).reshape(L, 30, 512)
        p_mk[:, c] = r['o_pk'].transpose(0, 3, 2, 1).reshape(L, 256, 4, 256)
        p_mv[:, c] = r['o_pv'].reshape(L, 256, 4, 256)
        s_h[:, sl] = r['o_sh'].transpose(0, 3, 2, 1).reshape(L, 16, 512)
        s_rg[:, sl] = r['o_srg'].transpose(0, 3, 4, 2, 1).reshape(L, 16, 3, 512)
        scf = np.concatenate([r['o_scfh'], r['o_scfn']], axis=4)
        s_cf[:, sl] = scf.transpose(0, 3, 4, 2, 1).reshape(L, 16, 30, 512)
    return (y_prompt, y_sample, p_h, p_rg, p_cf, p_mk, p_mv, s_h, s_rg, s_cf)
```

```python
import itertools
import numpy as np
import concourse.bass as bass
import concourse.mybir as mybir
from concourse.bass_utils import run_bass_kernel_spmd

F32 = mybir.dt.float32
BF16 = mybir.dt.bfloat16
U8 = mybir.dt.uint8
AF = mybir.ActivationFunctionType
ALU = mybir.AluOpType

NCORES = 8
L = 2
D = 1024
KC = 8
NPR = 2048
NSM = 64
T = NPR + NSM
DFF = 2816
EPS = 1e-6
GRAN = 128
PSUM_BASE = 1 << 24
RING_SLOTS = 6
SLOT_BYTES = 8192

PV_GMIX, PV_GATT, PV_GFFN, PV_GMEM, PV_GFIN = 0, 16, 32, 48, 64
PV_RGW, PV_RGB, PV_BA, PV_BX, PV_LAM = 72, 104, 112, 120, 128
PV_CFW, PV_CFB, PV_LNG, PV_LNB = 136, 384, 392, 400
PV_N = 408


def _esz(dt):
    return 4 if dt == F32 else (2 if dt == BF16 else 1)


class Trk:
    def __init__(self, nc):
        self.nc = nc
        self.E = {'pe': nc.tensor, 'act': nc.scalar, 'dve': nc.vector, 'pool': nc.gpsimd, 'sp': nc.sync}
        self.sems = {}
        self.val = {}
        self.waited = {e: {} for e in self.E}
        self.lastw = {}
        self.rd = {}
        self.base = {}

    def reg(self, handle, addr):
        self.base[handle.name] = addr

    def sem(self, key):
        if key not in self.sems:
            self.sems[key] = self.nc.alloc_semaphore('s_' + key)
            self.val[key] = 0
        return self.sems[key]

    def grans(self, aps):
        out = set()
        for ap in aps:
            nm = ap.tensor.name
            if nm not in self.base:
                continue
            base = self.base[nm]
            es = _esz(ap.dtype)
            pat = ap.ap
            rowstride = pat[0][0]
            off = int(ap.offset)
            if rowstride > 0:
                off = off % rowstride
            free = pat[1:]
            if len(free) == 0:
                rngs = [(off, off + 1)]
            else:
                ist, icnt = free[-1]
                ilen = (icnt - 1) * abs(ist) + 1
                outer = free[:-1]
                rngs = []
                for idx in itertools.product(*[range(c) for (_, c) in outer]):
                    st = off + sum(i * s for i, (s, _) in zip(idx, outer))
                    rngs.append((st, st + ilen))
            for lo, hi in rngs:
                blo = base + lo * es
                bhi = base + hi * es
                for g in range(blo // GRAN, (bhi - 1) // GRAN + 1):
                    out.add(g)
        return out

    def _deps(self, rg, wg):
        need = {}
        for g in rg:
            w = self.lastw.get(g)
            if w is not None and need.get(w[0], 0) < w[1]:
                need[w[0]] = w[1]
        for g in wg:
            w = self.lastw.get(g)
            if w is not None and need.get(w[0], 0) < w[1]:
                need[w[0]] = w[1]
            r = self.rd.get(g)
            if r:
                for k, v in r.items():
                    if need.get(k, 0) < v:
                        need[k] = v
        return need

    def _commit(self, rg, wg, key, val):
        for g in rg:
            d = self.rd.get(g)
            if d is None:
                self.rd[g] = {key: val}
            else:
                d[key] = val
        for g in wg:
            self.lastw[g] = (key, val)
            self.rd[g] = None

    def _wait(self, e, need):
        eng = self.E[e]
        wd = self.waited[e]
        for k, v in need.items():
            if k not in self.E:
                v = self.val[k]
            if wd.get(k, 0) < v:
                eng.wait_ge(self.sems[k], v)
                wd[k] = v

    def op(self, e, reads, writes, fn):
        rg = self.grans(reads)
        wg = self.grans(writes)
        self._wait(e, self._deps(rg, wg))
        ins = fn()
        s = self.sem(e)
        self.val[e] += 1
        ins.then_inc(s, 1)
        self._commit(rg, wg, e, self.val[e])

    def dma(self, q, out, in_, key):
        reads = [in_] if in_.tensor.name in self.base else []
        writes = [out] if out.tensor.name in self.base else []
        rg = self.grans(reads)
        wg = self.grans(writes)
        self._wait(q, self._deps(rg, wg))
        s = self.sem(key)
        ins = self.E[q].dma_start(out=out, in_=in_)
        self.val[key] += 16
        ins.then_inc(s, 16)
        self._commit(rg, wg, key, self.val[key])


class Region:
    def __init__(self, nc, trk, base, size):
        self.nc, self.trk, self.base, self.size = nc, trk, base, size
        self.ptr = 0

    def reset(self):
        self.ptr = 0

    def alloc(self, name, shape, dt):
        n = 1
        for s in shape[1:]:
            n *= s
        nbytes = n * _esz(dt)
        nbytes = (nbytes + GRAN - 1) // GRAN * GRAN
        assert self.ptr + nbytes <= self.size, (name, self.ptr, nbytes, self.size)
        addr = self.base + self.ptr
        t = self.nc.alloc_sbuf_tensor_at(name, list(shape), dt, offset=addr)
        self.trk.reg(t, addr)
        self.ptr += nbytes
        return t


class _Stop(Exception):
    pass


_STOP = None


class Builder:
    def ck(self, name):
        if _STOP == name:
            raise _Stop()

    def __init__(self):
        self.nc = bass.Bass("TRN2", target_bir_lowering=False)
        self.trk = Trk(self.nc)
        self.pb = 0
        self.cf_prebuilt = set()
        self.cf_handle = {}

    def bank(self):
        self.pb = (self.pb + 1) % 8
        return self.ps[self.pb]

    def mm(self, out, pairs, extra_reads=()):
        reads = []
        for a, b in pairs:
            reads.append(a)
            reads.append(b)
        n = len(pairs)

        def fn():
            ins = None
            for i, (a, b) in enumerate(pairs):
                ins = self.nc.tensor.matmul(out, lhsT=a, rhs=b, start=(i == 0), stop=(i == n - 1))
            return ins
        self.trk.op('pe', reads, [out], fn)

    def mm_groups(self, groups):
        reads, writes = [], []
        for out, pairs in groups:
            writes.append(out)
            for a, b in pairs:
                reads.append(a)
                reads.append(b)

        def fn():
            ins = None
            for out, pairs in groups:
                n = len(pairs)
                for i, (a, b) in enumerate(pairs):
                    ins = self.nc.tensor.matmul(out, lhsT=a, rhs=b, start=(i == 0), stop=(i == n - 1))
            return ins
        self.trk.op('pe', reads, writes, fn)

    def act(self, out, in_, func, bias=None, scale=None):
        reads = [in_]
        kw = {}
        if bias is not None:
            kw['bias'] = bias
            if not isinstance(bias, (int, float)):
                reads.append(bias)
        if scale is not None:
            kw['scale'] = scale
            if not isinstance(scale, (int, float)):
                reads.append(scale)
        self.trk.op('act', reads, [out], lambda: self.nc.scalar.activation(out=out, in_=in_, func=func, **kw))

    def tt(self, out, in0, in1, op, eng='dve'):
        e = self.trk.E[eng]
        self.trk.op(eng, [in0, in1], [out], lambda: e.tensor_tensor(out=out, in0=in0, in1=in1, op=op))

    def ts(self, out, in0, s1, s2, op0, op1=None, eng='dve'):
        e = self.trk.E[eng]
        reads = [in0]
        if not isinstance(s1, (int, float)):
            reads.append(s1)
        if s2 is not None and not isinstance(s2, (int, float)):
            reads.append(s2)
        if op1 is None:
            self.trk.op(eng, reads, [out], lambda: e.tensor_scalar(out=out, in0=in0, scalar1=s1, scalar2=None, op0=op0))
        else:
            self.trk.op(eng, reads, [out], lambda: e.tensor_scalar(out=out, in0=in0, scalar1=s1, scalar2=s2, op0=op0, op1=op1))

    def stt(self, out, in0, scalar, in1, op0, op1):
        reads = [in0, in1]
        if not isinstance(scalar, (int, float)):
            reads.append(scalar)
        self.trk.op('dve', reads, [out], lambda: self.nc.vector.scalar_tensor_tensor(
            out=out, in0=in0, scalar=scalar, in1=in1, op0=op0, op1=op1))

    def scan(self, out, d0, d1, initial):
        reads = [d0, d1]
        if not isinstance(initial, (int, float)):
            reads.append(initial)
        self.trk.op('dve', reads, [out], lambda: self.nc.vector.tensor_tensor_scan(
            out=out, data0=d0, data1=d1, initial=initial, op0=ALU.mult, op1=ALU.add))

    def recip(self, out, in_):
        self.trk.op('dve', [in_], [out], lambda: self.nc.vector.reciprocal(out=out, in_=in_))

    def copy(self, out, in_, eng='dve'):
        e = self.trk.E[eng]
        self.trk.op(eng, [in_], [out], lambda: e.tensor_copy(out=out, in_=in_))

    def memset(self, out, v, eng='dve'):
        e = self.trk.E[eng]
        self.trk.op(eng, [], [out], lambda: e.memset(out, v))

    def ring_plan(self, blocks):
        self.blocks = blocks
        self.blk_view = [None] * len(blocks)
        self.blk_emitted = 0
        self.blk_released = [False] * len(blocks)

    def _ring_emit(self, i):
        ap = self.blocks[i]
        a, b = ap.shape[1], ap.shape[2]
        slot = i % RING_SLOTS
        v = self.ring[slot][:, 0:a * b].rearrange("p (a b) -> p a b", a=a)
        self.trk.dma('pool', v, ap, 'ring%d' % slot)
        self.blk_view[i] = v

    def _ring_pump(self, upto=None):
        while self.blk_emitted < len(self.blocks):
            j = self.blk_emitted
            if upto is not None and j <= upto:
                pass
            elif j >= RING_SLOTS and not self.blk_released[j - RING_SLOTS]:
                break
            if j >= RING_SLOTS:
                assert self.blk_released[j - RING_SLOTS], "ring overflow: block %d" % j
            self._ring_emit(j)
            self.blk_emitted += 1

    def wget(self, i):
        self._ring_pump(upto=i)
        return self.blk_view[i]

    def wrel(self, i):
        self.blk_released[i] = True
        self._ring_pump()

    def norm(self, xin, gvec, out_fn, N, sq, rstd):
        for kc in range(KC):
            self.act(sq[:, kc, 0:N], xin(kc), AF.Square)
        bk = self.bank()
        self.mm(bk[:, 0:N], [(self.c1024[:], sq[:, kc, 0:N]) for kc in range(KC)])
        self.act(rstd[:, 0:N], bk[:, 0:N], AF.Ln, bias=EPS)
        self.act(rstd[:, 0:N], rstd[:, 0:N], AF.Exp, scale=-0.5)
        for kc in range(KC):
            self.stt(out_fn(kc), xin(kc), gvec(kc), rstd[:, 0:N], ALU.mult, ALU.mult)

    def build(self):
        nc = self.nc
        trk = self.trk
        dr = {}

        def din(name, shape):
            dr[name] = nc.dram_tensor(name, list(shape), F32, kind="ExternalInput").ap()
            return dr[name]

        def dout(name, shape):
            dr[name] = nc.dram_tensor(name, list(shape), F32, kind="ExternalOutput").ap()
            return dr[name]

        xT = din("xT", [D, T])
        memT = din("memT", [D, 256])
        KcT = din("KcT", [L, 16, D, 256])
        Vc = din("Vc", [L, 16, 256, D])
        h0 = din("h0", [L, 128, 4, 16])
        rgs = din("rgs", [L, 128, 4, 16, 3])
        cfs = din("cfs", [L, 128, 4, 16, 30])
        pvd = din("pv", [128, PV_N])
        rg_wa = din("rg_wa", [L, 8, 64, 64])
        rg_wx = din("rg_wx", [L, 8, 64, 64])
        w_in = din("w_in", [L, D, 2048])
        w_out = din("w_out", [L, D, D])
        w_q = din("w_q", [L, D, D])
        w_k = din("w_k", [L, D, D])
        w_v = din("w_v", [L, D, D])
        w_o = din("w_o", [L, D, D])
        w_gate = din("w_gate", [L, D, DFF])
        w_up = din("w_up", [L, D, DFF])
        w_down = din("w_down", [L, DFF, D])

        yT = dout("yT", [D, T])
        o_ph = dout("o_ph", [L, 128, 4])
        o_prg = dout("o_prg", [L, 128, 4, 3])
        o_pcf = dout("o_pcf", [L, 128, 4, 30])
        o_pk = dout("o_pk", [L, 128, 8, 256])
        o_pv = dout("o_pv", [L, 256, D])
        o_sh = dout("o_sh", [L, 128, 4, 16])
        o_srg = dout("o_srg", [L, 128, 4, 16, 3])
        o_scfh = dout("o_scfh", [L, 128, 4, 16, 26])
        o_scfn = dout("o_scfn", [L, 128, 4, 16, 4])

        ARENA = 212800
        arena = nc.alloc_sbuf_tensor("arena", [128, ARENA], U8)
        abase = nc.lookup_mloc(arena).addr
        P = Region(nc, trk, abase, ARENA)
        X = P.alloc("X", [128, KC, T], F32)
        identf = P.alloc("identf", [128, 128], F32)
        ident = P.alloc("ident", [128, 128], BF16)
        self.c1024 = P.alloc("c1024", [128, 128], BF16)
        c512 = P.alloc("c512", [128, 128], BF16)
        ones = P.alloc("ones", [128, 128], BF16)
        pv = P.alloc("pvs", [128, PV_N], F32)
        cl = P.alloc("cl", [128, L * 4], F32)
        cl2 = P.alloc("cl2", [128, L * 4], F32)
        wabd = P.alloc("wabd", [128, L * 2 * 4, 128], BF16)
        hstate = P.alloc("hstate", [128, 4], F32)
        self.ring = [P.alloc("ring%d" % i, [128, SLOT_BYTES // 2], BF16) for i in range(RING_SLOTS)]
        rbase = abase + P.ptr
        R = Region(nc, trk, rbase, ARENA - P.ptr)
        self.ps = []
        for i in range(8):
            t = nc.alloc_psum_tensor("ps%d" % i, [128, 512], F32)
            trk.reg(t, PSUM_BASE + i * 2048)
            self.ps.append(t)

        blocks = []
        bidx = {}

        def wv(w, l):
            return w[l].rearrange("(kc p) n -> p kc n", p=128)

        for l in range(L):
            for c in range(4):
                bidx[(l, 'in', c)] = len(blocks)
                blocks.append(wv(w_in, l)[:, :, c * 512:(c + 1) * 512])
            for c in range(2):
                bidx[(l, 'out', c)] = len(blocks)
                blocks.append(wv(w_out, l)[:, :, c * 512:(c + 1) * 512])
            for nm, w in (('k', w_k), ('v', w_v), ('q', w_q), ('o', w_o)):
                for c in range(2):
                    bidx[(l, nm, c)] = len(blocks)
                    blocks.append(wv(w, l)[:, :, c * 512:(c + 1) * 512])
            for g in range(3):
                ncols = 1024 if g < 2 else 768
                c0 = g * 1024
                for hb in range(2):
                    lo = c0 + hb * 512
                    hi = min(c0 + ncols, lo + 512)
                    bidx[(l, 'gate', g, hb)] = len(blocks)
                    blocks.append(wv(w_gate, l)[:, :, lo:hi])
                    bidx[(l, 'up', g, hb)] = len(blocks)
                    blocks.append(wv(w_up, l)[:, :, lo:hi])
                nch = ncols // 128
                for hb in range(2):
                    k0 = hb * 4
                    k1 = min(nch, k0 + 4)
                    bidx[(l, 'down', g, hb)] = len(blocks)
                    blocks.append(w_down[l, c0 + k0 * 128:c0 + k1 * 128, :].rearrange("(kc p) n -> p kc n", p=128))
        self.ring_plan(blocks)

        A = self.act
        xv = xT.rearrange("(kc p) n -> p kc n", p=128)
        yv = yT.rearrange("(kc p) n -> p kc n", p=128)

        def pvc(off, n=1):
            return pv[:, off:off + n]

        with nc.Block():
            trk.dma('sp', pv[:], pvd[:, :], 'ld_pv')
            xtiles = [(ti, c0, min(512, T - c0)) for ti, c0 in enumerate(range(0, T, 512))]
            ti, c0, n = xtiles[0]
            trk.dma('sp', X[:, :, c0:c0 + n], xv[:, :, c0:c0 + n], 'ld_x%d' % ti)
            self.memset(identf[:], 0.0, eng='dve')
            trk.op('pool', [identf[:]], [identf[:]], lambda: nc.gpsimd.affine_select(
                out=identf[:], in_=identf[:], pattern=[[-1, 128]], compare_op=ALU.not_equal, fill=1.0,
                base=0, channel_multiplier=1))
            self.copy(ident[:], identf[:])
            self.memset(self.c1024[:], 1.0 / 1024.0)
            self.memset(c512[:], 1.0 / 512.0)
            self.memset(ones[:], 1.0)
            R.reset()
            R.ptr = 71936
            wst = R.alloc("wst", [128, L * 2 * 4, 128], F32)
            self.memset(wst[:], 0.0)
            for l in range(L):
                for gi, wsrc in enumerate((rg_wa, rg_wx)):
                    for hh in range(8):
                        j, e = hh // 2, hh % 2
                        trk.dma('sp', wst[e * 64:(e + 1) * 64, (l * 2 + gi) * 4 + j, e * 64:(e + 1) * 64],
                                wsrc[l, hh, :, :], 'ld_w%d' % ((l * 2 + gi) % 2))
            self.deferred = lambda: self.copy(wabd[:], wst[:])
            for ti, c0, n in xtiles[1:]:
                trk.dma('sp', X[:, :, c0:c0 + n], xv[:, :, c0:c0 + n], 'ld_x%d' % ti)
            for l in range(L):
                for j in range(4):
                    trk.dma('sp', o_scfh[l, :, j], cfs[l, :, j, :, 4:30], 'st_h')
            spt = R.alloc("spt", [128, L * 4], F32)
            A(spt[:], pvc(PV_LAM, 8), AF.Exp, scale=-1.0)
            A(spt[:], spt[:], AF.Ln, bias=1.0)
            self.ts(cl[:], spt[:], -8.0, None, ALU.mult)
            self.ts(cl2[:], spt[:], -16.0, None, ALU.mult)

            try:
                self.ck('P')
                for l in range(L):
                    self.layer(l, locals())
                if not getattr(self, 'final_done', False):
                    self.final_norm(locals())
            except _Stop:
                pass
            for k, v in trk.val.items():
                if k not in trk.E and v > 0:
                    nc.sync.wait_ge(trk.sems[k], v)
        return nc

    def build_diag(self, l, cfD, rgD, identf, pvc, do_cf=True, do_rg=True):
        MUL = ALU.mult
        for j in range(4):
            if do_rg:
                for k in range(4):
                    self.ts(rgD[:, j * 4 + k, :], identf[:], pvc(PV_RGW + (l * 4 + j) * 4 + k), None, MUL)
            if do_cf:
                for k in range(31):
                    if k % 3 == 2:
                        self.act(cfD[:, j * 31 + k, :], identf[:], AF.Copy,
                                 scale=pvc(PV_CFW + (l * 4 + j) * 31 + k))
                    else:
                        self.ts(cfD[:, j * 31 + k, :], identf[:], pvc(PV_CFW + (l * 4 + j) * 31 + k), None, MUL)

    def layer(self, l, env):
        nc, trk = self.nc, self.trk
        R = env['R']; X = env['X']; pv = env['pv']; bidx = env['bidx']
        ident, identf, c512, ones = env['ident'], env['identf'], env['c512'], env['ones']
        cl, cl2, wabd, hstate = env['cl'], env['cl2'], env['wabd'], env['hstate']
        dr_ = env['dr']
        A = self.act
        MUL, ADD, SUB = ALU.mult, ALU.add, ALU.subtract

        def pvc(off, n=1):
            return pv[:, off:off + n]

        R.reset()
        cfD = R.alloc("cfD", [128, 4 * 31, 128], BF16)
        if l in self.cf_handle:
            cfD = self.cf_handle[l]
        rgD = R.alloc("rgD", [128, 4 * 4, 128], BF16)
        NA = 256
        ycat = R.alloc("a_ycat", [128, KC, NA], BF16)
        sq = R.alloc("a_sq", [128, KC, NA], BF16)
        rstd = R.alloc("a_rstd", [128, NA], F32)
        xn = R.alloc("a_xn", [128, KC, NA], BF16)
        o_x = R.ptr
        xrb0 = R.alloc("a_xrb0", [128, 4, 3 + NA], BF16)
        cb0 = R.alloc("a_cb0", [128, 4, 30 + NA], BF16)
        o_end = R.ptr
        R.ptr = o_x
        csb = R.alloc("a_csb", [128, 4, 16, 34], BF16)
        assert R.ptr <= o_end
        R.ptr = o_end
        xrb1 = R.alloc("a_xrb1", [128, 4, 3 + NA], BF16)
        cb1 = R.alloc("a_cb1", [128, 4, 30 + NA], BF16)
        xrb2 = [xrb0, xrb1]
        cb2 = [cb0, cb1]
        gg2 = [R.alloc("a_gg%d" % i, [128, 4, NA], BF16) for i in range(2)]
        sg = R.alloc("a_sg", [128, NA], F32)
        xc2 = [R.alloc("a_xc%d" % i, [128, NA], F32) for i in range(2)]
        xcb = R.alloc("a_xcb", [128, NA], BF16)
        r_ = R.alloc("a_r", [128, NA], F32)
        i_ = R.alloc("a_i", [128, NA], F32)
        a_ = R.alloc("a_a", [128, NA], F32)
        m_ = R.alloc("a_m", [128, NA], F32)
        b_ = R.alloc("a_b", [128, NA], F32)
        h_ = R.alloc("a_h", [128, NA], F32)
        ccf = R.alloc("a_ccf", [128, 4, NA], F32)
        ccb = R.alloc("a_ccb", [128, 4, NA], BF16)
        sqb = R.alloc("a_sqb", [128, 4, NA], BF16)
        mean = R.alloc("a_mean", [128, NA], F32)
        var = R.alloc("a_var", [128, NA], F32)
        rstc = R.alloc("a_rstc", [128, NA], F32)
        o_dd = R.ptr
        dd = R.alloc("a_dd", [128, NA], F32)
        dd2 = [dd, sg]
        prg = R.alloc("a_prg", [128, 4, 3], F32)
        pcf = R.alloc("a_pcf", [128, 4, 30], F32)
        xrs = R.alloc("a_xrs", [128, 4, 16, 7], BF16)
        o_keep = R.ptr
        R.ptr = o_dd
        xrsf = R.alloc("a_xrsf", [128, 4, 16, 3], F32)
        R.ptr = o_keep
        srg = R.alloc("a_srg", [128, 4, 16, 3], F32)
        cs4 = R.alloc("a_cs4", [128, 4, 16, 4], F32)
        h0s = R.alloc("a_h0s", [128, 4, 16], F32)
        shl = R.alloc("a_shl", [128, 4, 16], F32)
        tmp16 = R.alloc("a_tmp16", [128, 16], F32)
        nb2 = R.alloc("a_nb2", [128, 8], F32)

        self.build_diag(l, cfD, rgD, identf, pvc, do_cf=(l not in self.cf_prebuilt))
        if getattr(self, 'deferred', None) is not None:
            self.deferred()
            self.deferred = None
        self.ts(nb2[:, 0:4], pvc(PV_BA + l * 4, 4), -1.0, None, MUL)
        self.ts(nb2[:, 4:8], pvc(PV_BX + l * 4, 4), -1.0, None, MUL)
        self.memset(xrb0[:, :, 0:3], 0.0)
        self.memset(cb0[:, :, 0:30], 0.0)
        self.memset(hstate[:], 0.0)

        win = [self.wget(bidx[(l, 'in', c)]) for c in range(4)]
        wout = [self.wget(bidx[(l, 'out', c)]) for c in range(2)]
        tiles = [(c0, NA, False) for c0 in range(0, NPR, NA)] + [(NPR, NSM, True)]
        NT = len(tiles)

        def load_sample_state():
            trk.dma('sp', xrsf[:], dr_['rgs'][l], 'ld_st0')
            trk.dma('sp', h0s[:], dr_['h0'][l], 'ld_st1')
            for j in range(4):
                trk.dma('pool', csb[:, j, :, 0:30], dr_['cfs'][l, :, j], 'ld_st%d' % (2 + j))
            self.copy(xrs[:, :, :, 0:3], xrsf[:])

        def v3(ap):
            return ap.rearrange("p (b t) -> p b t", t=4)

        def front(ti):
            c0, N, smp = tiles[ti]
            sset = ti % 2
            xrb, cb, gg = xrb2[sset], cb2[sset], gg2[sset]
            last_p = (not smp) and (c0 + N == NPR)
            if smp:
                load_sample_state()
            elif ti >= 1:
                self.copy(xrb[:, :, 0:3], xrb2[1 - sset][:, :, NA:NA + 3])
                self.copy(cb[:, :, 0:30], cb2[1 - sset][:, :, NA:NA + 30])
            self.norm(lambda kc: X[:, kc, c0:c0 + N], lambda kc: pvc(PV_GMIX + l * 8 + kc),
                      lambda kc: xn[:, kc, 0:N], N, sq, rstd)
            yield

            def proj(blk, j):
                bk = self.bank()
                self.mm(bk[:, 0:N], [(win[blk][:, kc, j * 128:(j + 1) * 128], xn[:, kc, 0:N]) for kc in range(KC)])
                return bk
            for j in range(4):
                bk = proj(0, j)
                if smp:
                    A(xrs[:, j, :, 3:7], v3(bk[:, 0:N]), AF.Copy)
                    A(srg[:, j, :, :], v3(bk[:, 0:N])[:, :, 1:4], AF.Copy)
                else:
                    A(xrb[:, j, 3:3 + N], bk[:, 0:N], AF.Copy)
                    if last_p:
                        A(prg[:, j, :], bk[:, N - 3:N], AF.Copy)
            yield
            yield
            yield
            yield
            for j in range(4):
                bk = proj(1, j)
                A(gg[:, j, 0:N], bk[:, 0:N], AF.Gelu_apprx_tanh)
            yield
            for half in range(1):
                for j in range(4):
                    bv = proj(2, j)
                    bg = proj(3, j)
                    A(sg[:, 0:N], bg[:, 0:N], AF.Sigmoid)
                    if smp:
                        self.tt(cs4[:, j, :, :], v3(bv[:, 0:N]), v3(sg[:, 0:N]), MUL)
                        self.copy(csb[:, j, :, 30:34], cs4[:, j, :, :])
                    else:
                        self.tt(cb[:, j, 30:30 + N], bv[:, 0:N], sg[:, 0:N], MUL)
                        if last_p:
                            self.tt(pcf[:, j, :], bv[:, N - 30:N], sg[:, N - 30:N], MUL)
                yield

        def back(ti):
            c0, N, smp = tiles[ti]
            sset = ti % 2
            xrb, cb, gg = xrb2[sset], cb2[sset], gg2[sset]

            def tail(j):
                xc = xc2[j % 2]
                self.tt(b_[:, 0:N], i_[:, 0:N], xc[:, 0:N], MUL)
                self.tt(b_[:, 0:N], b_[:, 0:N], m_[:, 0:N], MUL)
                if smp:
                    av = v3(a_[:, 0:N])
                    bvw = v3(b_[:, 0:N])
                    self.tt(tmp16[:], av[:, :, 0], h0s[:, j, :], MUL)
                    self.tt(bvw[:, :, 0], bvw[:, :, 0], tmp16[:], ADD)
                    self.memset(av[:, :, 0], 0.0)
                    self.scan(h_[:, 0:N], a_[:, 0:N], b_[:, 0:N], 0.0)
                    self.copy(shl[:, j, :], v3(h_[:, 0:N])[:, :, 3])
                else:
                    self.scan(h_[:, 0:N], a_[:, 0:N], b_[:, 0:N], hstate[:, j:j + 1])
                    self.copy(hstate[:, j:j + 1], h_[:, N - 1:N])
                self.tt(ycat[:, j, 0:N], h_[:, 0:N], gg[:, j, 0:N], MUL)

            for j in range(4):
                xc = xc2[j % 2]
                bk = self.bank()
                if smp:
                    self.mm(bk[:, 0:N], [(rgD[:, j * 4 + k, :], xrs[:, j, :, k:k + 4]) for k in range(4)])
                else:
                    self.mm(bk[:, 0:N], [(rgD[:, j * 4 + k, :], xrb[:, j, k:k + N]) for k in range(4)])
                bias = pvc(PV_RGB + l * 4 + j)
                self.ts(xcb[:, 0:N], bk[:, 0:N], bias, None, ADD)
                self.ts(xc[:, 0:N], bk[:, 0:N], bias, None, ADD)
                bc = self.bank()
                if smp:
                    self.mm(bc[:, 0:N], [(cfD[:, j * 31 + k, :], csb[:, j, :, k:k + 4]) for k in range(31)])
                else:
                    self.mm(bc[:, 0:N], [(cfD[:, j * 31 + k, :], cb[:, j, k:k + N]) for k in range(31)])
                ba = self.bank()
                self.mm(ba[:, 0:N], [(wabd[:, (l * 2 + 0) * 4 + j, :], xcb[:, 0:N])])
                bx = self.bank()
                self.mm(bx[:, 0:N], [(wabd[:, (l * 2 + 1) * 4 + j, :], xcb[:, 0:N])])
                cbias = pvc(PV_CFB + l * 4 + j)
                self.ts(ccf[:, j, 0:N], bc[:, 0:N], cbias, None, ADD)
                self.ts(ccb[:, j, 0:N], bc[:, 0:N], cbias, None, ADD)
                self.stt(sqb[:, j, 0:N], bc[:, 0:N], cbias, ccf[:, j, 0:N], ADD, MUL)
                if j >= 1:
                    tail(j - 1)
                A(r_[:, 0:N], ba[:, 0:N], AF.Exp, scale=-1.0, bias=nb2[:, j:j + 1])
                A(i_[:, 0:N], bx[:, 0:N], AF.Exp, scale=-1.0, bias=nb2[:, 4 + j:5 + j])
                A(r_[:, 0:N], r_[:, 0:N], AF.Ln, bias=1.0)
                A(i_[:, 0:N], i_[:, 0:N], AF.Ln, bias=1.0)
                A(r_[:, 0:N], r_[:, 0:N], AF.Exp, scale=-1.0)
                A(i_[:, 0:N], i_[:, 0:N], AF.Exp, scale=-1.0)
                A(a_[:, 0:N], r_[:, 0:N], AF.Exp, scale=cl[:, l * 4 + j:l * 4 + j + 1])
                A(m_[:, 0:N], r_[:, 0:N], AF.Exp, scale=cl2[:, l * 4 + j:l * 4 + j + 1])
                A(m_[:, 0:N], m_[:, 0:N], AF.Ln, scale=-1.0, bias=1.0000001)
                A(m_[:, 0:N], m_[:, 0:N], AF.Exp, scale=0.5)
                yield
            tail(3)
            bm = self.bank()
            self.mm(bm[:, 0:N], [(c512[:], ccb[:, j, 0:N]) for j in range(4)])
            bq = self.bank()
            self.mm(bq[:, 0:N], [(c512[:], sqb[:, j, 0:N]) for j in range(4)])
            A(mean[:, 0:N], bm[:, 0:N], AF.Copy)
            self.tt(var[:, 0:N], mean[:, 0:N], mean[:, 0:N], MUL)
            self.tt(var[:, 0:N], bq[:, 0:N], var[:, 0:N], SUB)
            A(rstc[:, 0:N], var[:, 0:N], AF.Ln, bias=EPS)
            A(rstc[:, 0:N], rstc[:, 0:N], AF.Exp, scale=-0.5)
            yield
            for j in range(4):
                dd = dd2[j % 2]
                self.tt(dd[:, 0:N], ccf[:, j, 0:N], mean[:, 0:N], SUB)
                self.tt(dd[:, 0:N], dd[:, 0:N], rstc[:, 0:N], MUL)
                A(ycat[:, 4 + j, 0:N], dd[:, 0:N], AF.Silu, bias=pvc(PV_LNB + l * 4 + j),
                  scale=pvc(PV_LNG + l * 4 + j))
            yield
            for oc in range(KC):
                bk = self.bank()
                self.mm(bk[:, 0:N], [(wout[oc // 4][:, kc, (oc % 4) * 128:(oc % 4 + 1) * 128], ycat[:, kc, 0:N])
                                     for kc in range(KC)])
                self.tt(X[:, oc, c0:c0 + N], X[:, oc, c0:c0 + N], bk[:, 0:N], ADD)
            yield

        def drive(gens):
            gens = [g for g in gens if g is not None]
            while gens:
                alive = []
                for g in gens:
                    try:
                        next(g)
                        alive.append(g)
                    except StopIteration:
                        pass
                gens = alive

        drive([front(0)])
        for ti in range(NT):
            drive([front(ti + 1) if ti + 1 < NT else None, back(ti)])
            if ti == NT - 2:
                for c in range(4):
                    self.wrel(bidx[(l, 'in', c)])

        trk.dma('sp', dr_['o_ph'][l], hstate[:], 'st_a0')
        trk.dma('sp', dr_['o_prg'][l], prg[:], 'st_a1')
        trk.dma('sp', dr_['o_pcf'][l], pcf[:], 'st_a2')
        trk.dma('sp', dr_['o_sh'][l], shl[:], 'st_a3')
        trk.dma('sp', dr_['o_srg'][l], srg[:], 'st_a4')
        trk.dma('sp', dr_['o_scfn'][l], cs4[:], 'st_a5')
        for c in range(2):
            self.wrel(bidx[(l, 'out', c)])

        self.ck('A%d' % l)
        R.reset()
        NB = 512
        KpT = R.alloc("b_KpT", [128, KC, 256], BF16)
        Vp = R.alloc("b_Vp", [128, 2, D], BF16)
        Ks = [R.alloc("b_Ks%d" % i, [128, KC, 256], BF16) for i in range(2)]
        Vs = [R.alloc("b_Vs%d" % i, [128, 2, D], BF16) for i in range(2)]
        sq = R.alloc("b_sq", [128, KC, NB], BF16)
        rstd = R.alloc("b_rstd", [128, NB], F32)
        xn = R.alloc("b_xn", [128, KC, NB], BF16)
        sq_s = R.alloc("b_sqs", [128, KC, NSM], BF16)
        rstd_s = R.alloc("b_rstds", [128, NSM], F32)
        xn_s = R.alloc("b_xns", [128, KC, NSM], BF16)
        mark = R.ptr
        memf = R.alloc("b_memf", [128, KC, 256], F32)
        memn = R.alloc("b_memn", [128, KC, 256], BF16)
        msq = R.alloc("b_msq", [128, KC, 256], BF16)
        mrs = R.alloc("b_mrs", [128, 256], F32)
        kst = R.alloc("b_kst", [128, KC, 256], F32)
        vst = R.alloc("b_vst", [128, 2, D], F32)
        trk.dma('sp', memf[:], dr_['memT'].rearrange("(kc p) n -> p kc n", p=128), 'ld_mem')
        self.norm(lambda kc: memf[:, kc, :], lambda kc: pvc(PV_GMEM + l * 8 + kc),
                  lambda kc: memn[:, kc, :], 256, msq, mrs)
        self.norm(lambda kc: X[:, kc, NPR:NPR + NSM], lambda kc: pvc(PV_GATT + l * 8 + kc),
                  lambda kc: xn_s[:, kc, 0:NSM], NSM, sq_s, rstd_s)
        self.norm(lambda kc: X[:, kc, 0:NB], lambda kc: pvc(PV_GATT + l * 8 + kc),
                  lambda kc: xn[:, kc, 0:NB], NB, sq, rstd)
        self.ck('Bn%d' % l)
        wk = [self.wget(bidx[(l, 'k', c)]) for c in range(2)]
        self.ck('Bw%d' % l)
        for dc in range(KC):
            bk = self.bank()
            self.mm(bk[:, 0:256], [(wk[dc // 4][:, kc, (dc % 4) * 128:(dc % 4 + 1) * 128], memn[:, kc, :])
                                   for kc in range(KC)])
            A(KpT[:, dc, :], bk[:, 0:256], AF.Copy)
            A(kst[:, dc, :], bk[:, 0:256], AF.Copy)
        self.ck('Bk%d' % l)
        trk.dma('sp', dr_['o_pk'][l], kst[:], 'st_bk')
        for c in range(2):
            self.wrel(bidx[(l, 'k', c)])
        wvv = [self.wget(bidx[(l, 'v', c)]) for c in range(2)]
        for mt in range(2):
            for cbk in range(2):
                bk = self.bank()
                self.mm(bk[:, :], [(memn[:, kc, mt * 128:(mt + 1) * 128], wvv[cbk][:, kc, :]) for kc in range(KC)])
                A(Vp[:, mt, cbk * 512:(cbk + 1) * 512], bk[:, :], AF.Copy)
                A(vst[:, mt, cbk * 512:(cbk + 1) * 512], bk[:, :], AF.Copy)
        trk.dma('sp', dr_['o_pv'][l].rearrange("(j p) d -> p j d", p=128), vst[:], 'st_bv')
        for c in range(2):
            self.wrel(bidx[(l, 'v', c)])
        self.ck('Ba%d' % l)
        R.ptr = mark
        qT2 = [R.alloc("b_qT%d" % i, [128, KC, NB], BF16) for i in range(2)]
        eT2 = [R.alloc("b_eT%d" % i, [128, 2, NB], BF16) for i in range(2)]
        rs2 = [R.alloc("b_rs%d" % i, [128, NB], F32) for i in range(2)]
        oT = R.alloc("b_oT", [128, KC, NB], BF16)
        qTs = R.alloc("b_qTs", [128, KC, NSM], BF16)
        oTs = R.alloc("b_oTs", [128, KC, NSM], BF16)
        esb = [R.alloc("b_esb%d" % i, [128, 32], BF16) for i in range(2)]
        ssb = R.alloc("b_ssb", [128, 32], F32)
        rsb = R.alloc("b_rsb", [128, 16], F32)
        wq = [self.wget(bidx[(l, 'q', c)]) for c in range(2)]
        wo = [self.wget(bidx[(l, 'o', c)]) for c in range(2)]

        def qproj(c0, N, qdst, xn=xn, do_norm=True):
            if do_norm:
                self.norm(lambda kc: X[:, kc, c0:c0 + N], lambda kc: pvc(PV_GATT + l * 8 + kc),
                          lambda kc: xn[:, kc, 0:N], N, sq, rstd)
            yield
            for oc in range(KC):
                bk = self.bank()
                self.mm(bk[:, 0:N], [(wq[oc // 4][:, kc, (oc % 4) * 128:(oc % 4 + 1) * 128], xn[:, kc, 0:N])
                                     for kc in range(KC)])
                A(qdst[:, oc, 0:N], bk[:, 0:N], AF.Copy, scale=1.0 / 16.0)
                if oc == 3:
                    yield
            yield

        def oproj(c0, N, osrc):
            for oc in range(KC):
                bk = self.bank()
                self.mm(bk[:, 0:N], [(wo[oc // 4][:, kc, (oc % 4) * 128:(oc % 4 + 1) * 128], osrc[:, kc, 0:N])
                                     for kc in range(KC)])
                self.tt(X[:, oc, c0:c0 + N], X[:, oc, c0:c0 + N], bk[:, 0:N], ADD)

        def kv_load(b):
            trk.dma('pool', Ks[b % 2][:], dr_['KcT'][l, b].rearrange("(kc p) m -> p kc m", p=128), 'ld_k%d' % (b % 2))
            trk.dma('pool', Vs[b % 2][:], dr_['Vc'][l, b].rearrange("(j p) d -> p j d", p=128), 'ld_v%d' % (b % 2))

        def sample_batch(b):
            if b + 1 < 16:
                kv_load(b + 1)
            kb, vb, es = Ks[b % 2], Vs[b % 2], esb[b % 2]
            bsc = self.bank()
            groups = []
            for hh in range(4):
                for jm in range(2):
                    col = hh * 8 + jm * 4
                    groups.append((bsc[:, col:col + 4],
                                   [(kb[:, 2 * hh + dc, jm * 128:(jm + 1) * 128],
                                     qTs[:, 2 * hh + dc, b * 4:(b + 1) * 4]) for dc in range(2)]))
            self.mm_groups(groups)
            A(es[:, :], bsc[:, 0:32], AF.Exp)
            bs = self.bank()
            self.mm(bs[:, 0:32], [(ones[:], es[:, :])])
            A(ssb[:, :], bs[:, 0:32], AF.Copy)
            sv = ssb[:, :].rearrange("p (h j t) -> p h j t", j=2, t=4)
            self.tt(rsb[:, :].rearrange("p (h t) -> p h t", t=4), sv[:, :, 0, :], sv[:, :, 1, :], ADD)
            self.recip(rsb[:, :], rsb[:, :])
            bo = self.bank()
            groups = []
            for hh in range(4):
                for dc in range(2):
                    col = (hh * 2 + dc) * 4
                    groups.append((bo[:, col:col + 4],
                                   [(vb[:, jm, hh * 256 + dc * 128:hh * 256 + (dc + 1) * 128],
                                     es[:, hh * 8 + jm * 4:hh * 8 + jm * 4 + 4]) for jm in range(2)]))
            self.mm_groups(groups)
            for dc in range(2):
                ov = oTs[:, :, b * 4:(b + 1) * 4].rearrange("p (h dc) t -> p dc h t", dc=2)[:, dc]
                iv = bo[:, 0:32].rearrange("p (h dc t) -> p dc h t", dc=2, t=4)[:, dc]
                rv = rsb[:, :].rearrange("p (h t) -> p h t", t=4)
                self.tt(ov, iv, rv, MUL)

        def scores(hh, N, qT):
            eT = eT2[hh % 2]
            for jm in range(2):
                bk = self.bank()
                self.mm(bk[:, 0:N], [(KpT[:, 2 * hh + dc, jm * 128:(jm + 1) * 128], qT[:, 2 * hh + dc, 0:N])
                                     for dc in range(2)])
                A(eT[:, jm, 0:N], bk[:, 0:N], AF.Exp)

        def sum_pv(hh, N):
            eT, rs = eT2[hh % 2], rs2[hh % 2]
            bs = self.bank()
            self.mm(bs[:, 0:N], [(ones[:], eT[:, jm, 0:N]) for jm in range(2)])
            A(rs[:, 0:N], bs[:, 0:N], AF.Ln)
            A(rs[:, 0:N], rs[:, 0:N], AF.Exp, scale=-1.0)
            for dc in range(2):
                bo = self.bank()
                self.mm(bo[:, 0:N], [(Vp[:, jm, hh * 256 + dc * 128:hh * 256 + (dc + 1) * 128], eT[:, jm, 0:N])
                                     for jm in range(2)])
                self.tt(oT[:, 2 * hh + dc, 0:N], bo[:, 0:N], rs[:, 0:N], MUL)

        def drive(gens):
            gens = [g for g in gens if g is not None]
            while gens:
                alive = []
                for g in gens:
                    try:
                        next(g)
                        alive.append(g)
                    except StopIteration:
                        pass
                gens = alive

        ptiles = list(range(0, NPR, NB))

        def backB(ti):
            qT = qT2[ti % 2]
            scores(0, NB, qT)
            for hh in range(4):
                if hh + 1 < 4:
                    scores(hh + 1, NB, qT)
                sum_pv(hh, NB)
                sample_batch(ti * 4 + hh)
                yield
            oproj(ptiles[ti], NB, oT)
            yield

        kv_load(0)
        drive([qproj(NPR, NSM, qTs, xn=xn_s, do_norm=False)])
        drive([qproj(ptiles[0], NB, qT2[0], do_norm=False)])
        for ti in range(len(ptiles)):
            nxt = qproj(ptiles[ti + 1], NB, qT2[(ti + 1) % 2]) if ti + 1 < len(ptiles) else None
            drive([nxt, backB(ti)])
        oproj(NPR, NSM, oTs)
        for c in range(2):
            self.wrel(bidx[(l, 'q', c)])
        for c in range(2):
            self.wrel(bidx[(l, 'o', c)])

        self.ck('B%d' % l)
        R.reset()
        NC_ = 512
        o_xnf = R.ptr
        xnf = R.alloc("c_xn", [128, KC, T], BF16)
        hb = R.alloc("c_h", [128, 8, T], BF16)
        sq = R.alloc("c_sq", [128, KC, NC_], BF16)
        rstd = R.alloc("c_rstd", [128, NC_], F32)
        sgt = [R.alloc("c_sg%d" % i, [128, NC_], F32) for i in range(2)]
        tiles = [(c0, min(NC_, T - c0)) for c0 in range(0, T, NC_)]
        def cnorm(ti):
            c0, N = tiles[ti]
            self.norm(lambda kc: X[:, kc, c0:c0 + N], lambda kc: pvc(PV_GFFN + l * 8 + kc),
                      lambda kc: xnf[:, kc, c0:c0 + N], N, sq, rstd)
        cnorm(0)
        cnorm(1)
        normed = 2
        cnt = 0
        for g in range(3):
            nch = 8 if g < 2 else 6
            for hbk in range(2):
                k0 = hbk * 4
                k1 = min(nch, k0 + 4)
                if k1 <= k0:
                    continue
                wg = self.wget(bidx[(l, 'gate', g, hbk)])
                wu = self.wget(bidx[(l, 'up', g, hbk)])
                for fcl in range(k0, k1):
                    cc = (fcl - k0) * 128
                    for ti_, (c0, N) in enumerate(tiles):
                        if normed < len(tiles) and ti_ + 2 >= normed:
                            cnorm(normed)
                            normed += 1
                        bg = self.bank()
                        self.mm(bg[:, 0:N], [(wg[:, kc, cc:cc + 128], xnf[:, kc, c0:c0 + N]) for kc in range(KC)])
                        bu = self.bank()
                        self.mm(bu[:, 0:N], [(wu[:, kc, cc:cc + 128], xnf[:, kc, c0:c0 + N]) for kc in range(KC)])
                        st = sgt[cnt % 2]
                        cnt += 1
                        A(st[:, 0:N], bg[:, 0:N], AF.Silu)
                        self.tt(hb[:, fcl, c0:c0 + N], bu[:, 0:N], st[:, 0:N], MUL)
                self.wrel(bidx[(l, 'gate', g, hbk)])
                self.wrel(bidx[(l, 'up', g, hbk)])
            wd = [self.wget(bidx[(l, 'down', g, hbk)]) for hbk in range(2)]
            if g == 2 and l + 1 < L:
                o_keep = R.ptr
                R.ptr = 0
                cfD_n = R.alloc("cfD_n", [128, 4 * 31, 128], BF16)
                R.ptr = o_keep
                assert o_xnf == 0
                self.build_diag(l + 1, cfD_n, None, identf, pvc, do_cf=True, do_rg=False)
                self.cf_prebuilt.add(l + 1)
                self.cf_handle[l + 1] = cfD_n
            if l == L - 1 and g == 2:
                o_keep = R.ptr
                R.ptr = o_xnf
                ys = [R.alloc("f_y%d" % i, [128, KC, NC_], F32) for i in range(2)]
                R.ptr = o_keep
                yv = env['yv']
                for ti, (c0, N) in enumerate(tiles):
                    for oc in range(KC):
                        bk = self.bank()
                        self.mm(bk[:, 0:N], [(wd[kcl // 4][:, kcl % 4, oc * 128:(oc + 1) * 128],
                                              hb[:, kcl, c0:c0 + N]) for kcl in range(nch)])
                        self.tt(X[:, oc, c0:c0 + N], X[:, oc, c0:c0 + N], bk[:, 0:N], ADD)
                    y = ys[ti % 2]
                    self.norm(lambda kc: X[:, kc, c0:c0 + N], lambda kc: pvc(PV_GFIN + kc),
                              lambda kc: y[:, kc, 0:N], N, sq, rstd)
                    trk.dma('sp', yv[:, :, c0:c0 + N], y[:, :, 0:N], 'st_y%d' % (ti % 2))
                self.final_done = True
            else:
              for oc in range(KC):
                for (c0, N) in tiles:
                    bk = self.bank()
                    self.mm(bk[:, 0:N], [(wd[kcl // 4][:, kcl % 4, oc * 128:(oc + 1) * 128], hb[:, kcl, c0:c0 + N])
                                         for kcl in range(nch)])
                    self.tt(X[:, oc, c0:c0 + N], X[:, oc, c0:c0 + N], bk[:, 0:N], ADD)
            for hbk in range(2):
                self.wrel(bidx[(l, 'down', g, hbk)])

        self.ck('C%d' % l)

    def final_norm(self, env):
        trk = self.trk
        R = env['R']; X = env['X']; pv = env['pv']; yv = env['yv']
        R.reset()
        NF = 512
        sqs = [R.alloc("f_sq%d" % i, [128, KC, NF], BF16) for i in range(2)]
        rstds = [R.alloc("f_rstd%d" % i, [128, NF], F32) for i in range(2)]
        ys = [R.alloc("f_y%d" % i, [128, KC, NF], F32) for i in range(2)]
        for ti, c0 in enumerate(range(0, T, NF)):
            N = min(NF, T - c0)
            y = ys[ti % 2]
            self.norm(lambda kc: X[:, kc, c0:c0 + N], lambda kc: pv[:, PV_GFIN + kc:PV_GFIN + kc + 1],
                      lambda kc: y[:, kc, 0:N], N, sqs[ti % 2], rstds[ti % 2])
            trk.dma('sp', yv[:, :, c0:c0 + N], y[:, :, 0:N], 'st_y%d' % (ti % 2))


_CACHE = {}


def _get_nc():
    if 'nc' not in _CACHE:
        _CACHE['nc'] = Builder().build()
    return _CACHE['nc']


def _pack_pv(inp):
    pv = np.zeros((128, PV_N), np.float32)

    def feat(v):
        return np.ascontiguousarray(v.reshape(L, KC, 128).transpose(2, 0, 1)).reshape(128, L * KC)

    def chan(v):
        return np.ascontiguousarray(v.reshape(L, 4, 128).transpose(2, 0, 1)).reshape(128, L * 4)

    pv[:, PV_GMIX:PV_GMIX + 16] = feat(inp['norm_mix_g'])
    pv[:, PV_GATT:PV_GATT + 16] = feat(inp['norm_attn_g'])
    pv[:, PV_GFFN:PV_GFFN + 16] = feat(inp['norm_ffn_g'])
    pv[:, PV_GMEM:PV_GMEM + 16] = feat(inp['norm_mem_g'])
    pv[:, PV_GFIN:PV_GFIN + 8] = inp['norm_final_g'].reshape(KC, 128).T
    rgw = inp['rg_conv_w'].reshape(L, 4, 4, 128).transpose(3, 0, 2, 1)
    pv[:, PV_RGW:PV_RGW + 32] = np.ascontiguousarray(rgw).reshape(128, 32)
    pv[:, PV_RGB:PV_RGB + 8] = chan(inp['rg_conv_b'])
    pv[:, PV_BA:PV_BA + 8] = chan(inp['rg_ba'])
    pv[:, PV_BX:PV_BX + 8] = chan(inp['rg_bx'])
    pv[:, PV_LAM:PV_LAM + 8] = chan(inp['rg_lambda'])
    cfw = inp['cf_conv_w'].reshape(L, 31, 4, 128).transpose(3, 0, 2, 1)
    pv[:, PV_CFW:PV_CFW + 248] = np.ascontiguousarray(cfw).reshape(128, 248)
    pv[:, PV_CFB:PV_CFB + 8] = chan(inp['cf_conv_b'])
    pv[:, PV_LNG:PV_LNG + 8] = chan(inp['cf_ln_g'])
    pv[:, PV_LNB:PV_LNB + 8] = chan(inp['cf_ln_b'])
    return pv


def _prep(inp):
    inp = {k: np.asarray(v) for k, v in inp.items()}
    f32 = np.float32
    pv = _pack_pv(inp)
    shared = {
        'pv': pv,
        'rg_wa': np.ascontiguousarray(inp['rg_wa'], f32), 'rg_wx': np.ascontiguousarray(inp['rg_wx'], f32),
        'w_in': np.ascontiguousarray(inp['w_in'], f32), 'w_out': np.ascontiguousarray(inp['w_out'], f32),
        'w_q': np.ascontiguousarray(inp['w_q'], f32), 'w_k': np.ascontiguousarray(inp['w_k'], f32),
        'w_v': np.ascontiguousarray(inp['w_v'], f32), 'w_o': np.ascontiguousarray(inp['w_o'], f32),
        'w_gate': np.ascontiguousarray(inp['w_gate'], f32), 'w_up': np.ascontiguousarray(inp['w_up'], f32),
        'w_down': np.ascontiguousarray(inp['w_down'], f32),
    }
    in_maps = []
    for c in range(NCORES):
        sl = slice(16 * c, 16 * (c + 1))
        xs = inp['x_sample'][sl].reshape(NSM, D)
        xT = np.ascontiguousarray(np.concatenate([inp['x_prompt'][c].T, xs.T], axis=1), f32)
        memT = np.ascontiguousarray(inp['mem_prompt'][c].T, f32)
        KcT = np.ascontiguousarray(inp['cache_mem_k'][:, sl].reshape(L, 16, 256, D).transpose(0, 1, 3, 2), f32)
        Vc = np.ascontiguousarray(inp['cache_mem_v'][:, sl].reshape(L, 16, 256, D), f32)
        h0 = np.ascontiguousarray(inp['state_rglru_h'][:, sl].reshape(L, 16, 4, 128).transpose(0, 3, 2, 1), f32)
        rgs = np.ascontiguousarray(
            inp['state_rglru_conv'][:, sl].reshape(L, 16, 3, 4, 128).transpose(0, 4, 3, 1, 2), f32)
        cfs = np.ascontiguousarray(
            inp['state_conf_conv'][:, sl].reshape(L, 16, 30, 4, 128).transpose(0, 4, 3, 1, 2), f32)
        m = dict(shared)
        m.update({'xT': xT, 'memT': memT, 'KcT': KcT, 'Vc': Vc, 'h0': h0, 'rgs': rgs, 'cfs': cfs})
        in_maps.append(m)
    return in_maps


def kernel(**inp):
    nc = _get_nc()
    in_maps = _prep(inp)
    res = run_bass_kernel_spmd(nc, in_maps, core_ids=list(range(NCORES)))
    return _post(res.results)


def _post(rs):
    f32 = np.float32
    B = NCORES
    y_prompt = np.empty((B, NPR, D), f32)
    y_sample = np.empty((128, 4, D), f32)
    p_h = np.empty((L, B, 512), f32)
    p_rg = np.empty((L, B, 3, 512), f32)
    p_cf = np.empty((L, B, 30, 512), f32)
    p_mk = np.empty((L, B, 256, 4, 256), f32)
    p_mv = np.empty((L, B, 256, 4, 256), f32)
    s_h = np.empty((L, 128, 512), f32)
    s_rg = np.empty((L, 128, 3, 512), f32)
    s_cf = np.empty((L, 128, 30, 512), f32)
    for c in range(NCORES):
        r = rs[c]
        sl = slice(16 * c, 16 * (c + 1))
        yT = r['yT']
        y_prompt[c] = yT[:, :NPR].T
        y_sample[sl] = yT[:, NPR:].T.reshape(16, 4, D)
        p_h[:, c] = r['o_ph'].transpose(0, 2, 1).reshape(L, 512)
        p_rg[:, c] = r['o_prg'].transpose(0, 3, 2, 1).reshape(L, 3, 512)
        p_cf[:, c] = r['o_pcf'].transpose(0, 3, 2, 1).reshape(L, 30, 512)
        p_mk[:, c] = r['o_pk'].transpose(0, 3, 2, 1).reshape(L, 256, 4, 256)
        p_mv[:, c] = r['o_pv'].reshape(L, 256, 4, 256)
        s_h[:, sl] = r['o_sh'].transpose(0, 3, 2, 1).reshape(L, 16, 512)
        s_rg[:, sl] = r['o_srg'].transpose(0, 3, 4, 2, 1).reshape(L, 16, 3, 512)
        scf = np.concatenate([r['o_scfh'], r['o_scfn']], axis=4)
        s_cf[:, sl] = scf.transpose(0, 3, 4, 2, 1).reshape(L, 16, 30, 512)
    return (y_prompt, y_sample, p_h, p_rg, p_cf, p_mk, p_mv, s_h, s_rg, s_cf)
```

```python
import itertools
import numpy as np
import concourse.bass as bass
import concourse.mybir as mybir
from concourse.bass_utils import run_bass_kernel_spmd

F32 = mybir.dt.float32
BF16 = mybir.dt.bfloat16
U8 = mybir.dt.uint8
AF = mybir.ActivationFunctionType
ALU = mybir.AluOpType

NCORES = 8
L = 2
D = 1024
KC = 8
NPR = 2048
NSM = 64
T = NPR + NSM
DFF = 2816
EPS = 1e-6
GRAN = 128
PSUM_BASE = 1 << 24
RING_SLOTS = 6
SLOT_BYTES = 8192

PV_GMIX, PV_GATT, PV_GFFN, PV_GMEM, PV_GFIN = 0, 16, 32, 48, 64
PV_RGW, PV_RGB, PV_BA, PV_BX, PV_LAM = 72, 104, 112, 120, 128
PV_CFW, PV_CFB, PV_LNG, PV_LNB = 136, 384, 392, 400
PV_N = 408


def _esz(dt):
    return 4 if dt == F32 else (2 if dt == BF16 else 1)


class Trk:
    def __init__(self, nc):
        self.nc = nc
        self.E = {'pe': nc.tensor, 'act': nc.scalar, 'dve': nc.vector, 'pool': nc.gpsimd, 'sp': nc.sync}
        self.sems = {}
        self.val = {}
        self.waited = {e: {} for e in self.E}
        self.lastw = {}
        self.rd = {}
        self.base = {}

    def reg(self, handle, addr):
        self.base[handle.name] = addr

    def sem(self, key):
        if key not in self.sems:
            self.sems[key] = self.nc.alloc_semaphore('s_' + key)
            self.val[key] = 0
        return self.sems[key]

    def grans(self, aps):
        out = set()
        for ap in aps:
            nm = ap.tensor.name
            if nm not in self.base:
                continue
            base = self.base[nm]
            es = _esz(ap.dtype)
            pat = ap.ap
            rowstride = pat[0][0]
            off = int(ap.offset)
            if rowstride > 0:
                off = off % rowstride
            free = pat[1:]
            if len(free) == 0:
                rngs = [(off, off + 1)]
            else:
                ist, icnt = free[-1]
                ilen = (icnt - 1) * abs(ist) + 1
                outer = free[:-1]
                rngs = []
                for idx in itertools.product(*[range(c) for (_, c) in outer]):
                    st = off + sum(i * s for i, (s, _) in zip(idx, outer))
                    rngs.append((st, st + ilen))
            for lo, hi in rngs:
                blo = base + lo * es
                bhi = base + hi * es
                for g in range(blo // GRAN, (bhi - 1) // GRAN + 1):
                    out.add(g)
        return out

    def _deps(self, rg, wg):
        need = {}
        for g in rg:
            w = self.lastw.get(g)
            if w is not None and need.get(w[0], 0) < w[1]:
                need[w[0]] = w[1]
        for g in wg:
            w = self.lastw.get(g)
            if w is not None and need.get(w[0], 0) < w[1]:
                need[w[0]] = w[1]
            r = self.rd.get(g)
            if r:
                for k, v in r.items():
                    if need.get(k, 0) < v:
                        need[k] = v
        return need

    def _commit(self, rg, wg, key, val):
        for g in rg:
            d = self.rd.get(g)
            if d is None:
                self.rd[g] = {key: val}
            else:
                d[key] = val
        for g in wg:
            self.lastw[g] = (key, val)
            self.rd[g] = None

    def _wait(self, e, need):
        eng = self.E[e]
        wd = self.waited[e]
        for k, v in need.items():
            if k not in self.E:
                v = self.val[k]
            if wd.get(k, 0) < v:
                eng.wait_ge(self.sems[k], v)
                wd[k] = v

    def op(self, e, reads, writes, fn):
        rg = self.grans(reads)
        wg = self.grans(writes)
        self._wait(e, self._deps(rg, wg))
        ins = fn()
        s = self.sem(e)
        self.val[e] += 1
        ins.then_inc(s, 1)
        self._commit(rg, wg, e, self.val[e])

    def dma(self, q, out, in_, key):
        reads = [in_] if in_.tensor.name in self.base else []
        writes = [out] if out.tensor.name in self.base else []
        rg = self.grans(reads)
        wg = self.grans(writes)
        self._wait(q, self._deps(rg, wg))
        s = self.sem(key)
        ins = self.E[q].dma_start(out=out, in_=in_)
        self.val[key] += 16
        ins.then_inc(s, 16)
        self._commit(rg, wg, key, self.val[key])


class Region:
    def __init__(self, nc, trk, base, size):
        self.nc, self.trk, self.base, self.size = nc, trk, base, size
        self.ptr = 0

    def reset(self):
        self.ptr = 0

    def alloc(self, name, shape, dt):
        n = 1
        for s in shape[1:]:
            n *= s
        nbytes = n * _esz(dt)
        nbytes = (nbytes + GRAN - 1) // GRAN * GRAN
        assert self.ptr + nbytes <= self.size, (name, self.ptr, nbytes, self.size)
        addr = self.base + self.ptr
        t = self.nc.alloc_sbuf_tensor_at(name, list(shape), dt, offset=addr)
        self.trk.reg(t, addr)
        self.ptr += nbytes
        return t


class _Stop(Exception):
    pass


_STOP = None


class Builder:
    def ck(self, name):
        if _STOP == name:
            raise _Stop()

    def __init__(self):
        self.nc = bass.Bass("TRN2", target_bir_lowering=False)
        self.trk = Trk(self.nc)
        self.pb = 0
        self.cf_prebuilt = set()
        self.cf_handle = {}

    def bank(self):
        self.pb = (self.pb + 1) % 8
        return self.ps[self.pb]

    def mm(self, out, pairs, extra_reads=()):
        reads = []
        for a, b in pairs:
            reads.append(a)
            reads.append(b)
        n = len(pairs)

        def fn():
            ins = None
            for i, (a, b) in enumerate(pairs):
                ins = self.nc.tensor.matmul(out, lhsT=a, rhs=b, start=(i == 0), stop=(i == n - 1))
            return ins
        self.trk.op('pe', reads, [out], fn)

    def mm_groups(self, groups):
        reads, writes = [], []
        for out, pairs in groups:
            writes.append(out)
            for a, b in pairs:
                reads.append(a)
                reads.append(b)

        def fn():
            ins = None
            for out, pairs in groups:
                n = len(pairs)
                for i, (a, b) in enumerate(pairs):
                    ins = self.nc.tensor.matmul(out, lhsT=a, rhs=b, start=(i == 0), stop=(i == n - 1))
            return ins
        self.trk.op('pe', reads, writes, fn)

    def act(self, out, in_, func, bias=None, scale=None):
        reads = [in_]
        kw = {}
        if bias is not None:
            kw['bias'] = bias
            if not isinstance(bias, (int, float)):
                reads.append(bias)
        if scale is not None:
            kw['scale'] = scale
            if not isinstance(scale, (int, float)):
                reads.append(scale)
        self.trk.op('act', reads, [out], lambda: self.nc.scalar.activation(out=out, in_=in_, func=func, **kw))

    def tt(self, out, in0, in1, op, eng='dve'):
        e = self.trk.E[eng]
        self.trk.op(eng, [in0, in1], [out], lambda: e.tensor_tensor(out=out, in0=in0, in1=in1, op=op))

    def ts(self, out, in0, s1, s2, op0, op1=None, eng='dve'):
        e = self.trk.E[eng]
        reads = [in0]
        if not isinstance(s1, (int, float)):
            reads.append(s1)
        if s2 is not None and not isinstance(s2, (int, float)):
            reads.append(s2)
        if op1 is None:
            self.trk.op(eng, reads, [out], lambda: e.tensor_scalar(out=out, in0=in0, scalar1=s1, scalar2=None, op0=op0))
        else:
            self.trk.op(eng, reads, [out], lambda: e.tensor_scalar(out=out, in0=in0, scalar1=s1, scalar2=s2, op0=op0, op1=op1))

    def stt(self, out, in0, scalar, in1, op0, op1):
        reads = [in0, in1]
        if not isinstance(scalar, (int, float)):
            reads.append(scalar)
        self.trk.op('dve', reads, [out], lambda: self.nc.vector.scalar_tensor_tensor(
            out=out, in0=in0, scalar=scalar, in1=in1, op0=op0, op1=op1))

    def scan(self, out, d0, d1, initial):
        reads = [d0, d1]
        if not isinstance(initial, (int, float)):
            reads.append(initial)
        self.trk.op('dve', reads, [out], lambda: self.nc.vector.tensor_tensor_scan(
            out=out, data0=d0, data1=d1, initial=initial, op0=ALU.mult, op1=ALU.add))

    def recip(self, out, in_):
        self.trk.op('dve', [in_], [out], lambda: self.nc.vector.reciprocal(out=out, in_=in_))

    def copy(self, out, in_, eng='dve'):
        e = self.trk.E[eng]
        self.trk.op(eng, [in_], [out], lambda: e.tensor_copy(out=out, in_=in_))

    def memset(self, out, v, eng='dve'):
        e = self.trk.E[eng]
        self.trk.op(eng, [], [out], lambda: e.memset(out, v))

    def ring_plan(self, blocks):
        self.blocks = blocks
        self.blk_view = [None] * len(blocks)
        self.blk_emitted = 0
        self.blk_released = [False] * len(blocks)

    def _ring_emit(self, i):
        ap = self.blocks[i]
        a, b = ap.shape[1], ap.shape[2]
        slot = i % RING_SLOTS
        v = self.ring[slot][:, 0:a * b].rearrange("p (a b) -> p a b", a=a)
        self.trk.dma('pool', v, ap, 'ring%d' % slot)
        self.blk_view[i] = v

    def _ring_pump(self, upto=None):
        while self.blk_emitted < len(self.blocks):
            j = self.blk_emitted
            if upto is not None and j <= upto:
                pass
            elif j >= RING_SLOTS and not self.blk_released[j - RING_SLOTS]:
                break
            if j >= RING_SLOTS:
                assert self.blk_released[j - RING_SLOTS], "ring overflow: block %d" % j
            self._ring_emit(j)
            self.blk_emitted += 1

    def wget(self, i):
        self._ring_pump(upto=i)
        return self.blk_view[i]

    def wrel(self, i):
        self.blk_released[i] = True
        self._ring_pump()

    def norm(self, xin, gvec, out_fn, N, sq, rstd):
        for kc in range(KC):
            self.act(sq[:, kc, 0:N], xin(kc), AF.Square)
        bk = self.bank()
        self.mm(bk[:, 0:N], [(self.c1024[:], sq[:, kc, 0:N]) for kc in range(KC)])
        self.act(rstd[:, 0:N], bk[:, 0:N], AF.Ln, bias=EPS)
        self.act(rstd[:, 0:N], rstd[:, 0:N], AF.Exp, scale=-0.5)
        for kc in range(KC):
            self.stt(out_fn(kc), xin(kc), gvec(kc), rstd[:, 0:N], ALU.mult, ALU.mult)

    def build(self):
        nc = self.nc
        trk = self.trk
        dr = {}

        def din(name, shape):
            dr[name] = nc.dram_tensor(name, list(shape), F32, kind="ExternalInput").ap()
            return dr[name]

        def dout(name, shape):
            dr[name] = nc.dram_tensor(name, list(shape), F32, kind="ExternalOutput").ap()
            return dr[name]

        xT = din("xT", [D, T])
        memT = din("memT", [D, 256])
        KcT = din("KcT", [L, 16, D, 256])
        Vc = din("Vc", [L, 16, 256, D])
        h0 = din("h0", [L, 128, 4, 16])
        rgs = din("rgs", [L, 128, 4, 16, 3])
        cfs = din("cfs", [L, 128, 4, 16, 30])
        pvd = din("pv", [128, PV_N])
        rg_wa = din("rg_wa", [L, 8, 64, 64])
        rg_wx = din("rg_wx", [L, 8, 64, 64])
        w_in = din("w_in", [L, D, 2048])
        w_out = din("w_out", [L, D, D])
        w_q = din("w_q", [L, D, D])
        w_k = din("w_k", [L, D, D])
        w_v = din("w_v", [L, D, D])
        w_o = din("w_o", [L, D, D])
        w_gate = din("w_gate", [L, D, DFF])
        w_up = din("w_up", [L, D, DFF])
        w_down = din("w_down", [L, DFF, D])

        yT = dout("yT", [D, T])
        o_ph = dout("o_ph", [L, 128, 4])
        o_prg = dout("o_prg", [L, 128, 4, 3])
        o_pcf = dout("o_pcf", [L, 128, 4, 30])
        o_pk = dout("o_pk", [L, 128, 8, 256])
        o_pv = dout("o_pv", [L, 256, D])
        o_sh = dout("o_sh", [L, 128, 4, 16])
        o_srg = dout("o_srg", [L, 128, 4, 16, 3])
        o_scfh = dout("o_scfh", [L, 128, 4, 16, 26])
        o_scfn = dout("o_scfn", [L, 128, 4, 16, 4])

        ARENA = 212800
        arena = nc.alloc_sbuf_tensor("arena", [128, ARENA], U8)
        abase = nc.lookup_mloc(arena).addr
        P = Region(nc, trk, abase, ARENA)
        X = P.alloc("X", [128, KC, T], F32)
        identf = P.alloc("identf", [128, 128], F32)
        ident = P.alloc("ident", [128, 128], BF16)
        self.c1024 = P.alloc("c1024", [128, 128], BF16)
        c512 = P.alloc("c512", [128, 128], BF16)
        ones = P.alloc("ones", [128, 128], BF16)
        pv = P.alloc("pvs", [128, PV_N], F32)
        cl = P.alloc("cl", [128, L * 4], F32)
        cl2 = P.alloc("cl2", [128, L * 4], F32)
        wabd = P.alloc("wabd", [128, L * 2 * 4, 128], BF16)
        hstate = P.alloc("hstate", [128, 4], F32)
        self.ring = [P.alloc("ring%d" % i, [128, SLOT_BYTES // 2], BF16) for i in range(RING_SLOTS)]
        rbase = abase + P.ptr
        R = Region(nc, trk, rbase, ARENA - P.ptr)
        self.ps = []
        for i in range(8):
            t = nc.alloc_psum_tensor("ps%d" % i, [128, 512], F32)
            trk.reg(t, PSUM_BASE + i * 2048)
            self.ps.append(t)

        blocks = []
        bidx = {}

        def wv(w, l):
            return w[l].rearrange("(kc p) n -> p kc n", p=128)

        for l in range(L):
            for c in range(4):
                bidx[(l, 'in', c)] = len(blocks)
                blocks.append(wv(w_in, l)[:, :, c * 512:(c + 1) * 512])
            for c in range(2):
                bidx[(l, 'out', c)] = len(blocks)
                blocks.append(wv(w_out, l)[:, :, c * 512:(c + 1) * 512])
            for nm, w in (('k', w_k), ('v', w_v), ('q', w_q), ('o', w_o)):
                for c in range(2):
                    bidx[(l, nm, c)] = len(blocks)
                    blocks.append(wv(w, l)[:, :, c * 512:(c + 1) * 512])
            for g in range(3):
                ncols = 1024 if g < 2 else 768
                c0 = g * 1024
                for hb in range(2):
                    lo = c0 + hb * 512
                    hi = min(c0 + ncols, lo + 512)
                    bidx[(l, 'gate', g, hb)] = len(blocks)
                    blocks.append(wv(w_gate, l)[:, :, lo:hi])
                    bidx[(l, 'up', g, hb)] = len(blocks)
                    blocks.append(wv(w_up, l)[:, :, lo:hi])
                nch = ncols // 128
                for hb in range(2):
                    k0 = hb * 4
                    k1 = min(nch, k0 + 4)
                    bidx[(l, 'down', g, hb)] = len(blocks)
                    blocks.append(w_down[l, c0 + k0 * 128:c0 + k1 * 128, :].rearrange("(kc p) n -> p kc n", p=128))
        self.ring_plan(blocks)

        A = self.act
        xv = xT.rearrange("(kc p) n -> p kc n", p=128)
        yv = yT.rearrange("(kc p) n -> p kc n", p=128)

        def pvc(off, n=1):
            return pv[:, off:off + n]

        with nc.Block():
            trk.dma('sp', pv[:], pvd[:, :], 'ld_pv')
            xtiles = [(ti, c0, min(512, T - c0)) for ti, c0 in enumerate(range(0, T, 512))]
            ti, c0, n = xtiles[0]
            trk.dma('sp', X[:, :, c0:c0 + n], xv[:, :, c0:c0 + n], 'ld_x%d' % ti)
            self.memset(identf[:], 0.0, eng='dve')
            trk.op('pool', [identf[:]], [identf[:]], lambda: nc.gpsimd.affine_select(
                out=identf[:], in_=identf[:], pattern=[[-1, 128]], compare_op=ALU.not_equal, fill=1.0,
                base=0, channel_multiplier=1))
            self.copy(ident[:], identf[:])
            self.memset(self.c1024[:], 1.0 / 1024.0)
            self.memset(c512[:], 1.0 / 512.0)
            self.memset(ones[:], 1.0)
            R.reset()
            R.ptr = 71936
            wst = R.alloc("wst", [128, L * 2 * 4, 128], F32)
            self.memset(wst[:], 0.0)
            for l in range(L):
                for gi, wsrc in enumerate((rg_wa, rg_wx)):
                    for hh in range(8):
                        j, e = hh // 2, hh % 2
                        trk.dma('sp', wst[e * 64:(e + 1) * 64, (l * 2 + gi) * 4 + j, e * 64:(e + 1) * 64],
                                wsrc[l, hh, :, :], 'ld_w%d' % ((l * 2 + gi) % 2))
            self.deferred = lambda: self.copy(wabd[:], wst[:])
            for ti, c0, n in xtiles[1:]:
                trk.dma('sp', X[:, :, c0:c0 + n], xv[:, :, c0:c0 + n], 'ld_x%d' % ti)
            for l in range(L):
                for j in range(4):
                    trk.dma('sp', o_scfh[l, :, j], cfs[l, :, j, :, 4:30], 'st_h')
            spt = R.alloc("spt", [128, L * 4], F32)
            A(spt[:], pvc(PV_LAM, 8), AF.Exp, scale=-1.0)
            A(spt[:], spt[:], AF.Ln, bias=1.0)
            self.ts(cl[:], spt[:], -8.0, None, ALU.mult)
            self.ts(cl2[:], spt[:], -16.0, None, ALU.mult)

            try:
                self.ck('P')
                for l in range(L):
                    self.layer(l, locals())
                if not getattr(self, 'final_done', False):
                    self.final_norm(locals())
            except _Stop:
                pass
            for k, v in trk.val.items():
                if k not in trk.E and v > 0:
                    nc.sync.wait_ge(trk.sems[k], v)
        return nc

    def build_diag(self, l, cfD, rgD, identf, pvc, do_cf=True, do_rg=True):
        MUL = ALU.mult
        for j in range(4):
            if do_rg:
                for k in range(4):
                    self.ts(rgD[:, j * 4 + k, :], identf[:], pvc(PV_RGW + (l * 4 + j) * 4 + k), None, MUL)
            if do_cf:
                for k in range(31):
                    if k % 3 == 2:
                        self.act(cfD[:, j * 31 + k, :], identf[:], AF.Copy,
                                 scale=pvc(PV_CFW + (l * 4 + j) * 31 + k))
                    else:
                        self.ts(cfD[:, j * 31 + k, :], identf[:], pvc(PV_CFW + (l * 4 + j) * 31 + k), None, MUL)

    def layer(self, l, env):
        nc, trk = self.nc, self.trk
        R = env['R']; X = env['X']; pv = env['pv']; bidx = env['bidx']
        ident, identf, c512, ones = env['ident'], env['identf'], env['c512'], env['ones']
        cl, cl2, wabd, hstate = env['cl'], env['cl2'], env['wabd'], env['hstate']
        dr_ = env['dr']
        A = self.act
        MUL, ADD, SUB = ALU.mult, ALU.add, ALU.subtract

        def pvc(off, n=1):
            return pv[:, off:off + n]

        R.reset()
        cfD = R.alloc("cfD", [128, 4 * 31, 128], BF16)
        if l in self.cf_handle:
            cfD = self.cf_handle[l]
        rgD = R.alloc("rgD", [128, 4 * 4, 128], BF16)
        NA = 256
        ycat = R.alloc("a_ycat", [128, KC, NA], BF16)
        sq = R.alloc("a_sq", [128, KC, NA], BF16)
        rstd = R.alloc("a_rstd", [128, NA], F32)
        xn = R.alloc("a_xn", [128, KC, NA], BF16)
        o_x = R.ptr
        xrb0 = R.alloc("a_xrb0", [128, 4, 3 + NA], BF16)
        cb0 = R.alloc("a_cb0", [128, 4, 30 + NA], BF16)
        o_end = R.ptr
        R.ptr = o_x
        csb = R.alloc("a_csb", [128, 4, 16, 34], BF16)
        assert R.ptr <= o_end
        R.ptr = o_end
        xrb1 = R.alloc("a_xrb1", [128, 4, 3 + NA], BF16)
        cb1 = R.alloc("a_cb1", [128, 4, 30 + NA], BF16)
        xrb2 = [xrb0, xrb1]
        cb2 = [cb0, cb1]
        gg2 = [R.alloc("a_gg%d" % i, [128, 4, NA], BF16) for i in range(2)]
        sg = R.alloc("a_sg", [128, NA], F32)
        xc2 = [R.alloc("a_xc%d" % i, [128, NA], F32) for i in range(2)]
        xcb = R.alloc("a_xcb", [128, NA], BF16)
        r_ = R.alloc("a_r", [128, NA], F32)
        i_ = R.alloc("a_i", [128, NA], F32)
        a_ = R.alloc("a_a", [128, NA], F32)
        m_ = R.alloc("a_m", [128, NA], F32)
        b_ = R.alloc("a_b", [128, NA], F32)
        h_ = R.alloc("a_h", [128, NA], F32)
        ccf = R.alloc("a_ccf", [128, 4, NA], F32)
        ccb = R.alloc("a_ccb", [128, 4, NA], BF16)
        sqb = R.alloc("a_sqb", [128, 4, NA], BF16)
        mean = R.alloc("a_mean", [128, NA], F32)
        var = R.alloc("a_var", [128, NA], F32)
        rstc = R.alloc("a_rstc", [128, NA], F32)
        o_dd = R.ptr
        dd = R.alloc("a_dd", [128, NA], F32)
        dd2 = [dd, sg]
        prg = R.alloc("a_prg", [128, 4, 3], F32)
        pcf = R.alloc("a_pcf", [128, 4, 30], F32)
        xrs = R.alloc("a_xrs", [128, 4, 16, 7], BF16)
        o_keep = R.ptr
        R.ptr = o_dd
        xrsf = R.alloc("a_xrsf", [128, 4, 16, 3], F32)
        R.ptr = o_keep
        srg = R.alloc("a_srg", [128, 4, 16, 3], F32)
        cs4 = R.alloc("a_cs4", [128, 4, 16, 4], F32)
        h0s = R.alloc("a_h0s", [128, 4, 16], F32)
        shl = R.alloc("a_shl", [128, 4, 16], F32)
        tmp16 = R.alloc("a_tmp16", [128, 16], F32)
        nb2 = R.alloc("a_nb2", [128, 8], F32)

        self.build_diag(l, cfD, rgD, identf, pvc, do_cf=(l not in self.cf_prebuilt))
        if getattr(self, 'deferred', None) is not None:
            self.deferred()
            self.deferred = None
        self.ts(nb2[:, 0:4], pvc(PV_BA + l * 4, 4), -1.0, None, MUL)
        self.ts(nb2[:, 4:8], pvc(PV_BX + l * 4, 4), -1.0, None, MUL)
        self.memset(xrb0[:, :, 0:3], 0.0)
        self.memset(cb0[:, :, 0:30], 0.0)
        self.memset(hstate[:], 0.0)

        win = [self.wget(bidx[(l, 'in', c)]) for c in range(4)]
        wout = [self.wget(bidx[(l, 'out', c)]) for c in range(2)]
        tiles = [(c0, NA, False) for c0 in range(0, NPR, NA)] + [(NPR, NSM, True)]
        NT = len(tiles)

        def load_sample_state():
            trk.dma('sp', xrsf[:], dr_['rgs'][l], 'ld_st0')
            trk.dma('sp', h0s[:], dr_['h0'][l], 'ld_st1')
            for j in range(4):
                trk.dma('pool', csb[:, j, :, 0:30], dr_['cfs'][l, :, j], 'ld_st%d' % (2 + j))
            self.copy(xrs[:, :, :, 0:3], xrsf[:])

        def v3(ap):
            return ap.rearrange("p (b t) -> p b t", t=4)

        def front(ti):
            c0, N, smp = tiles[ti]
            sset = ti % 2
            xrb, cb, gg = xrb2[sset], cb2[sset], gg2[sset]
            last_p = (not smp) and (c0 + N == NPR)
            if smp:
                load_sample_state()
            elif ti >= 1:
                self.copy(xrb[:, :, 0:3], xrb2[1 - sset][:, :, NA:NA + 3])
                self.copy(cb[:, :, 0:30], cb2[1 - sset][:, :, NA:NA + 30])
            for kc in range(KC):
                A(sq[:, kc, 0:N], X[:, kc, c0:c0 + N], AF.Square)
            bkn = self.bank()
            self.mm(bkn[:, 0:N], [(self.c1024[:], sq[:, kc, 0:N]) for kc in range(KC)])
            A(rstd[:, 0:N], bkn[:, 0:N], AF.Ln, bias=EPS)
            A(rstd[:, 0:N], rstd[:, 0:N], AF.Exp, scale=-0.5)
            yield
            for kc in range(KC):
                self.stt(xn[:, kc, 0:N], X[:, kc, c0:c0 + N], pvc(PV_GMIX + l * 8 + kc), rstd[:, 0:N], MUL, MUL)
            yield

            def proj(blk, j):
                bk = self.bank()
                self.mm(bk[:, 0:N], [(win[blk][:, kc, j * 128:(j + 1) * 128], xn[:, kc, 0:N]) for kc in range(KC)])
                return bk
            yield
            for j in range(4):
                bk = proj(1, j)
                A(gg[:, j, 0:N], bk[:, 0:N], AF.Gelu_apprx_tanh)
            yield
            yield
            for j in range(4):
                bk = proj(0, j)
                if smp:
                    A(xrs[:, j, :, 3:7], v3(bk[:, 0:N]), AF.Copy)
                    A(srg[:, j, :, :], v3(bk[:, 0:N])[:, :, 1:4], AF.Copy)
                else:
                    A(xrb[:, j, 3:3 + N], bk[:, 0:N], AF.Copy)
                    if last_p:
                        A(prg[:, j, :], bk[:, N - 3:N], AF.Copy)
            yield
            for half in range(1):
                for j in range(4):
                    bv = proj(2, j)
                    bg = proj(3, j)
                    A(sg[:, 0:N], bg[:, 0:N], AF.Sigmoid)
                    if smp:
                        self.tt(cs4[:, j, :, :], v3(bv[:, 0:N]), v3(sg[:, 0:N]), MUL)
                        self.copy(csb[:, j, :, 30:34], cs4[:, j, :, :])
                    else:
                        self.tt(cb[:, j, 30:30 + N], bv[:, 0:N], sg[:, 0:N], MUL)
                        if last_p:
                            self.tt(pcf[:, j, :], bv[:, N - 30:N], sg[:, N - 30:N], MUL)
                yield

        def back(ti):
            c0, N, smp = tiles[ti]
            sset = ti % 2
            xrb, cb, gg = xrb2[sset], cb2[sset], gg2[sset]

            def tail(j):
                xc = xc2[j % 2]
                self.tt(b_[:, 0:N], i_[:, 0:N], xc[:, 0:N], MUL)
                self.tt(b_[:, 0:N], b_[:, 0:N], m_[:, 0:N], MUL)
                if smp:
                    av = v3(a_[:, 0:N])
                    bvw = v3(b_[:, 0:N])
                    self.tt(tmp16[:], av[:, :, 0], h0s[:, j, :], MUL)
                    self.tt(bvw[:, :, 0], bvw[:, :, 0], tmp16[:], ADD)
                    self.memset(av[:, :, 0], 0.0)
                    self.scan(h_[:, 0:N], a_[:, 0:N], b_[:, 0:N], 0.0)
                    self.copy(shl[:, j, :], v3(h_[:, 0:N])[:, :, 3])
                else:
                    self.scan(h_[:, 0:N], a_[:, 0:N], b_[:, 0:N], hstate[:, j:j + 1])
                    self.copy(hstate[:, j:j + 1], h_[:, N - 1:N])
                self.tt(ycat[:, j, 0:N], h_[:, 0:N], gg[:, j, 0:N], MUL)

            for j in range(4):
                xc = xc2[j % 2]
                bk = self.bank()
                if smp:
                    self.mm(bk[:, 0:N], [(rgD[:, j * 4 + k, :], xrs[:, j, :, k:k + 4]) for k in range(4)])
                else:
                    self.mm(bk[:, 0:N], [(rgD[:, j * 4 + k, :], xrb[:, j, k:k + N]) for k in range(4)])
                bias = pvc(PV_RGB + l * 4 + j)
                self.ts(xcb[:, 0:N], bk[:, 0:N], bias, None, ADD)
                self.ts(xc[:, 0:N], bk[:, 0:N], bias, None, ADD)
                bc = self.bank()
                if smp:
                    self.mm(bc[:, 0:N], [(cfD[:, j * 31 + k, :], csb[:, j, :, k:k + 4]) for k in range(31)])
                else:
                    self.mm(bc[:, 0:N], [(cfD[:, j * 31 + k, :], cb[:, j, k:k + N]) for k in range(31)])
                ba = self.bank()
                self.mm(ba[:, 0:N], [(wabd[:, (l * 2 + 0) * 4 + j, :], xcb[:, 0:N])])
                bx = self.bank()
                self.mm(bx[:, 0:N], [(wabd[:, (l * 2 + 1) * 4 + j, :], xcb[:, 0:N])])
                cbias = pvc(PV_CFB + l * 4 + j)
                self.ts(ccf[:, j, 0:N], bc[:, 0:N], cbias, None, ADD)
                self.ts(ccb[:, j, 0:N], bc[:, 0:N], cbias, None, ADD)
                self.stt(sqb[:, j, 0:N], bc[:, 0:N], cbias, ccf[:, j, 0:N], ADD, MUL)
                if j == 3:
                    bm = self.bank()
                    self.mm(bm[:, 0:N], [(c512[:], ccb[:, jj, 0:N]) for jj in range(4)])
                    bq = self.bank()
                    self.mm(bq[:, 0:N], [(c512[:], sqb[:, jj, 0:N]) for jj in range(4)])
                    A(mean[:, 0:N], bm[:, 0:N], AF.Copy)
                    self.tt(var[:, 0:N], mean[:, 0:N], mean[:, 0:N], MUL)
                    self.tt(var[:, 0:N], bq[:, 0:N], var[:, 0:N], SUB)
                    A(rstc[:, 0:N], var[:, 0:N], AF.Ln, bias=EPS)
                    A(rstc[:, 0:N], rstc[:, 0:N], AF.Exp, scale=-0.5)
                if j >= 1:
                    tail(j - 1)
                A(r_[:, 0:N], ba[:, 0:N], AF.Exp, scale=-1.0, bias=nb2[:, j:j + 1])
                A(i_[:, 0:N], bx[:, 0:N], AF.Exp, scale=-1.0, bias=nb2[:, 4 + j:5 + j])
                A(r_[:, 0:N], r_[:, 0:N], AF.Ln, bias=1.0)
                A(i_[:, 0:N], i_[:, 0:N], AF.Ln, bias=1.0)
                A(r_[:, 0:N], r_[:, 0:N], AF.Exp, scale=-1.0)
                A(i_[:, 0:N], i_[:, 0:N], AF.Exp, scale=-1.0)
                A(a_[:, 0:N], r_[:, 0:N], AF.Exp, scale=cl[:, l * 4 + j:l * 4 + j + 1])
                A(m_[:, 0:N], r_[:, 0:N], AF.Exp, scale=cl2[:, l * 4 + j:l * 4 + j + 1])
                A(m_[:, 0:N], m_[:, 0:N], AF.Ln, scale=-1.0, bias=1.0000001)
                A(m_[:, 0:N], m_[:, 0:N], AF.Exp, scale=0.5)
                yield
            for j in range(4):
                dd = dd2[j % 2]
                self.tt(dd[:, 0:N], ccf[:, j, 0:N], mean[:, 0:N], SUB)
                self.tt(dd[:, 0:N], dd[:, 0:N], rstc[:, 0:N], MUL)
                A(ycat[:, 4 + j, 0:N], dd[:, 0:N], AF.Silu, bias=pvc(PV_LNB + l * 4 + j),
                  scale=pvc(PV_LNG + l * 4 + j))
            yield
            tail(3)
            yield
            wout_proj(ti)
            yield

        def wout_proj(ti):
            c0, N, smp = tiles[ti]
            for oc in range(KC):
                bk = self.bank()
                self.mm(bk[:, 0:N], [(wout[oc // 4][:, kc, (oc % 4) * 128:(oc % 4 + 1) * 128], ycat[:, kc, 0:N])
                                     for kc in range(KC)])
                self.tt(X[:, oc, c0:c0 + N], X[:, oc, c0:c0 + N], bk[:, 0:N], ADD)

        def drive(gens):
            gens = [g for g in gens if g is not None]
            while gens:
                alive = []
                for g in gens:
                    try:
                        next(g)
                        alive.append(g)
                    except StopIteration:
                        pass
                gens = alive

        drive([front(0)])
        for ti in range(NT):
            drive([front(ti + 1) if ti + 1 < NT else None, back(ti)])
            if ti == NT - 2:
                for c in range(4):
                    self.wrel(bidx[(l, 'in', c)])

        trk.dma('sp', dr_['o_ph'][l], hstate[:], 'st_a0')
        trk.dma('sp', dr_['o_prg'][l], prg[:], 'st_a1')
        trk.dma('sp', dr_['o_pcf'][l], pcf[:], 'st_a2')
        trk.dma('sp', dr_['o_sh'][l], shl[:], 'st_a3')
        trk.dma('sp', dr_['o_srg'][l], srg[:], 'st_a4')
        trk.dma('sp', dr_['o_scfn'][l], cs4[:], 'st_a5')
        for c in range(2):
            self.wrel(bidx[(l, 'out', c)])

        self.ck('A%d' % l)
        R.reset()
        NB = 512
        KpT = R.alloc("b_KpT", [128, KC, 256], BF16)
        Vp = R.alloc("b_Vp", [128, 2, D], BF16)
        Ks = [R.alloc("b_Ks%d" % i, [128, KC, 256], BF16) for i in range(2)]
        Vs = [R.alloc("b_Vs%d" % i, [128, 2, D], BF16) for i in range(2)]
        sq = R.alloc("b_sq", [128, KC, NB], BF16)
        rstd = R.alloc("b_rstd", [128, NB], F32)
        xn = R.alloc("b_xn", [128, KC, NB], BF16)
        sq_s = R.alloc("b_sqs", [128, KC, NSM], BF16)
        rstd_s = R.alloc("b_rstds", [128, NSM], F32)
        xn_s = R.alloc("b_xns", [128, KC, NSM], BF16)
        mark = R.ptr
        memf = R.alloc("b_memf", [128, KC, 256], F32)
        memn = R.alloc("b_memn", [128, KC, 256], BF16)
        msq = R.alloc("b_msq", [128, KC, 256], BF16)
        mrs = R.alloc("b_mrs", [128, 256], F32)
        kst = R.alloc("b_kst", [128, KC, 256], F32)
        vst = R.alloc("b_vst", [128, 2, D], F32)
        trk.dma('sp', memf[:], dr_['memT'].rearrange("(kc p) n -> p kc n", p=128), 'ld_mem')
        self.norm(lambda kc: memf[:, kc, :], lambda kc: pvc(PV_GMEM + l * 8 + kc),
                  lambda kc: memn[:, kc, :], 256, msq, mrs)
        self.norm(lambda kc: X[:, kc, NPR:NPR + NSM], lambda kc: pvc(PV_GATT + l * 8 + kc),
                  lambda kc: xn_s[:, kc, 0:NSM], NSM, sq_s, rstd_s)
        self.norm(lambda kc: X[:, kc, 0:NB], lambda kc: pvc(PV_GATT + l * 8 + kc),
                  lambda kc: xn[:, kc, 0:NB], NB, sq, rstd)
        self.ck('Bn%d' % l)
        wk = [self.wget(bidx[(l, 'k', c)]) for c in range(2)]
        self.ck('Bw%d' % l)
        for dc in range(KC):
            bk = self.bank()
            self.mm(bk[:, 0:256], [(wk[dc // 4][:, kc, (dc % 4) * 128:(dc % 4 + 1) * 128], memn[:, kc, :])
                                   for kc in range(KC)])
            A(KpT[:, dc, :], bk[:, 0:256], AF.Copy)
            A(kst[:, dc, :], bk[:, 0:256], AF.Copy)
        self.ck('Bk%d' % l)
        trk.dma('sp', dr_['o_pk'][l], kst[:], 'st_bk')
        for c in range(2):
            self.wrel(bidx[(l, 'k', c)])
        wvv = [self.wget(bidx[(l, 'v', c)]) for c in range(2)]
        for mt in range(2):
            for cbk in range(2):
                bk = self.bank()
                self.mm(bk[:, :], [(memn[:, kc, mt * 128:(mt + 1) * 128], wvv[cbk][:, kc, :]) for kc in range(KC)])
                A(Vp[:, mt, cbk * 512:(cbk + 1) * 512], bk[:, :], AF.Copy)
                A(vst[:, mt, cbk * 512:(cbk + 1) * 512], bk[:, :], AF.Copy)
        trk.dma('sp', dr_['o_pv'][l].rearrange("(j p) d -> p j d", p=128), vst[:], 'st_bv')
        for c in range(2):
            self.wrel(bidx[(l, 'v', c)])
        self.ck('Ba%d' % l)
        R.ptr = mark
        qT2 = [R.alloc("b_qT%d" % i, [128, KC, NB], BF16) for i in range(2)]
        eT2 = [R.alloc("b_eT%d" % i, [128, 2, NB], BF16) for i in range(2)]
        rs2 = [R.alloc("b_rs%d" % i, [128, NB], F32) for i in range(2)]
        oT = R.alloc("b_oT", [128, KC, NB], BF16)
        qTs = R.alloc("b_qTs", [128, KC, NSM], BF16)
        oTs = R.alloc("b_oTs", [128, KC, NSM], BF16)
        esb = [R.alloc("b_esb%d" % i, [128, 32], BF16) for i in range(2)]
        ssb = R.alloc("b_ssb", [128, 32], F32)
        rsb = R.alloc("b_rsb", [128, 16], F32)
        wq = [self.wget(bidx[(l, 'q', c)]) for c in range(2)]
        wo = [self.wget(bidx[(l, 'o', c)]) for c in range(2)]

        def qproj(c0, N, qdst, xn=xn, do_norm=True):
            if do_norm:
                self.norm(lambda kc: X[:, kc, c0:c0 + N], lambda kc: pvc(PV_GATT + l * 8 + kc),
                          lambda kc: xn[:, kc, 0:N], N, sq, rstd)
            yield
            for oc in range(KC):
                bk = self.bank()
                self.mm(bk[:, 0:N], [(wq[oc // 4][:, kc, (oc % 4) * 128:(oc % 4 + 1) * 128], xn[:, kc, 0:N])
                                     for kc in range(KC)])
                A(qdst[:, oc, 0:N], bk[:, 0:N], AF.Copy, scale=1.0 / 16.0)
                if oc == 3:
                    yield
            yield

        def oproj(c0, N, osrc):
            for oc in range(KC):
                bk = self.bank()
                self.mm(bk[:, 0:N], [(wo[oc // 4][:, kc, (oc % 4) * 128:(oc % 4 + 1) * 128], osrc[:, kc, 0:N])
                                     for kc in range(KC)])
                self.tt(X[:, oc, c0:c0 + N], X[:, oc, c0:c0 + N], bk[:, 0:N], ADD)

        def kv_load(b):
            trk.dma('pool', Ks[b % 2][:], dr_['KcT'][l, b].rearrange("(kc p) m -> p kc m", p=128), 'ld_k%d' % (b % 2))
            trk.dma('pool', Vs[b % 2][:], dr_['Vc'][l, b].rearrange("(j p) d -> p j d", p=128), 'ld_v%d' % (b % 2))

        def sample_batch(b):
            if b + 1 < 16:
                kv_load(b + 1)
            kb, vb, es = Ks[b % 2], Vs[b % 2], esb[b % 2]
            bsc = self.bank()
            groups = []
            for hh in range(4):
                for jm in range(2):
                    col = hh * 8 + jm * 4
                    groups.append((bsc[:, col:col + 4],
                                   [(kb[:, 2 * hh + dc, jm * 128:(jm + 1) * 128],
                                     qTs[:, 2 * hh + dc, b * 4:(b + 1) * 4]) for dc in range(2)]))
            self.mm_groups(groups)
            A(es[:, :], bsc[:, 0:32], AF.Exp)
            bs = self.bank()
            self.mm(bs[:, 0:32], [(ones[:], es[:, :])])
            A(ssb[:, :], bs[:, 0:32], AF.Copy)
            sv = ssb[:, :].rearrange("p (h j t) -> p h j t", j=2, t=4)
            self.tt(rsb[:, :].rearrange("p (h t) -> p h t", t=4), sv[:, :, 0, :], sv[:, :, 1, :], ADD)
            self.recip(rsb[:, :], rsb[:, :])
            bo = self.bank()
            groups = []
            for hh in range(4):
                for dc in range(2):
                    col = (hh * 2 + dc) * 4
                    groups.append((bo[:, col:col + 4],
                                   [(vb[:, jm, hh * 256 + dc * 128:hh * 256 + (dc + 1) * 128],
                                     es[:, hh * 8 + jm * 4:hh * 8 + jm * 4 + 4]) for jm in range(2)]))
            self.mm_groups(groups)
            for dc in range(2):
                ov = oTs[:, :, b * 4:(b + 1) * 4].rearrange("p (h dc) t -> p dc h t", dc=2)[:, dc]
                iv = bo[:, 0:32].rearrange("p (h dc t) -> p dc h t", dc=2, t=4)[:, dc]
                rv = rsb[:, :].rearrange("p (h t) -> p h t", t=4)
                self.tt(ov, iv, rv, MUL)

        def scores(hh, N, qT):
            eT = eT2[hh % 2]
            for jm in range(2):
                bk = self.bank()
                self.mm(bk[:, 0:N], [(KpT[:, 2 * hh + dc, jm * 128:(jm + 1) * 128], qT[:, 2 * hh + dc, 0:N])
                                     for dc in range(2)])
                A(eT[:, jm, 0:N], bk[:, 0:N], AF.Exp)

        def sum_pv(hh, N):
            eT, rs = eT2[hh % 2], rs2[hh % 2]
            bs = self.bank()
            self.mm(bs[:, 0:N], [(ones[:], eT[:, jm, 0:N]) for jm in range(2)])
            A(rs[:, 0:N], bs[:, 0:N], AF.Ln)
            A(rs[:, 0:N], rs[:, 0:N], AF.Exp, scale=-1.0)
            for dc in range(2):
                bo = self.bank()
                self.mm(bo[:, 0:N], [(Vp[:, jm, hh * 256 + dc * 128:hh * 256 + (dc + 1) * 128], eT[:, jm, 0:N])
                                     for jm in range(2)])
                self.tt(oT[:, 2 * hh + dc, 0:N], bo[:, 0:N], rs[:, 0:N], MUL)

        def drive(gens):
            gens = [g for g in gens if g is not None]
            while gens:
                alive = []
                for g in gens:
                    try:
                        next(g)
                        alive.append(g)
                    except StopIteration:
                        pass
                gens = alive

        ptiles = list(range(0, NPR, NB))

        def backB(ti):
            qT = qT2[ti % 2]
            scores(0, NB, qT)
            for hh in range(4):
                if hh + 1 < 4:
                    scores(hh + 1, NB, qT)
                sum_pv(hh, NB)
                sample_batch(ti * 4 + hh)
                yield
            oproj(ptiles[ti], NB, oT)
            yield

        kv_load(0)
        drive([qproj(NPR, NSM, qTs, xn=xn_s, do_norm=False)])
        drive([qproj(ptiles[0], NB, qT2[0], do_norm=False)])
        for ti in range(len(ptiles)):
            nxt = qproj(ptiles[ti + 1], NB, qT2[(ti + 1) % 2]) if ti + 1 < len(ptiles) else None
            drive([nxt, backB(ti)])
        oproj(NPR, NSM, oTs)
        for c in range(2):
            self.wrel(bidx[(l, 'q', c)])
        for c in range(2):
            self.wrel(bidx[(l, 'o', c)])

        self.ck('B%d' % l)
        R.reset()
        NC_ = 512
        o_xnf = R.ptr
        xnf = R.alloc("c_xn", [128, KC, T], BF16)
        hb = R.alloc("c_h", [128, 8, T], BF16)
        sq = R.alloc("c_sq", [128, KC, NC_], BF16)
        rstd = R.alloc("c_rstd", [128, NC_], F32)
        sgt = [R.alloc("c_sg%d" % i, [128, NC_], F32) for i in range(2)]
        tiles = [(c0, min(NC_, T - c0)) for c0 in range(0, T, NC_)]
        def cnorm(ti):
            c0, N = tiles[ti]
            self.norm(lambda kc: X[:, kc, c0:c0 + N], lambda kc: pvc(PV_GFFN + l * 8 + kc),
                      lambda kc: xnf[:, kc, c0:c0 + N], N, sq, rstd)
        cnorm(0)
        cnorm(1)
        normed = 2
        cnt = 0
        for g in range(3):
            nch = 8 if g < 2 else 6
            for hbk in range(2):
                k0 = hbk * 4
                k1 = min(nch, k0 + 4)
                if k1 <= k0:
                    continue
                wg = self.wget(bidx[(l, 'gate', g, hbk)])
                wu = self.wget(bidx[(l, 'up', g, hbk)])
                for fcl in range(k0, k1):
                    cc = (fcl - k0) * 128
                    for ti_, (c0, N) in enumerate(tiles):
                        if normed < len(tiles) and ti_ + 2 >= normed:
                            cnorm(normed)
                            normed += 1
                        bg = self.bank()
                        self.mm(bg[:, 0:N], [(wg[:, kc, cc:cc + 128], xnf[:, kc, c0:c0 + N]) for kc in range(KC)])
                        bu = self.bank()
                        self.mm(bu[:, 0:N], [(wu[:, kc, cc:cc + 128], xnf[:, kc, c0:c0 + N]) for kc in range(KC)])
                        st = sgt[cnt % 2]
                        cnt += 1
                        A(st[:, 0:N], bg[:, 0:N], AF.Silu)
                        self.tt(hb[:, fcl, c0:c0 + N], bu[:, 0:N], st[:, 0:N], MUL)
                self.wrel(bidx[(l, 'gate', g, hbk)])
                self.wrel(bidx[(l, 'up', g, hbk)])
            wd = [self.wget(bidx[(l, 'down', g, hbk)]) for hbk in range(2)]
            if g == 2 and l + 1 < L:
                o_keep = R.ptr
                R.ptr = 0
                cfD_n = R.alloc("cfD_n", [128, 4 * 31, 128], BF16)
                R.ptr = o_keep
                assert o_xnf == 0
                self.build_diag(l + 1, cfD_n, None, identf, pvc, do_cf=True, do_rg=False)
                self.cf_prebuilt.add(l + 1)
                self.cf_handle[l + 1] = cfD_n
            if l == L - 1 and g == 2:
                o_keep = R.ptr
                R.ptr = o_xnf
                ys = [R.alloc("f_y%d" % i, [128, KC, NC_], F32) for i in range(2)]
                R.ptr = o_keep
                yv = env['yv']
                for ti, (c0, N) in enumerate(tiles):
                    for oc in range(KC):
                        bk = self.bank()
                        self.mm(bk[:, 0:N], [(wd[kcl // 4][:, kcl % 4, oc * 128:(oc + 1) * 128],
                                              hb[:, kcl, c0:c0 + N]) for kcl in range(nch)])
                        self.tt(X[:, oc, c0:c0 + N], X[:, oc, c0:c0 + N], bk[:, 0:N], ADD)
                    y = ys[ti % 2]
                    self.norm(lambda kc: X[:, kc, c0:c0 + N], lambda kc: pvc(PV_GFIN + kc),
                              lambda kc: y[:, kc, 0:N], N, sq, rstd)
                    trk.dma('sp', yv[:, :, c0:c0 + N], y[:, :, 0:N], 'st_y%d' % (ti % 2))
                self.final_done = True
            else:
              for oc in range(KC):
                for (c0, N) in tiles:
                    bk = self.bank()
                    self.mm(bk[:, 0:N], [(wd[kcl // 4][:, kcl % 4, oc * 128:(oc + 1) * 128], hb[:, kcl, c0:c0 + N])
                                         for kcl in range(nch)])
                    self.tt(X[:, oc, c0:c0 + N], X[:, oc, c0:c0 + N], bk[:, 0:N], ADD)
            for hbk in range(2):
                self.wrel(bidx[(l, 'down', g, hbk)])

        self.ck('C%d' % l)

    def final_norm(self, env):
        trk = self.trk
        R = env['R']; X = env['X']; pv = env['pv']; yv = env['yv']
        R.reset()
        NF = 512
        sqs = [R.alloc("f_sq%d" % i, [128, KC, NF], BF16) for i in range(2)]
        rstds = [R.alloc("f_rstd%d" % i, [128, NF], F32) for i in range(2)]
        ys = [R.alloc("f_y%d" % i, [128, KC, NF], F32) for i in range(2)]
        for ti, c0 in enumerate(range(0, T, NF)):
            N = min(NF, T - c0)
            y = ys[ti % 2]
            self.norm(lambda kc: X[:, kc, c0:c0 + N], lambda kc: pv[:, PV_GFIN + kc:PV_GFIN + kc + 1],
                      lambda kc: y[:, kc, 0:N], N, sqs[ti % 2], rstds[ti % 2])
            trk.dma('sp', yv[:, :, c0:c0 + N], y[:, :, 0:N], 'st_y%d' % (ti % 2))


_CACHE = {}


def _get_nc():
    if 'nc' not in _CACHE:
        _CACHE['nc'] = Builder().build()
    return _CACHE['nc']


def _pack_pv(inp):
    pv = np.zeros((128, PV_N), np.float32)

    def feat(v):
        return np.ascontiguousarray(v.reshape(L, KC, 128).transpose(2, 0, 1)).reshape(128, L * KC)

    def chan(v):
        return np.ascontiguousarray(v.reshape(L, 4, 128).transpose(2, 0, 1)).reshape(128, L * 4)

    pv[:, PV_GMIX:PV_GMIX + 16] = feat(inp['norm_mix_g'])
    pv[:, PV_GATT:PV_GATT + 16] = feat(inp['norm_attn_g'])
    pv[:, PV_GFFN:PV_GFFN + 16] = feat(inp['norm_ffn_g'])
    pv[:, PV_GMEM:PV_GMEM + 16] = feat(inp['norm_mem_g'])
    pv[:, PV_GFIN:PV_GFIN + 8] = inp['norm_final_g'].reshape(KC, 128).T
    rgw = inp['rg_conv_w'].reshape(L, 4, 4, 128).transpose(3, 0, 2, 1)
    pv[:, PV_RGW:PV_RGW + 32] = np.ascontiguousarray(rgw).reshape(128, 32)
    pv[:, PV_RGB:PV_RGB + 8] = chan(inp['rg_conv_b'])
    pv[:, PV_BA:PV_BA + 8] = chan(inp['rg_ba'])
    pv[:, PV_BX:PV_BX + 8] = chan(inp['rg_bx'])
    pv[:, PV_LAM:PV_LAM + 8] = chan(inp['rg_lambda'])
    cfw = inp['cf_conv_w'].reshape(L, 31, 4, 128).transpose(3, 0, 2, 1)
    pv[:, PV_CFW:PV_CFW + 248] = np.ascontiguousarray(cfw).reshape(128, 248)
    pv[:, PV_CFB:PV_CFB + 8] = chan(inp['cf_conv_b'])
    pv[:, PV_LNG:PV_LNG + 8] = chan(inp['cf_ln_g'])
    pv[:, PV_LNB:PV_LNB + 8] = chan(inp['cf_ln_b'])
    return pv


def _prep(inp):
    inp = {k: np.asarray(v) for k, v in inp.items()}
    f32 = np.float32
    pv = _pack_pv(inp)
    shared = {
        'pv': pv,
        'rg_wa': np.ascontiguousarray(inp['rg_wa'], f32), 'rg_wx': np.ascontiguousarray(inp['rg_wx'], f32),
        'w_in': np.ascontiguousarray(inp['w_in'], f32), 'w_out': np.ascontiguousarray(inp['w_out'], f32),
        'w_q': np.ascontiguousarray(inp['w_q'], f32), 'w_k': np.ascontiguousarray(inp['w_k'], f32),
        'w_v': np.ascontiguousarray(inp['w_v'], f32), 'w_o': np.ascontiguousarray(inp['w_o'], f32),
        'w_gate': np.ascontiguousarray(inp['w_gate'], f32), 'w_up': np.ascontiguousarray(inp['w_up'], f32),
        'w_down': np.ascontiguousarray(inp['w_down'], f32),
    }
    in_maps = []
    for c in range(NCORES):
        sl = slice(16 * c, 16 * (c + 1))
        xs = inp['x_sample'][sl].reshape(NSM, D)
        xT = np.ascontiguousarray(np.concatenate([inp['x_prompt'][c].T, xs.T], axis=1), f32)
        memT = np.ascontiguousarray(inp['mem_prompt'][c].T, f32)
        KcT = np.ascontiguousarray(inp['cache_mem_k'][:, sl].reshape(L, 16, 256, D).transpose(0, 1, 3, 2), f32)
        Vc = np.ascontiguousarray(inp['cache_mem_v'][:, sl].reshape(L, 16, 256, D), f32)
        h0 = np.ascontiguousarray(inp['state_rglru_h'][:, sl].reshape(L, 16, 4, 128).transpose(0, 3, 2, 1), f32)
        rgs = np.ascontiguousarray(
            inp['state_rglru_conv'][:, sl].reshape(L, 16, 3, 4, 128).transpose(0, 4, 3, 1, 2), f32)
        cfs = np.ascontiguousarray(
            inp['state_conf_conv'][:, sl].reshape(L, 16, 30, 4, 128).transpose(0, 4, 3, 1, 2), f32)
        m = dict(shared)
        m.update({'xT': xT, 'memT': memT, 'KcT': KcT, 'Vc': Vc, 'h0': h0, 'rgs': rgs, 'cfs': cfs})
        in_maps.append(m)
    return in_maps


def kernel(**inp):
    nc = _get_nc()
    in_maps = _prep(inp)
    res = run_bass_kernel_spmd(nc, in_maps, core_ids=list(range(NCORES)))
    return _post(res.results)


def _post(rs):
    f32 = np.float32
    B = NCORES
    y_prompt = np.empty((B, NPR, D), f32)
    y_sample = np.empty((128, 4, D), f32)
    p_h = np.empty((L, B, 512), f32)
    p_rg = np.empty((L, B, 3, 512), f32)
    p_cf = np.empty((L, B, 30, 512), f32)
    p_mk = np.empty((L, B, 256, 4, 256), f32)
    p_mv = np.empty((L, B, 256, 4, 256), f32)
    s_h = np.empty((L, 128, 512), f32)
    s_rg = np.empty((L, 128, 3, 512), f32)
    s_cf = np.empty((L, 128, 30, 512), f32)
    for c in range(NCORES):
        r = rs[c]
        sl = slice(16 * c, 16 * (c + 1))
        yT = r['yT']
        y_prompt[c] = yT[:, :NPR].T
        y_sample[sl] = yT[:, NPR:].T.reshape(16, 4, D)
        p_h[:, c] = r['o_ph'].transpose(0, 2, 1).reshape(L, 512)
        p_rg[:, c] = r['o_prg'].transpose(0, 3, 2, 1).reshape(L, 3, 512)
        p_cf[:, c] = r['o_pcf'].transpose(0, 3, 2, 1).reshape(L, 30, 512)
        p_mk[:, c] = r['o_pk'].transpose(0, 3, 2, 1).reshape(L, 256, 4, 256)
        p_mv[:, c] = r['o_pv'].reshape(L, 256, 4, 256)
        s_h[:, sl] = r['o_sh'].transpose(0, 3, 2, 1).reshape(L, 16, 512)
        s_rg[:, sl] = r['o_srg'].transpose(0, 3, 4, 2, 1).reshape(L, 16, 3, 512)
        scf = np.concatenate([r['o_scfh'], r['o_scfn']], axis=4)
        s_cf[:, sl] = scf.transpose(0, 3, 4, 2, 1).reshape(L, 16, 30, 512)
    return (y_prompt, y_sample, p_h, p_rg, p_cf, p_mk, p_mv, s_h, s_rg, s_cf)
```

```python
import itertools
import numpy as np
import concourse.bass as bass
import concourse.mybir as mybir
from concourse.bass_utils import run_bass_kernel_spmd

F32 = mybir.dt.float32
BF16 = mybir.dt.bfloat16
U8 = mybir.dt.uint8
AF = mybir.ActivationFunctionType
ALU = mybir.AluOpType

NCORES = 8
L = 2
D = 1024
KC = 8
NPR = 2048
NSM = 64
T = NPR + NSM
DFF = 2816
EPS = 1e-6
GRAN = 128
PSUM_BASE = 1 << 24
RING_SLOTS = 6
SLOT_BYTES = 8192

PV_GMIX, PV_GATT, PV_GFFN, PV_GMEM, PV_GFIN = 0, 16, 32, 48, 64
PV_RGW, PV_RGB, PV_BA, PV_BX, PV_LAM = 72, 104, 112, 120, 128
PV_CFW, PV_CFB, PV_LNG, PV_LNB = 136, 384, 392, 400
PV_N = 408


def _esz(dt):
    return 4 if dt == F32 else (2 if dt == BF16 else 1)


class Trk:
    def __init__(self, nc):
        self.nc = nc
        self.E = {'pe': nc.tensor, 'act': nc.scalar, 'dve': nc.vector, 'pool': nc.gpsimd, 'sp': nc.sync}
        self.sems = {}
        self.val = {}
        self.waited = {e: {} for e in self.E}
        self.lastw = {}
        self.rd = {}
        self.base = {}

    def reg(self, handle, addr):
        self.base[handle.name] = addr

    def sem(self, key):
        if key not in self.sems:
            self.sems[key] = self.nc.alloc_semaphore('s_' + key)
            self.val[key] = 0
        return self.sems[key]

    def grans(self, aps):
        out = set()
        for ap in aps:
            nm = ap.tensor.name
            if nm not in self.base:
                continue
            base = self.base[nm]
            es = _esz(ap.dtype)
            pat = ap.ap
            rowstride = pat[0][0]
            off = int(ap.offset)
            if rowstride > 0:
                off = off % rowstride
            free = pat[1:]
            if len(free) == 0:
                rngs = [(off, off + 1)]
            else:
                ist, icnt = free[-1]
                ilen = (icnt - 1) * abs(ist) + 1
                outer = free[:-1]
                rngs = []
                for idx in itertools.product(*[range(c) for (_, c) in outer]):
                    st = off + sum(i * s for i, (s, _) in zip(idx, outer))
                    rngs.append((st, st + ilen))
            for lo, hi in rngs:
                blo = base + lo * es
                bhi = base + hi * es
                for g in range(blo // GRAN, (bhi - 1) // GRAN + 1):
                    out.add(g)
        return out

    def _deps(self, rg, wg):
        need = {}
        for g in rg:
            w = self.lastw.get(g)
            if w is not None and need.get(w[0], 0) < w[1]:
                need[w[0]] = w[1]
        for g in wg:
            w = self.lastw.get(g)
            if w is not None and need.get(w[0], 0) < w[1]:
                need[w[0]] = w[1]
            r = self.rd.get(g)
            if r:
                for k, v in r.items():
                    if need.get(k, 0) < v:
                        need[k] = v
        return need

    def _commit(self, rg, wg, key, val):
        for g in rg:
            d = self.rd.get(g)
            if d is None:
                self.rd[g] = {key: val}
            else:
                d[key] = val
        for g in wg:
            self.lastw[g] = (key, val)
            self.rd[g] = None

    def _wait(self, e, need):
        eng = self.E[e]
        wd = self.waited[e]
        for k, v in need.items():
            if e == 'pe' and k == 'pe':
                continue
            if k not in self.E:
                v = self.val[k]
            if wd.get(k, 0) < v:
                eng.wait_ge(self.sems[k], v)
                wd[k] = v

    def op(self, e, reads, writes, fn):
        rg = self.grans(reads)
        wg = self.grans(writes)
        self._wait(e, self._deps(rg, wg))
        ins = fn()
        s = self.sem(e)
        self.val[e] += 1
        ins.then_inc(s, 1)
        self._commit(rg, wg, e, self.val[e])

    def dma(self, q, out, in_, key):
        reads = [in_] if in_.tensor.name in self.base else []
        writes = [out] if out.tensor.name in self.base else []
        rg = self.grans(reads)
        wg = self.grans(writes)
        self._wait(q, self._deps(rg, wg))
        s = self.sem(key)
        ins = self.E[q].dma_start(out=out, in_=in_)
        self.val[key] += 16
        ins.then_inc(s, 16)
        self._commit(rg, wg, key, self.val[key])


class Region:
    def __init__(self, nc, trk, base, size):
        self.nc, self.trk, self.base, self.size = nc, trk, base, size
        self.ptr = 0

    def reset(self):
        self.ptr = 0

    def alloc(self, name, shape, dt):
        n = 1
        for s in shape[1:]:
            n *= s
        nbytes = n * _esz(dt)
        nbytes = (nbytes + GRAN - 1) // GRAN * GRAN
        assert self.ptr + nbytes <= self.size, (name, self.ptr, nbytes, self.size)
        addr = self.base + self.ptr
        t = self.nc.alloc_sbuf_tensor_at(name, list(shape), dt, offset=addr)
        self.trk.reg(t, addr)
        self.ptr += nbytes
        return t


class _Stop(Exception):
    pass


_STOP = None


class Builder:
    def ck(self, name):
        if _STOP == name:
            raise _Stop()

    def __init__(self):
        self.nc = bass.Bass("TRN2", target_bir_lowering=False)
        self.trk = Trk(self.nc)
        self.pb = 0
        self.cf_prebuilt = set()
        self.cf_handle = {}

    def bank(self):
        self.pb = (self.pb + 1) % 8
        return self.ps[self.pb]

    def mm(self, out, pairs, extra_reads=()):
        reads = []
        for a, b in pairs:
            reads.append(a)
            reads.append(b)
        n = len(pairs)

        def fn():
            ins = None
            for i, (a, b) in enumerate(pairs):
                ins = self.nc.tensor.matmul(out, lhsT=a, rhs=b, start=(i == 0), stop=(i == n - 1))
            return ins
        self.trk.op('pe', reads, [out], fn)

    def mm_groups(self, groups):
        reads, writes = [], []
        for out, pairs in groups:
            writes.append(out)
            for a, b in pairs:
                reads.append(a)
                reads.append(b)

        def fn():
            ins = None
            for out, pairs in groups:
                n = len(pairs)
                for i, (a, b) in enumerate(pairs):
                    ins = self.nc.tensor.matmul(out, lhsT=a, rhs=b, start=(i == 0), stop=(i == n - 1))
            return ins
        self.trk.op('pe', reads, writes, fn)

    def act(self, out, in_, func, bias=None, scale=None):
        reads = [in_]
        kw = {}
        if bias is not None:
            kw['bias'] = bias
            if not isinstance(bias, (int, float)):
                reads.append(bias)
        if scale is not None:
            kw['scale'] = scale
            if not isinstance(scale, (int, float)):
                reads.append(scale)
        self.trk.op('act', reads, [out], lambda: self.nc.scalar.activation(out=out, in_=in_, func=func, **kw))

    def tt(self, out, in0, in1, op, eng='dve'):
        e = self.trk.E[eng]
        self.trk.op(eng, [in0, in1], [out], lambda: e.tensor_tensor(out=out, in0=in0, in1=in1, op=op))

    def ts(self, out, in0, s1, s2, op0, op1=None, eng='dve'):
        e = self.trk.E[eng]
        reads = [in0]
        if not isinstance(s1, (int, float)):
            reads.append(s1)
        if s2 is not None and not isinstance(s2, (int, float)):
            reads.append(s2)
        if op1 is None:
            self.trk.op(eng, reads, [out], lambda: e.tensor_scalar(out=out, in0=in0, scalar1=s1, scalar2=None, op0=op0))
        else:
            self.trk.op(eng, reads, [out], lambda: e.tensor_scalar(out=out, in0=in0, scalar1=s1, scalar2=s2, op0=op0, op1=op1))

    def stt(self, out, in0, scalar, in1, op0, op1):
        reads = [in0, in1]
        if not isinstance(scalar, (int, float)):
            reads.append(scalar)
        self.trk.op('dve', reads, [out], lambda: self.nc.vector.scalar_tensor_tensor(
            out=out, in0=in0, scalar=scalar, in1=in1, op0=op0, op1=op1))

    def scan(self, out, d0, d1, initial):
        reads = [d0, d1]
        if not isinstance(initial, (int, float)):
            reads.append(initial)
        self.trk.op('dve', reads, [out], lambda: self.nc.vector.tensor_tensor_scan(
            out=out, data0=d0, data1=d1, initial=initial, op0=ALU.mult, op1=ALU.add))

    def recip(self, out, in_):
        self.trk.op('dve', [in_], [out], lambda: self.nc.vector.reciprocal(out=out, in_=in_))

    def copy(self, out, in_, eng='dve'):
        e = self.trk.E[eng]
        self.trk.op(eng, [in_], [out], lambda: e.tensor_copy(out=out, in_=in_))

    def memset(self, out, v, eng='dve'):
        e = self.trk.E[eng]
        self.trk.op(eng, [], [out], lambda: e.memset(out, v))

    def ring_plan(self, blocks):
        self.blocks = blocks
        self.blk_view = [None] * len(blocks)
        self.blk_emitted = 0
        self.blk_released = [False] * len(blocks)

    def _ring_emit(self, i):
        ap = self.blocks[i]
        a, b = ap.shape[1], ap.shape[2]
        slot = i % RING_SLOTS
        v = self.ring[slot][:, 0:a * b].rearrange("p (a b) -> p a b", a=a)
        self.trk.dma('pool', v, ap, 'ring%d' % slot)
        self.blk_view[i] = v

    def _ring_pump(self, upto=None):
        while self.blk_emitted < len(self.blocks):
            j = self.blk_emitted
            if upto is not None and j <= upto:
                pass
            elif j >= RING_SLOTS and not self.blk_released[j - RING_SLOTS]:
                break
            if j >= RING_SLOTS:
                assert self.blk_released[j - RING_SLOTS], "ring overflow: block %d" % j
            self._ring_emit(j)
            self.blk_emitted += 1

    def wget(self, i):
        self._ring_pump(upto=i)
        return self.blk_view[i]

    def wrel(self, i):
        self.blk_released[i] = True
        self._ring_pump()

    def norm(self, xin, gvec, out_fn, N, sq, rstd):
        for kc in range(KC):
            self.act(sq[:, kc, 0:N], xin(kc), AF.Square)
        bk = self.bank()
        self.mm(bk[:, 0:N], [(self.c1024[:], sq[:, kc, 0:N]) for kc in range(KC)])
        self.act(rstd[:, 0:N], bk[:, 0:N], AF.Ln, bias=EPS)
        self.act(rstd[:, 0:N], rstd[:, 0:N], AF.Exp, scale=-0.5)
        for kc in range(KC):
            self.stt(out_fn(kc), xin(kc), gvec(kc), rstd[:, 0:N], ALU.mult, ALU.mult)

    def build(self):
        nc = self.nc
        trk = self.trk
        dr = {}

        def din(name, shape):
            dr[name] = nc.dram_tensor(name, list(shape), F32, kind="ExternalInput").ap()
            return dr[name]

        def dout(name, shape):
            dr[name] = nc.dram_tensor(name, list(shape), F32, kind="ExternalOutput").ap()
            return dr[name]

        xT = din("xT", [D, T])
        memT = din("memT", [D, 256])
        KcT = din("KcT", [L, 16, D, 256])
        Vc = din("Vc", [L, 16, 256, D])
        h0 = din("h0", [L, 128, 4, 16])
        rgs = din("rgs", [L, 128, 4, 16, 3])
        cfs = din("cfs", [L, 128, 4, 16, 30])
        pvd = din("pv", [128, PV_N])
        rg_wa = din("rg_wa", [L, 8, 64, 64])
        rg_wx = din("rg_wx", [L, 8, 64, 64])
        w_in = din("w_in", [L, D, 2048])
        w_out = din("w_out", [L, D, D])
        w_q = din("w_q", [L, D, D])
        w_k = din("w_k", [L, D, D])
        w_v = din("w_v", [L, D, D])
        w_o = din("w_o", [L, D, D])
        w_gate = din("w_gate", [L, D, DFF])
        w_up = din("w_up", [L, D, DFF])
        w_down = din("w_down", [L, DFF, D])

        yT = dout("yT", [D, T])
        o_ph = dout("o_ph", [L, 128, 4])
        o_prg = dout("o_prg", [L, 128, 4, 3])
        o_pcf = dout("o_pcf", [L, 128, 4, 30])
        o_pk = dout("o_pk", [L, 128, 8, 256])
        o_pv = dout("o_pv", [L, 256, D])
        o_sh = dout("o_sh", [L, 128, 4, 16])
        o_srg = dout("o_srg", [L, 128, 4, 16, 3])
        o_scfh = dout("o_scfh", [L, 128, 4, 16, 26])
        o_scfn = dout("o_scfn", [L, 128, 4, 16, 4])

        ARENA = 212800
        arena = nc.alloc_sbuf_tensor("arena", [128, ARENA], U8)
        abase = nc.lookup_mloc(arena).addr
        P = Region(nc, trk, abase, ARENA)
        X = P.alloc("X", [128, KC, T], F32)
        identf = P.alloc("identf", [128, 128], F32)
        ident = P.alloc("ident", [128, 128], BF16)
        self.c1024 = P.alloc("c1024", [128, 128], BF16)
        c512 = P.alloc("c512", [128, 128], BF16)
        ones = P.alloc("ones", [128, 128], BF16)
        pv = P.alloc("pvs", [128, PV_N], F32)
        cl = P.alloc("cl", [128, L * 4], F32)
        cl2 = P.alloc("cl2", [128, L * 4], F32)
        wabd = P.alloc("wabd", [128, L * 2 * 4, 128], BF16)
        hstate = P.alloc("hstate", [128, 4], F32)
        self.ring = [P.alloc("ring%d" % i, [128, SLOT_BYTES // 2], BF16) for i in range(RING_SLOTS)]
        rbase = abase + P.ptr
        R = Region(nc, trk, rbase, ARENA - P.ptr)
        self.ps = []
        for i in range(8):
            t = nc.alloc_psum_tensor("ps%d" % i, [128, 512], F32)
            trk.reg(t, PSUM_BASE + i * 2048)
            self.ps.append(t)

        blocks = []
        bidx = {}

        def wv(w, l):
            return w[l].rearrange("(kc p) n -> p kc n", p=128)

        for l in range(L):
            for c in range(4):
                bidx[(l, 'in', c)] = len(blocks)
                blocks.append(wv(w_in, l)[:, :, c * 512:(c + 1) * 512])
            for c in range(2):
                bidx[(l, 'out', c)] = len(blocks)
                blocks.append(wv(w_out, l)[:, :, c * 512:(c + 1) * 512])
            for nm, w in (('k', w_k), ('v', w_v), ('q', w_q), ('o', w_o)):
                for c in range(2):
                    bidx[(l, nm, c)] = len(blocks)
                    blocks.append(wv(w, l)[:, :, c * 512:(c + 1) * 512])
            for g in range(3):
                ncols = 1024 if g < 2 else 768
                c0 = g * 1024
                for hb in range(2):
                    lo = c0 + hb * 512
                    hi = min(c0 + ncols, lo + 512)
                    bidx[(l, 'gate', g, hb)] = len(blocks)
                    blocks.append(wv(w_gate, l)[:, :, lo:hi])
                    bidx[(l, 'up', g, hb)] = len(blocks)
                    blocks.append(wv(w_up, l)[:, :, lo:hi])
                nch = ncols // 128
                for hb in range(2):
                    k0 = hb * 4
                    k1 = min(nch, k0 + 4)
                    bidx[(l, 'down', g, hb)] = len(blocks)
                    blocks.append(w_down[l, c0 + k0 * 128:c0 + k1 * 128, :].rearrange("(kc p) n -> p kc n", p=128))
        self.ring_plan(blocks)

        A = self.act
        xv = xT.rearrange("(kc p) n -> p kc n", p=128)
        yv = yT.rearrange("(kc p) n -> p kc n", p=128)

        def pvc(off, n=1):
            return pv[:, off:off + n]

        with nc.Block():
            trk.dma('sp', pv[:], pvd[:, :], 'ld_pv')
            xtiles = [(ti, c0, min(512, T - c0)) for ti, c0 in enumerate(range(0, T, 512))]
            ti, c0, n = xtiles[0]
            trk.dma('sp', X[:, :, c0:c0 + n], xv[:, :, c0:c0 + n], 'ld_x%d' % ti)
            self.memset(identf[:], 0.0, eng='dve')
            trk.op('pool', [identf[:]], [identf[:]], lambda: nc.gpsimd.affine_select(
                out=identf[:], in_=identf[:], pattern=[[-1, 128]], compare_op=ALU.not_equal, fill=1.0,
                base=0, channel_multiplier=1))
            self.copy(ident[:], identf[:])
            self.memset(self.c1024[:], 1.0 / 1024.0)
            self.memset(c512[:], 1.0 / 512.0)
            self.memset(ones[:], 1.0)
            R.reset()
            R.ptr = 71936
            wst = R.alloc("wst", [128, L * 2 * 4, 128], F32)
            self.memset(wst[:], 0.0)
            for l in range(L):
                for gi, wsrc in enumerate((rg_wa, rg_wx)):
                    for hh in range(8):
                        j, e = hh // 2, hh % 2
                        trk.dma('sp', wst[e * 64:(e + 1) * 64, (l * 2 + gi) * 4 + j, e * 64:(e + 1) * 64],
                                wsrc[l, hh, :, :], 'ld_w%d' % ((l * 2 + gi) % 2))
            self.deferred = lambda: self.copy(wabd[:], wst[:])
            for ti, c0, n in xtiles[1:]:
                trk.dma('sp', X[:, :, c0:c0 + n], xv[:, :, c0:c0 + n], 'ld_x%d' % ti)
            for l in range(L):
                for j in range(4):
                    trk.dma('sp', o_scfh[l, :, j], cfs[l, :, j, :, 4:30], 'st_h')
            spt = R.alloc("spt", [128, L * 4], F32)
            A(spt[:], pvc(PV_LAM, 8), AF.Exp, scale=-1.0)
            A(spt[:], spt[:], AF.Ln, bias=1.0)
            self.ts(cl[:], spt[:], -8.0, None, ALU.mult)
            self.ts(cl2[:], spt[:], -16.0, None, ALU.mult)

            try:
                self.ck('P')
                for l in range(L):
                    self.layer(l, locals())
                if not getattr(self, 'final_done', False):
                    self.final_norm(locals())
            except _Stop:
                pass
            for k, v in trk.val.items():
                if k not in trk.E and v > 0:
                    nc.sync.wait_ge(trk.sems[k], v)
        return nc

    def build_diag(self, l, cfD, rgD, identf, pvc, do_cf=True, do_rg=True):
        MUL = ALU.mult
        for j in range(4):
            if do_rg:
                for k in range(4):
                    self.ts(rgD[:, j * 4 + k, :], identf[:], pvc(PV_RGW + (l * 4 + j) * 4 + k), None, MUL)
            if do_cf:
                for k in range(31):
                    if k % 3 == 2:
                        self.act(cfD[:, j * 31 + k, :], identf[:], AF.Copy,
                                 scale=pvc(PV_CFW + (l * 4 + j) * 31 + k))
                    else:
                        self.ts(cfD[:, j * 31 + k, :], identf[:], pvc(PV_CFW + (l * 4 + j) * 31 + k), None, MUL)

    def layer(self, l, env):
        nc, trk = self.nc, self.trk
        R = env['R']; X = env['X']; pv = env['pv']; bidx = env['bidx']
        ident, identf, c512, ones = env['ident'], env['identf'], env['c512'], env['ones']
        cl, cl2, wabd, hstate = env['cl'], env['cl2'], env['wabd'], env['hstate']
        dr_ = env['dr']
        A = self.act
        MUL, ADD, SUB = ALU.mult, ALU.add, ALU.subtract

        def pvc(off, n=1):
            return pv[:, off:off + n]

        R.reset()
        cfD = R.alloc("cfD", [128, 4 * 31, 128], BF16)
        if l in self.cf_handle:
            cfD = self.cf_handle[l]
        rgD = R.alloc("rgD", [128, 4 * 4, 128], BF16)
        NA = 256
        ycat = R.alloc("a_ycat", [128, KC, NA], BF16)
        sq = R.alloc("a_sq", [128, KC, NA], BF16)
        rstd = R.alloc("a_rstd", [128, NA], F32)
        xn = R.alloc("a_xn", [128, KC, NA], BF16)
        o_x = R.ptr
        xrb0 = R.alloc("a_xrb0", [128, 4, 3 + NA], BF16)
        cb0 = R.alloc("a_cb0", [128, 4, 30 + NA], BF16)
        o_end = R.ptr
        R.ptr = o_x
        csb = R.alloc("a_csb", [128, 4, 16, 34], BF16)
        assert R.ptr <= o_end
        R.ptr = o_end
        xrb1 = R.alloc("a_xrb1", [128, 4, 3 + NA], BF16)
        cb1 = R.alloc("a_cb1", [128, 4, 30 + NA], BF16)
        xrb2 = [xrb0, xrb1]
        cb2 = [cb0, cb1]
        gg2 = [R.alloc("a_gg%d" % i, [128, 4, NA], BF16) for i in range(2)]
        sg = R.alloc("a_sg", [128, NA], F32)
        xc2 = [R.alloc("a_xc%d" % i, [128, NA], F32) for i in range(2)]
        xcb = R.alloc("a_xcb", [128, NA], BF16)
        r_ = R.alloc("a_r", [128, NA], F32)
        i_ = R.alloc("a_i", [128, NA], F32)
        a_ = R.alloc("a_a", [128, NA], F32)
        m_ = R.alloc("a_m", [128, NA], F32)
        b_ = R.alloc("a_b", [128, NA], F32)
        h_ = R.alloc("a_h", [128, NA], F32)
        ccf = R.alloc("a_ccf", [128, 4, NA], F32)
        ccb = R.alloc("a_ccb", [128, 4, NA], BF16)
        sqb = R.alloc("a_sqb", [128, 4, NA], BF16)
        mean = R.alloc("a_mean", [128, NA], F32)
        var = R.alloc("a_var", [128, NA], F32)
        rstc = R.alloc("a_rstc", [128, NA], F32)
        o_dd = R.ptr
        dd = R.alloc("a_dd", [128, NA], F32)
        dd2 = [dd, sg]
        prg = R.alloc("a_prg", [128, 4, 3], F32)
        pcf = R.alloc("a_pcf", [128, 4, 30], F32)
        xrs = R.alloc("a_xrs", [128, 4, 16, 7], BF16)
        o_keep = R.ptr
        R.ptr = o_dd
        xrsf = R.alloc("a_xrsf", [128, 4, 16, 3], F32)
        R.ptr = o_keep
        srg = R.alloc("a_srg", [128, 4, 16, 3], F32)
        cs4 = R.alloc("a_cs4", [128, 4, 16, 4], F32)
        h0s = R.alloc("a_h0s", [128, 4, 16], F32)
        shl = R.alloc("a_shl", [128, 4, 16], F32)
        tmp16 = R.alloc("a_tmp16", [128, 16], F32)
        nb2 = R.alloc("a_nb2", [128, 8], F32)

        self.build_diag(l, cfD, rgD, identf, pvc, do_cf=(l not in self.cf_prebuilt))
        if getattr(self, 'deferred', None) is not None:
            self.deferred()
            self.deferred = None
        self.ts(nb2[:, 0:4], pvc(PV_BA + l * 4, 4), -1.0, None, MUL)
        self.ts(nb2[:, 4:8], pvc(PV_BX + l * 4, 4), -1.0, None, MUL)
        self.memset(xrb0[:, :, 0:3], 0.0)
        self.memset(cb0[:, :, 0:30], 0.0)
        self.memset(hstate[:], 0.0)

        win = [self.wget(bidx[(l, 'in', c)]) for c in range(4)]
        wout = [self.wget(bidx[(l, 'out', c)]) for c in range(2)]
        tiles = [(c0, NA, False) for c0 in range(0, NPR, NA)] + [(NPR, NSM, True)]
        NT = len(tiles)

        def load_sample_state():
            trk.dma('sp', xrsf[:], dr_['rgs'][l], 'ld_st0')
            trk.dma('sp', h0s[:], dr_['h0'][l], 'ld_st1')
            for j in range(4):
                trk.dma('pool', csb[:, j, :, 0:30], dr_['cfs'][l, :, j], 'ld_st%d' % (2 + j))
            self.copy(xrs[:, :, :, 0:3], xrsf[:])

        def v3(ap):
            return ap.rearrange("p (b t) -> p b t", t=4)

        def front(ti):
            c0, N, smp = tiles[ti]
            sset = ti % 2
            xrb, cb, gg = xrb2[sset], cb2[sset], gg2[sset]
            last_p = (not smp) and (c0 + N == NPR)
            if smp:
                load_sample_state()
            elif ti >= 1:
                self.copy(xrb[:, :, 0:3], xrb2[1 - sset][:, :, NA:NA + 3])
                self.copy(cb[:, :, 0:30], cb2[1 - sset][:, :, NA:NA + 30])
            for kc in range(KC):
                A(sq[:, kc, 0:N], X[:, kc, c0:c0 + N], AF.Square)
            bkn = self.bank()
            self.mm(bkn[:, 0:N], [(self.c1024[:], sq[:, kc, 0:N]) for kc in range(KC)])
            A(rstd[:, 0:N], bkn[:, 0:N], AF.Ln, bias=EPS)
            A(rstd[:, 0:N], rstd[:, 0:N], AF.Exp, scale=-0.5)
            yield
            for kc in range(KC):
                self.stt(xn[:, kc, 0:N], X[:, kc, c0:c0 + N], pvc(PV_GMIX + l * 8 + kc), rstd[:, 0:N], MUL, MUL)
            yield

            def proj(blk, j):
                bk = self.bank()
                self.mm(bk[:, 0:N], [(win[blk][:, kc, j * 128:(j + 1) * 128], xn[:, kc, 0:N]) for kc in range(KC)])
                return bk
            yield
            for j in range(4):
                bk = proj(1, j)
                A(gg[:, j, 0:N], bk[:, 0:N], AF.Gelu_apprx_tanh)
            yield
            yield
            for j in range(4):
                bk = proj(0, j)
                if smp:
                    A(xrs[:, j, :, 3:7], v3(bk[:, 0:N]), AF.Copy)
                    A(srg[:, j, :, :], v3(bk[:, 0:N])[:, :, 1:4], AF.Copy)
                else:
                    A(xrb[:, j, 3:3 + N], bk[:, 0:N], AF.Copy)
                    if last_p:
                        A(prg[:, j, :], bk[:, N - 3:N], AF.Copy)
            yield
            for half in range(1):
                for j in range(4):
                    bv = proj(2, j)
                    bg = proj(3, j)
                    A(sg[:, 0:N], bg[:, 0:N], AF.Sigmoid)
                    if smp:
                        self.tt(cs4[:, j, :, :], v3(bv[:, 0:N]), v3(sg[:, 0:N]), MUL)
                        self.copy(csb[:, j, :, 30:34], cs4[:, j, :, :])
                    else:
                        self.tt(cb[:, j, 30:30 + N], bv[:, 0:N], sg[:, 0:N], MUL)
                        if last_p:
                            self.tt(pcf[:, j, :], bv[:, N - 30:N], sg[:, N - 30:N], MUL)
                yield

        def back(ti):
            c0, N, smp = tiles[ti]
            sset = ti % 2
            xrb, cb, gg = xrb2[sset], cb2[sset], gg2[sset]

            def tail(j):
                xc = xc2[j % 2]
                self.tt(b_[:, 0:N], i_[:, 0:N], xc[:, 0:N], MUL)
                self.tt(b_[:, 0:N], b_[:, 0:N], m_[:, 0:N], MUL)
                if smp:
                    av = v3(a_[:, 0:N])
                    bvw = v3(b_[:, 0:N])
                    self.tt(tmp16[:], av[:, :, 0], h0s[:, j, :], MUL)
                    self.tt(bvw[:, :, 0], bvw[:, :, 0], tmp16[:], ADD)
                    self.memset(av[:, :, 0], 0.0)
                    self.scan(h_[:, 0:N], a_[:, 0:N], b_[:, 0:N], 0.0)
                    self.copy(shl[:, j, :], v3(h_[:, 0:N])[:, :, 3])
                else:
                    self.scan(h_[:, 0:N], a_[:, 0:N], b_[:, 0:N], hstate[:, j:j + 1])
                    self.copy(hstate[:, j:j + 1], h_[:, N - 1:N])
                self.tt(ycat[:, j, 0:N], h_[:, 0:N], gg[:, j, 0:N], MUL)

            for j in range(4):
                xc = xc2[j % 2]
                bk = self.bank()
                if smp:
                    self.mm(bk[:, 0:N], [(rgD[:, j * 4 + k, :], xrs[:, j, :, k:k + 4]) for k in range(4)])
                else:
                    self.mm(bk[:, 0:N], [(rgD[:, j * 4 + k, :], xrb[:, j, k:k + N]) for k in range(4)])
                bias = pvc(PV_RGB + l * 4 + j)
                self.ts(xcb[:, 0:N], bk[:, 0:N], bias, None, ADD)
                self.ts(xc[:, 0:N], bk[:, 0:N], bias, None, ADD)
                bc = self.bank()
                if smp:
                    self.mm(bc[:, 0:N], [(cfD[:, j * 31 + k, :], csb[:, j, :, k:k + 4]) for k in range(31)])
                else:
                    self.mm(bc[:, 0:N], [(cfD[:, j * 31 + k, :], cb[:, j, k:k + N]) for k in range(31)])
                ba = self.bank()
                self.mm(ba[:, 0:N], [(wabd[:, (l * 2 + 0) * 4 + j, :], xcb[:, 0:N])])
                bx = self.bank()
                self.mm(bx[:, 0:N], [(wabd[:, (l * 2 + 1) * 4 + j, :], xcb[:, 0:N])])
                cbias = pvc(PV_CFB + l * 4 + j)
                self.ts(ccf[:, j, 0:N], bc[:, 0:N], cbias, None, ADD)
                self.ts(ccb[:, j, 0:N], bc[:, 0:N], cbias, None, ADD)
                self.stt(sqb[:, j, 0:N], bc[:, 0:N], cbias, ccf[:, j, 0:N], ADD, MUL)
                if j == 3:
                    bm = self.bank()
                    self.mm(bm[:, 0:N], [(c512[:], ccb[:, jj, 0:N]) for jj in range(4)])
                    bq = self.bank()
                    self.mm(bq[:, 0:N], [(c512[:], sqb[:, jj, 0:N]) for jj in range(4)])
                    A(mean[:, 0:N], bm[:, 0:N], AF.Copy)
                    self.tt(var[:, 0:N], mean[:, 0:N], mean[:, 0:N], MUL)
                    self.tt(var[:, 0:N], bq[:, 0:N], var[:, 0:N], SUB)
                    A(rstc[:, 0:N], var[:, 0:N], AF.Ln, bias=EPS)
                    A(rstc[:, 0:N], rstc[:, 0:N], AF.Exp, scale=-0.5)
                if j >= 1:
                    tail(j - 1)
                A(r_[:, 0:N], ba[:, 0:N], AF.Exp, scale=-1.0, bias=nb2[:, j:j + 1])
                A(i_[:, 0:N], bx[:, 0:N], AF.Exp, scale=-1.0, bias=nb2[:, 4 + j:5 + j])
                A(r_[:, 0:N], r_[:, 0:N], AF.Ln, bias=1.0)
                A(i_[:, 0:N], i_[:, 0:N], AF.Ln, bias=1.0)
                A(r_[:, 0:N], r_[:, 0:N], AF.Exp, scale=-1.0)
                A(i_[:, 0:N], i_[:, 0:N], AF.Exp, scale=-1.0)
                A(a_[:, 0:N], r_[:, 0:N], AF.Exp, scale=cl[:, l * 4 + j:l * 4 + j + 1])
                A(m_[:, 0:N], r_[:, 0:N], AF.Exp, scale=cl2[:, l * 4 + j:l * 4 + j + 1])
                A(m_[:, 0:N], m_[:, 0:N], AF.Ln, scale=-1.0, bias=1.0000001)
                A(m_[:, 0:N], m_[:, 0:N], AF.Exp, scale=0.5)
                yield
            for j in range(4):
                dd = dd2[j % 2]
                self.tt(dd[:, 0:N], ccf[:, j, 0:N], mean[:, 0:N], SUB)
                self.tt(dd[:, 0:N], dd[:, 0:N], rstc[:, 0:N], MUL)
                A(ycat[:, 4 + j, 0:N], dd[:, 0:N], AF.Silu, bias=pvc(PV_LNB + l * 4 + j),
                  scale=pvc(PV_LNG + l * 4 + j))
            yield
            tail(3)
            yield
            wout_proj(ti)
            yield

        def wout_proj(ti):
            c0, N, smp = tiles[ti]
            for oc in range(KC):
                bk = self.bank()
                self.mm(bk[:, 0:N], [(wout[oc // 4][:, kc, (oc % 4) * 128:(oc % 4 + 1) * 128], ycat[:, kc, 0:N])
                                     for kc in range(KC)])
                self.tt(X[:, oc, c0:c0 + N], X[:, oc, c0:c0 + N], bk[:, 0:N], ADD)

        def drive(gens):
            gens = [g for g in gens if g is not None]
            while gens:
                alive = []
                for g in gens:
                    try:
                        next(g)
                        alive.append(g)
                    except StopIteration:
                        pass
                gens = alive

        drive([front(0)])
        for ti in range(NT):
            drive([front(ti + 1) if ti + 1 < NT else None, back(ti)])
            if ti == NT - 2:
                for c in range(4):
                    self.wrel(bidx[(l, 'in', c)])

        trk.dma('sp', dr_['o_ph'][l], hstate[:], 'st_a0')
        trk.dma('sp', dr_['o_prg'][l], prg[:], 'st_a1')
        trk.dma('sp', dr_['o_pcf'][l], pcf[:], 'st_a2')
        trk.dma('sp', dr_['o_sh'][l], shl[:], 'st_a3')
        trk.dma('sp', dr_['o_srg'][l], srg[:], 'st_a4')
        trk.dma('sp', dr_['o_scfn'][l], cs4[:], 'st_a5')
        for c in range(2):
            self.wrel(bidx[(l, 'out', c)])

        self.ck('A%d' % l)
        R.reset()
        NB = 512
        KpT = R.alloc("b_KpT", [128, KC, 256], BF16)
        Vp = R.alloc("b_Vp", [128, 2, D], BF16)
        Ks = [R.alloc("b_Ks%d" % i, [128, KC, 256], BF16) for i in range(2)]
        Vs = [R.alloc("b_Vs%d" % i, [128, 2, D], BF16) for i in range(2)]
        sq = R.alloc("b_sq", [128, KC, NB], BF16)
        rstd = R.alloc("b_rstd", [128, NB], F32)
        xn = R.alloc("b_xn", [128, KC, NB], BF16)
        sq_s = R.alloc("b_sqs", [128, KC, NSM], BF16)
        rstd_s = R.alloc("b_rstds", [128, NSM], F32)
        xn_s = R.alloc("b_xns", [128, KC, NSM], BF16)
        mark = R.ptr
        memf = R.alloc("b_memf", [128, KC, 256], F32)
        memn = R.alloc("b_memn", [128, KC, 256], BF16)
        msq = R.alloc("b_msq", [128, KC, 256], BF16)
        mrs = R.alloc("b_mrs", [128, 256], F32)
        kst = R.alloc("b_kst", [128, KC, 256], F32)
        vst = R.alloc("b_vst", [128, 2, D], F32)
        trk.dma('sp', memf[:], dr_['memT'].rearrange("(kc p) n -> p kc n", p=128), 'ld_mem')
        self.norm(lambda kc: memf[:, kc, :], lambda kc: pvc(PV_GMEM + l * 8 + kc),
                  lambda kc: memn[:, kc, :], 256, msq, mrs)
        self.norm(lambda kc: X[:, kc, NPR:NPR + NSM], lambda kc: pvc(PV_GATT + l * 8 + kc),
                  lambda kc: xn_s[:, kc, 0:NSM], NSM, sq_s, rstd_s)
        self.norm(lambda kc: X[:, kc, 0:NB], lambda kc: pvc(PV_GATT + l * 8 + kc),
                  lambda kc: xn[:, kc, 0:NB], NB, sq, rstd)
        self.ck('Bn%d' % l)
        wk = [self.wget(bidx[(l, 'k', c)]) for c in range(2)]
        self.ck('Bw%d' % l)
        for dc in range(KC):
            bk = self.bank()
            self.mm(bk[:, 0:256], [(wk[dc // 4][:, kc, (dc % 4) * 128:(dc % 4 + 1) * 128], memn[:, kc, :])
                                   for kc in range(KC)])
            A(KpT[:, dc, :], bk[:, 0:256], AF.Copy)
            A(kst[:, dc, :], bk[:, 0:256], AF.Copy)
        self.ck('Bk%d' % l)
        trk.dma('sp', dr_['o_pk'][l], kst[:], 'st_bk')
        for c in range(2):
            self.wrel(bidx[(l, 'k', c)])
        wvv = [self.wget(bidx[(l, 'v', c)]) for c in range(2)]
        for mt in range(2):
            for cbk in range(2):
                bk = self.bank()
                self.mm(bk[:, :], [(memn[:, kc, mt * 128:(mt + 1) * 128], wvv[cbk][:, kc, :]) for kc in range(KC)])
                A(Vp[:, mt, cbk * 512:(cbk + 1) * 512], bk[:, :], AF.Copy)
                A(vst[:, mt, cbk * 512:(cbk + 1) * 512], bk[:, :], AF.Copy)
        trk.dma('sp', dr_['o_pv'][l].rearrange("(j p) d -> p j d", p=128), vst[:], 'st_bv')
        for c in range(2):
            self.wrel(bidx[(l, 'v', c)])
        self.ck('Ba%d' % l)
        R.ptr = mark
        qT2 = [R.alloc("b_qT%d" % i, [128, KC, NB], BF16) for i in range(2)]
        eT2 = [R.alloc("b_eT%d" % i, [128, 2, NB], BF16) for i in range(2)]
        rs2 = [R.alloc("b_rs%d" % i, [128, NB], F32) for i in range(2)]
        oT = R.alloc("b_oT", [128, KC, NB], BF16)
        qTs = R.alloc("b_qTs", [128, KC, NSM], BF16)
        oTs = R.alloc("b_oTs", [128, KC, NSM], BF16)
        esb = [R.alloc("b_esb%d" % i, [128, 32], BF16) for i in range(2)]
        ssb = R.alloc("b_ssb", [128, 32], F32)
        rsb = R.alloc("b_rsb", [128, 16], F32)
        wq = [self.wget(bidx[(l, 'q', c)]) for c in range(2)]
        wo = [self.wget(bidx[(l, 'o', c)]) for c in range(2)]

        def qproj(c0, N, qdst, xn=xn, do_norm=True):
            if do_norm:
                self.norm(lambda kc: X[:, kc, c0:c0 + N], lambda kc: pvc(PV_GATT + l * 8 + kc),
                          lambda kc: xn[:, kc, 0:N], N, sq, rstd)
            yield
            for oc in range(KC):
                bk = self.bank()
                self.mm(bk[:, 0:N], [(wq[oc // 4][:, kc, (oc % 4) * 128:(oc % 4 + 1) * 128], xn[:, kc, 0:N])
                                     for kc in range(KC)])
                A(qdst[:, oc, 0:N], bk[:, 0:N], AF.Copy, scale=1.0 / 16.0)
                if oc == 3:
                    yield
            yield

        def oproj(c0, N, osrc):
            for oc in range(KC):
                bk = self.bank()
                self.mm(bk[:, 0:N], [(wo[oc // 4][:, kc, (oc % 4) * 128:(oc % 4 + 1) * 128], osrc[:, kc, 0:N])
                                     for kc in range(KC)])
                self.tt(X[:, oc, c0:c0 + N], X[:, oc, c0:c0 + N], bk[:, 0:N], ADD)

        def kv_load(b):
            trk.dma('pool', Ks[b % 2][:], dr_['KcT'][l, b].rearrange("(kc p) m -> p kc m", p=128), 'ld_k%d' % (b % 2))
            trk.dma('pool', Vs[b % 2][:], dr_['Vc'][l, b].rearrange("(j p) d -> p j d", p=128), 'ld_v%d' % (b % 2))

        def sample_batch(b):
            if b + 1 < 16:
                kv_load(b + 1)
            kb, vb, es = Ks[b % 2], Vs[b % 2], esb[b % 2]
            bsc = self.bank()
            groups = []
            for hh in range(4):
                for jm in range(2):
                    col = hh * 8 + jm * 4
                    groups.append((bsc[:, col:col + 4],
                                   [(kb[:, 2 * hh + dc, jm * 128:(jm + 1) * 128],
                                     qTs[:, 2 * hh + dc, b * 4:(b + 1) * 4]) for dc in range(2)]))
            self.mm_groups(groups)
            A(es[:, :], bsc[:, 0:32], AF.Exp)
            bs = self.bank()
            self.mm(bs[:, 0:32], [(ones[:], es[:, :])])
            A(ssb[:, :], bs[:, 0:32], AF.Copy)
            sv = ssb[:, :].rearrange("p (h j t) -> p h j t", j=2, t=4)
            self.tt(rsb[:, :].rearrange("p (h t) -> p h t", t=4), sv[:, :, 0, :], sv[:, :, 1, :], ADD)
            self.recip(rsb[:, :], rsb[:, :])
            bo = self.bank()
            groups = []
            for hh in range(4):
                for dc in range(2):
                    col = (hh * 2 + dc) * 4
                    groups.append((bo[:, col:col + 4],
                                   [(vb[:, jm, hh * 256 + dc * 128:hh * 256 + (dc + 1) * 128],
                                     es[:, hh * 8 + jm * 4:hh * 8 + jm * 4 + 4]) for jm in range(2)]))
            self.mm_groups(groups)
            for dc in range(2):
                ov = oTs[:, :, b * 4:(b + 1) * 4].rearrange("p (h dc) t -> p dc h t", dc=2)[:, dc]
                iv = bo[:, 0:32].rearrange("p (h dc t) -> p dc h t", dc=2, t=4)[:, dc]
                rv = rsb[:, :].rearrange("p (h t) -> p h t", t=4)
                self.tt(ov, iv, rv, MUL)

        def scores(hh, N, qT):
            eT = eT2[hh % 2]
            for jm in range(2):
                bk = self.bank()
                self.mm(bk[:, 0:N], [(KpT[:, 2 * hh + dc, jm * 128:(jm + 1) * 128], qT[:, 2 * hh + dc, 0:N])
                                     for dc in range(2)])
                A(eT[:, jm, 0:N], bk[:, 0:N], AF.Exp)

        def sum_pv(hh, N):
            eT, rs = eT2[hh % 2], rs2[hh % 2]
            bs = self.bank()
            self.mm(bs[:, 0:N], [(ones[:], eT[:, jm, 0:N]) for jm in range(2)])
            A(rs[:, 0:N], bs[:, 0:N], AF.Ln)
            A(rs[:, 0:N], rs[:, 0:N], AF.Exp, scale=-1.0)
            for dc in range(2):
                bo = self.bank()
                self.mm(bo[:, 0:N], [(Vp[:, jm, hh * 256 + dc * 128:hh * 256 + (dc + 1) * 128], eT[:, jm, 0:N])
                                     for jm in range(2)])
                self.tt(oT[:, 2 * hh + dc, 0:N], bo[:, 0:N], rs[:, 0:N], MUL)

        def drive(gens):
            gens = [g for g in gens if g is not None]
            while gens:
                alive = []
                for g in gens:
                    try:
                        next(g)
                        alive.append(g)
                    except StopIteration:
                        pass
                gens = alive

        ptiles = list(range(0, NPR, NB))

        def backB(ti):
            qT = qT2[ti % 2]
            scores(0, NB, qT)
            for hh in range(4):
                if hh + 1 < 4:
                    scores(hh + 1, NB, qT)
                sum_pv(hh, NB)
                sample_batch(ti * 4 + hh)
                yield
            oproj(ptiles[ti], NB, oT)
            yield

        kv_load(0)
        drive([qproj(NPR, NSM, qTs, xn=xn_s, do_norm=False)])
        drive([qproj(ptiles[0], NB, qT2[0], do_norm=False)])
        for ti in range(len(ptiles)):
            nxt = qproj(ptiles[ti + 1], NB, qT2[(ti + 1) % 2]) if ti + 1 < len(ptiles) else None
            drive([nxt, backB(ti)])
        oproj(NPR, NSM, oTs)
        for c in range(2):
            self.wrel(bidx[(l, 'q', c)])
        for c in range(2):
            self.wrel(bidx[(l, 'o', c)])

        self.ck('B%d' % l)
        R.reset()
        NC_ = 512
        o_xnf = R.ptr
        xnf = R.alloc("c_xn", [128, KC, T], BF16)
        hb = R.alloc("c_h", [128, 8, T], BF16)
        sq = R.alloc("c_sq", [128, KC, NC_], BF16)
        rstd = R.alloc("c_rstd", [128, NC_], F32)
        sgt = [R.alloc("c_sg%d" % i, [128, NC_], F32) for i in range(2)]
        tiles = [(c0, min(NC_, T - c0)) for c0 in range(0, T, NC_)]
        def cnorm(ti):
            c0, N = tiles[ti]
            self.norm(lambda kc: X[:, kc, c0:c0 + N], lambda kc: pvc(PV_GFFN + l * 8 + kc),
                      lambda kc: xnf[:, kc, c0:c0 + N], N, sq, rstd)
        cnorm(0)
        cnorm(1)
        normed = 2
        cnt = 0
        for g in range(3):
            nch = 8 if g < 2 else 6
            for hbk in range(2):
                k0 = hbk * 4
                k1 = min(nch, k0 + 4)
                if k1 <= k0:
                    continue
                wg = self.wget(bidx[(l, 'gate', g, hbk)])
                wu = self.wget(bidx[(l, 'up', g, hbk)])
                for fcl in range(k0, k1):
                    cc = (fcl - k0) * 128
                    for ti_, (c0, N) in enumerate(tiles):
                        if normed < len(tiles) and ti_ + 2 >= normed:
                            cnorm(normed)
                            normed += 1
                        bg = self.bank()
                        self.mm(bg[:, 0:N], [(wg[:, kc, cc:cc + 128], xnf[:, kc, c0:c0 + N]) for kc in range(KC)])
                        bu = self.bank()
                        self.mm(bu[:, 0:N], [(wu[:, kc, cc:cc + 128], xnf[:, kc, c0:c0 + N]) for kc in range(KC)])
                        st = sgt[cnt % 2]
                        cnt += 1
                        A(st[:, 0:N], bg[:, 0:N], AF.Silu)
                        self.tt(hb[:, fcl, c0:c0 + N], bu[:, 0:N], st[:, 0:N], MUL)
                self.wrel(bidx[(l, 'gate', g, hbk)])
                self.wrel(bidx[(l, 'up', g, hbk)])
            wd = [self.wget(bidx[(l, 'down', g, hbk)]) for hbk in range(2)]
            if g == 2 and l + 1 < L:
                o_keep = R.ptr
                R.ptr = 0
                cfD_n = R.alloc("cfD_n", [128, 4 * 31, 128], BF16)
                R.ptr = o_keep
                assert o_xnf == 0
                self.build_diag(l + 1, cfD_n, None, identf, pvc, do_cf=True, do_rg=False)
                self.cf_prebuilt.add(l + 1)
                self.cf_handle[l + 1] = cfD_n
            if l == L - 1 and g == 2:
                o_keep = R.ptr
                R.ptr = o_xnf
                ys = [R.alloc("f_y%d" % i, [128, KC, NC_], F32) for i in range(2)]
                R.ptr = o_keep
                yv = env['yv']
                for ti, (c0, N) in enumerate(tiles):
                    for oc in range(KC):
                        bk = self.bank()
                        self.mm(bk[:, 0:N], [(wd[kcl // 4][:, kcl % 4, oc * 128:(oc + 1) * 128],
                                              hb[:, kcl, c0:c0 + N]) for kcl in range(nch)])
                        self.tt(X[:, oc, c0:c0 + N], X[:, oc, c0:c0 + N], bk[:, 0:N], ADD)
                    y = ys[ti % 2]
                    self.norm(lambda kc: X[:, kc, c0:c0 + N], lambda kc: pvc(PV_GFIN + kc),
                              lambda kc: y[:, kc, 0:N], N, sq, rstd)
                    trk.dma('sp', yv[:, :, c0:c0 + N], y[:, :, 0:N], 'st_y%d' % (ti % 2))
                self.final_done = True
            else:
              for oc in range(KC):
                for (c0, N) in tiles:
                    bk = self.bank()
                    self.mm(bk[:, 0:N], [(wd[kcl // 4][:, kcl % 4, oc * 128:(oc + 1) * 128], hb[:, kcl, c0:c0 + N])
                                         for kcl in range(nch)])
                    self.tt(X[:, oc, c0:c0 + N], X[:, oc, c0:c0 + N], bk[:, 0:N], ADD)
            for hbk in range(2):
                self.wrel(bidx[(l, 'down', g, hbk)])

        self.ck('C%d' % l)

    def final_norm(self, env):
        trk = self.trk
        R = env['R']; X = env['X']; pv = env['pv']; yv = env['yv']
        R.reset()
        NF = 512
        sqs = [R.alloc("f_sq%d" % i, [128, KC, NF], BF16) for i in range(2)]
        rstds = [R.alloc("f_rstd%d" % i, [128, NF], F32) for i in range(2)]
        ys = [R.alloc("f_y%d" % i, [128, KC, NF], F32) for i in range(2)]
        for ti, c0 in enumerate(range(0, T, NF)):
            N = min(NF, T - c0)
            y = ys[ti % 2]
            self.norm(lambda kc: X[:, kc, c0:c0 + N], lambda kc: pv[:, PV_GFIN + kc:PV_GFIN + kc + 1],
                      lambda kc: y[:, kc, 0:N], N, sqs[ti % 2], rstds[ti % 2])
            trk.dma('sp', yv[:, :, c0:c0 + N], y[:, :, 0:N], 'st_y%d' % (ti % 2))


_CACHE = {}


def _get_nc():
    if 'nc' not in _CACHE:
        _CACHE['nc'] = Builder().build()
    return _CACHE['nc']


def _pack_pv(inp):
    pv = np.zeros((128, PV_N), np.float32)

    def feat(v):
        return np.ascontiguousarray(v.reshape(L, KC, 128).transpose(2, 0, 1)).reshape(128, L * KC)

    def chan(v):
        return np.ascontiguousarray(v.reshape(L, 4, 128).transpose(2, 0, 1)).reshape(128, L * 4)

    pv[:, PV_GMIX:PV_GMIX + 16] = feat(inp['norm_mix_g'])
    pv[:, PV_GATT:PV_GATT + 16] = feat(inp['norm_attn_g'])
    pv[:, PV_GFFN:PV_GFFN + 16] = feat(inp['norm_ffn_g'])
    pv[:, PV_GMEM:PV_GMEM + 16] = feat(inp['norm_mem_g'])
    pv[:, PV_GFIN:PV_GFIN + 8] = inp['norm_final_g'].reshape(KC, 128).T
    rgw = inp['rg_conv_w'].reshape(L, 4, 4, 128).transpose(3, 0, 2, 1)
    pv[:, PV_RGW:PV_RGW + 32] = np.ascontiguousarray(rgw).reshape(128, 32)
    pv[:, PV_RGB:PV_RGB + 8] = chan(inp['rg_conv_b'])
    pv[:, PV_BA:PV_BA + 8] = chan(inp['rg_ba'])
    pv[:, PV_BX:PV_BX + 8] = chan(inp['rg_bx'])
    pv[:, PV_LAM:PV_LAM + 8] = chan(inp['rg_lambda'])
    cfw = inp['cf_conv_w'].reshape(L, 31, 4, 128).transpose(3, 0, 2, 1)
    pv[:, PV_CFW:PV_CFW + 248] = np.ascontiguousarray(cfw).reshape(128, 248)
    pv[:, PV_CFB:PV_CFB + 8] = chan(inp['cf_conv_b'])
    pv[:, PV_LNG:PV_LNG + 8] = chan(inp['cf_ln_g'])
    pv[:, PV_LNB:PV_LNB + 8] = chan(inp['cf_ln_b'])
    return pv


def _prep(inp):
    inp = {k: np.asarray(v) for k, v in inp.items()}
    f32 = np.float32
    pv = _pack_pv(inp)
    shared = {
        'pv': pv,
        'rg_wa': np.ascontiguousarray(inp['rg_wa'], f32), 'rg_wx': np.ascontiguousarray(inp['rg_wx'], f32),
        'w_in': np.ascontiguousarray(inp['w_in'], f32), 'w_out': np.ascontiguousarray(inp['w_out'], f32),
        'w_q': np.ascontiguousarray(inp['w_q'], f32), 'w_k': np.ascontiguousarray(inp['w_k'], f32),
        'w_v': np.ascontiguousarray(inp['w_v'], f32), 'w_o': np.ascontiguousarray(inp['w_o'], f32),
        'w_gate': np.ascontiguousarray(inp['w_gate'], f32), 'w_up': np.ascontiguousarray(inp['w_up'], f32),
        'w_down': np.ascontiguousarray(inp['w_down'], f32),
    }
    in_maps = []
    for c in range(NCORES):
        sl = slice(16 * c, 16 * (c + 1))
        xs = inp['x_sample'][sl].reshape(NSM, D)
        xT = np.ascontiguousarray(np.concatenate([inp['x_prompt'][c].T, xs.T], axis=1), f32)
        memT = np.ascontiguousarray(inp['mem_prompt'][c].T, f32)
        KcT = np.ascontiguousarray(inp['cache_mem_k'][:, sl].reshape(L, 16, 256, D).transpose(0, 1, 3, 2), f32)
        Vc = np.ascontiguousarray(inp['cache_mem_v'][:, sl].reshape(L, 16, 256, D), f32)
        h0 = np.ascontiguousarray(inp['state_rglru_h'][:, sl].reshape(L, 16, 4, 128).transpose(0, 3, 2, 1), f32)
        rgs = np.ascontiguousarray(
            inp['state_rglru_conv'][:, sl].reshape(L, 16, 3, 4, 128).transpose(0, 4, 3, 1, 2), f32)
        cfs = np.ascontiguousarray(
            inp['state_conf_conv'][:, sl].reshape(L, 16, 30, 4, 128).transpose(0, 4, 3, 1, 2), f32)
        m = dict(shared)
        m.update({'xT': xT, 'memT': memT, 'KcT': KcT, 'Vc': Vc, 'h0': h0, 'rgs': rgs, 'cfs': cfs})
        in_maps.append(m)
    return in_maps


def kernel(**inp):
    nc = _get_nc()
    in_maps = _prep(inp)
    res = run_bass_kernel_spmd(nc, in_maps, core_ids=list(range(NCORES)))
    return _post(res.results)


def _post(rs):
    f32 = np.float32
    B = NCORES
    y_prompt = np.empty((B, NPR, D), f32)
    y_sample = np.empty((128, 4, D), f32)
    p_h = np.empty((L, B, 512), f32)
    p_rg = np.empty((L, B, 3, 512), f32)
    p_cf = np.empty((L, B, 30, 512), f32)
    p_mk = np.empty((L, B, 256, 4, 256), f32)
    p_mv = np.empty((L, B, 256, 4, 256), f32)
    s_h = np.empty((L, 128, 512), f32)
    s_rg = np.empty((L, 128, 3, 512), f32)
    s_cf = np.empty((L, 128, 30, 512), f32)
    for c in range(NCORES):
        r = rs[c]
        sl = slice(16 * c, 16 * (c + 1))
        yT = r['yT']
        y_prompt[c] = yT[:, :NPR].T
        y_sample[sl] = yT[:, NPR:].T.reshape(16, 4, D)
        p_h[:, c] = r['o_ph'].transpose(0, 2, 1).reshape(L, 512)
        p_rg[:, c] = r['o_prg'].transpose(0, 3, 2, 1).reshape(L, 3, 512)
        p_cf[:, c] = r['o_pcf'].transpose(0, 3, 2, 1).reshape(L, 30, 512)
        p_mk[:, c] = r['o_pk'].transpose(0, 3, 2, 1).reshape(L, 256, 4, 256)
        p_mv[:, c] = r['o_pv'].reshape(L, 256, 4, 256)
        s_h[:, sl] = r['o_sh'].transpose(0, 3, 2, 1).reshape(L, 16, 512)
        s_rg[:, sl] = r['o_srg'].transpose(0, 3, 4, 2, 1).reshape(L, 16, 3, 512)
        scf = np.concatenate([r['o_scfh'], r['o_scfn']], axis=4)
        s_cf[:, sl] = scf.transpose(0, 3, 4, 2, 1).reshape(L, 16, 30, 512)
    return (y_prompt, y_sample, p_h, p_rg, p_cf, p_mk, p_mv, s_h, s_rg, s_cf)
```

```python
import itertools
import numpy as np
import concourse.bass as bass
import concourse.mybir as mybir
from concourse.bass_utils import run_bass_kernel_spmd

F32 = mybir.dt.float32
BF16 = mybir.dt.bfloat16
U8 = mybir.dt.uint8
AF = mybir.ActivationFunctionType
ALU = mybir.AluOpType

NCORES = 8
L = 2
D = 1024
KC = 8
NPR = 2048
NSM = 64
T = NPR + NSM
DFF = 2816
EPS = 1e-6
GRAN = 128
PSUM_BASE = 1 << 24
RING_SLOTS = 6
SLOT_BYTES = 8192

PV_GMIX, PV_GATT, PV_GFFN, PV_GMEM, PV_GFIN = 0, 16, 32, 48, 64
PV_RGW, PV_RGB, PV_BA, PV_BX, PV_LAM = 72, 104, 112, 120, 128
PV_CFW, PV_CFB, PV_LNG, PV_LNB = 136, 384, 392, 400
PV_ZERO = 408
PV_N = 416


def _esz(dt):
    return 4 if dt == F32 else (2 if dt == BF16 else 1)


class Trk:
    def __init__(self, nc):
        self.nc = nc
        self.E = {'pe': nc.tensor, 'act': nc.scalar, 'dve': nc.vector, 'pool': nc.gpsimd, 'sp': nc.sync}
        self.sems = {}
        self.val = {}
        self.waited = {e: {} for e in self.E}
        self.lastw = {}
        self.rd = {}
        self.base = {}

    def reg(self, handle, addr):
        self.base[handle.name] = addr

    def sem(self, key):
        if key not in self.sems:
            self.sems[key] = self.nc.alloc_semaphore('s_' + key)
            self.val[key] = 0
        return self.sems[key]

    def grans(self, aps):
        out = set()
        for ap in aps:
            nm = ap.tensor.name
            if nm not in self.base:
                continue
            base = self.base[nm]
            es = _esz(ap.dtype)
            pat = ap.ap
            rowstride = pat[0][0]
            off = int(ap.offset)
            if rowstride > 0:
                off = off % rowstride
            free = pat[1:]
            if len(free) == 0:
                rngs = [(off, off + 1)]
            else:
                ist, icnt = free[-1]
                ilen = (icnt - 1) * abs(ist) + 1
                outer = free[:-1]
                rngs = []
                for idx in itertools.product(*[range(c) for (_, c) in outer]):
                    st = off + sum(i * s for i, (s, _) in zip(idx, outer))
                    rngs.append((st, st + ilen))
            for lo, hi in rngs:
                blo = base + lo * es
                bhi = base + hi * es
                for g in range(blo // GRAN, (bhi - 1) // GRAN + 1):
                    out.add(g)
        return out

    def _deps(self, rg, wg):
        need = {}
        for g in rg:
            w = self.lastw.get(g)
            if w is not None and need.get(w[0], 0) < w[1]:
                need[w[0]] = w[1]
        for g in wg:
            w = self.lastw.get(g)
            if w is not None and need.get(w[0], 0) < w[1]:
                need[w[0]] = w[1]
            r = self.rd.get(g)
            if r:
                for k, v in r.items():
                    if need.get(k, 0) < v:
                        need[k] = v
        return need

    def _commit(self, rg, wg, key, val):
        for g in rg:
            d = self.rd.get(g)
            if d is None:
                self.rd[g] = {key: val}
            else:
                d[key] = val
        for g in wg:
            self.lastw[g] = (key, val)
            self.rd[g] = None

    def _wait(self, e, need):
        eng = self.E[e]
        wd = self.waited[e]
        for k, v in need.items():
            if e == 'pe' and k == 'pe':
                continue
            if k not in self.E:
                v = self.val[k]
            if wd.get(k, 0) < v:
                eng.wait_ge(self.sems[k], v)
                wd[k] = v

    def op(self, e, reads, writes, fn):
        rg = self.grans(reads)
        wg = self.grans(writes)
        self._wait(e, self._deps(rg, wg))
        ins = fn()
        s = self.sem(e)
        self.val[e] += 1
        ins.then_inc(s, 1)
        self._commit(rg, wg, e, self.val[e])

    def dma(self, q, out, in_, key):
        reads = [in_] if in_.tensor.name in self.base else []
        writes = [out] if out.tensor.name in self.base else []
        rg = self.grans(reads)
        wg = self.grans(writes)
        self._wait(q, self._deps(rg, wg))
        s = self.sem(key)
        ins = self.E[q].dma_start(out=out, in_=in_)
        self.val[key] += 16
        ins.then_inc(s, 16)
        self._commit(rg, wg, key, self.val[key])


class Region:
    def __init__(self, nc, trk, base, size):
        self.nc, self.trk, self.base, self.size = nc, trk, base, size
        self.ptr = 0

    def reset(self):
        self.ptr = 0

    def alloc(self, name, shape, dt):
        n = 1
        for s in shape[1:]:
            n *= s
        nbytes = n * _esz(dt)
        nbytes = (nbytes + GRAN - 1) // GRAN * GRAN
        assert self.ptr + nbytes <= self.size, (name, self.ptr, nbytes, self.size)
        addr = self.base + self.ptr
        t = self.nc.alloc_sbuf_tensor_at(name, list(shape), dt, offset=addr)
        self.trk.reg(t, addr)
        self.ptr += nbytes
        return t


class _Stop(Exception):
    pass


_STOP = None


class Builder:
    def ck(self, name):
        if _STOP == name:
            raise _Stop()

    def __init__(self):
        self.nc = bass.Bass("TRN2", target_bir_lowering=False)
        self.trk = Trk(self.nc)
        self.pb = 0
        self.cf_prebuilt = set()
        self.cf_handle = {}

    def bank(self):
        self.pb = (self.pb + 1) % 8
        return self.ps[self.pb]

    def mm(self, out, pairs, extra_reads=()):
        reads = []
        for a, b in pairs:
            reads.append(a)
            reads.append(b)
        n = len(pairs)

        def fn():
            ins = None
            for i, (a, b) in enumerate(pairs):
                ins = self.nc.tensor.matmul(out, lhsT=a, rhs=b, start=(i == 0), stop=(i == n - 1))
            return ins
        self.trk.op('pe', reads, [out], fn)

    def mm_groups(self, groups):
        reads, writes = [], []
        for out, pairs in groups:
            writes.append(out)
            for a, b in pairs:
                reads.append(a)
                reads.append(b)

        def fn():
            ins = None
            for out, pairs in groups:
                n = len(pairs)
                for i, (a, b) in enumerate(pairs):
                    ins = self.nc.tensor.matmul(out, lhsT=a, rhs=b, start=(i == 0), stop=(i == n - 1))
            return ins
        self.trk.op('pe', reads, writes, fn)

    def act(self, out, in_, func, bias=None, scale=None):
        reads = [in_]
        kw = {}
        if bias is not None:
            kw['bias'] = bias
            if not isinstance(bias, (int, float)):
                reads.append(bias)
        if scale is not None:
            kw['scale'] = scale
            if not isinstance(scale, (int, float)):
                reads.append(scale)
        self.trk.op('act', reads, [out], lambda: self.nc.scalar.activation(out=out, in_=in_, func=func, **kw))

    def tt(self, out, in0, in1, op, eng='dve'):
        e = self.trk.E[eng]
        self.trk.op(eng, [in0, in1], [out], lambda: e.tensor_tensor(out=out, in0=in0, in1=in1, op=op))

    def ts(self, out, in0, s1, s2, op0, op1=None, eng='dve'):
        e = self.trk.E[eng]
        reads = [in0]
        if not isinstance(s1, (int, float)):
            reads.append(s1)
        if s2 is not None and not isinstance(s2, (int, float)):
            reads.append(s2)
        if op1 is None:
            self.trk.op(eng, reads, [out], lambda: e.tensor_scalar(out=out, in0=in0, scalar1=s1, scalar2=None, op0=op0))
        else:
            self.trk.op(eng, reads, [out], lambda: e.tensor_scalar(out=out, in0=in0, scalar1=s1, scalar2=s2, op0=op0, op1=op1))

    def stt(self, out, in0, scalar, in1, op0, op1):
        reads = [in0, in1]
        if not isinstance(scalar, (int, float)):
            reads.append(scalar)
        self.trk.op('dve', reads, [out], lambda: self.nc.vector.scalar_tensor_tensor(
            out=out, in0=in0, scalar=scalar, in1=in1, op0=op0, op1=op1))

    def scan(self, out, d0, d1, initial):
        reads = [d0, d1]
        if not isinstance(initial, (int, float)):
            reads.append(initial)
        self.trk.op('dve', reads, [out], lambda: self.nc.vector.tensor_tensor_scan(
            out=out, data0=d0, data1=d1, initial=initial, op0=ALU.mult, op1=ALU.add))

    def recip(self, out, in_):
        self.trk.op('dve', [in_], [out], lambda: self.nc.vector.reciprocal(out=out, in_=in_))

    def copy(self, out, in_, eng='dve'):
        e = self.trk.E[eng]
        self.trk.op(eng, [in_], [out], lambda: e.tensor_copy(out=out, in_=in_))

    def memset(self, out, v, eng='dve'):
        e = self.trk.E[eng]
        self.trk.op(eng, [], [out], lambda: e.memset(out, v))

    def ring_plan(self, blocks):
        self.blocks = blocks
        self.blk_view = [None] * len(blocks)
        self.blk_emitted = 0
        self.blk_released = [False] * len(blocks)

    def _ring_emit(self, i):
        ap = self.blocks[i]
        a, b = ap.shape[1], ap.shape[2]
        slot = i % RING_SLOTS
        v = self.ring[slot][:, 0:a * b].rearrange("p (a b) -> p a b", a=a)
        self.trk.dma('pool', v, ap, 'ring%d' % slot)
        self.blk_view[i] = v

    def _ring_pump(self, upto=None):
        while self.blk_emitted < len(self.blocks):
            j = self.blk_emitted
            if upto is not None and j <= upto:
                pass
            elif j >= RING_SLOTS and not self.blk_released[j - RING_SLOTS]:
                break
            if j >= RING_SLOTS:
                assert self.blk_released[j - RING_SLOTS], "ring overflow: block %d" % j
            self._ring_emit(j)
            self.blk_emitted += 1

    def wget(self, i):
        self._ring_pump(upto=i)
        return self.blk_view[i]

    def wrel(self, i):
        self.blk_released[i] = True
        self._ring_pump()

    def norm(self, xin, gvec, out_fn, N, sq, rstd):
        for kc in range(KC):
            self.act(sq[:, kc, 0:N], xin(kc), AF.Square)
        bk = self.bank()
        self.mm(bk[:, 0:N], [(self.c1024[:], sq[:, kc, 0:N]) for kc in range(KC)])
        self.act(rstd[:, 0:N], bk[:, 0:N], AF.Ln, bias=EPS)
        self.act(rstd[:, 0:N], rstd[:, 0:N], AF.Exp, scale=-0.5)
        for kc in range(KC):
            self.stt(out_fn(kc), xin(kc), gvec(kc), rstd[:, 0:N], ALU.mult, ALU.mult)

    def build(self):
        nc = self.nc
        trk = self.trk
        dr = {}

        def din(name, shape):
            dr[name] = nc.dram_tensor(name, list(shape), F32, kind="ExternalInput").ap()
            return dr[name]

        def dout(name, shape):
            dr[name] = nc.dram_tensor(name, list(shape), F32, kind="ExternalOutput").ap()
            return dr[name]

        xT = din("xT", [D, T])
        memT = din("memT", [D, 256])
        KcT = din("KcT", [L, 16, D, 256])
        Vc = din("Vc", [L, 16, 256, D])
        h0 = din("h0", [L, 128, 4, 16])
        rgs = din("rgs", [L, 128, 4, 16, 3])
        cfs = din("cfs", [L, 128, 4, 16, 30])
        pvd = din("pv", [128, PV_N])
        rg_wa = din("rg_wa", [L, 8, 64, 64])
        rg_wx = din("rg_wx", [L, 8, 64, 64])
        w_in = din("w_in", [L, D, 2048])
        w_out = din("w_out", [L, D, D])
        w_q = din("w_q", [L, D, D])
        w_k = din("w_k", [L, D, D])
        w_v = din("w_v", [L, D, D])
        w_o = din("w_o", [L, D, D])
        w_gate = din("w_gate", [L, D, DFF])
        w_up = din("w_up", [L, D, DFF])
        w_down = din("w_down", [L, DFF, D])

        yT = dout("yT", [D, T])
        o_ph = dout("o_ph", [L, 128, 4])
        o_prg = dout("o_prg", [L, 128, 4, 3])
        o_pcf = dout("o_pcf", [L, 128, 4, 30])
        o_pk = dout("o_pk", [L, 128, 8, 256])
        o_pv = dout("o_pv", [L, 256, D])
        o_sh = dout("o_sh", [L, 128, 4, 16])
        o_srg = dout("o_srg", [L, 128, 4, 16, 3])
        o_scfh = dout("o_scfh", [L, 128, 4, 16, 26])
        o_scfn = dout("o_scfn", [L, 128, 4, 16, 4])

        ARENA = 212800
        arena = nc.alloc_sbuf_tensor("arena", [128, ARENA], U8)
        abase = nc.lookup_mloc(arena).addr
        P = Region(nc, trk, abase, ARENA)
        X = P.alloc("X", [128, KC, T], F32)
        identf = P.alloc("identf", [128, 128], F32)
        ident = P.alloc("ident", [128, 128], BF16)
        self.c1024 = P.alloc("c1024", [128, 128], BF16)
        c512 = P.alloc("c512", [128, 128], BF16)
        ones = P.alloc("ones", [128, 128], BF16)
        pv = P.alloc("pvs", [128, PV_N], F32)
        cl = P.alloc("cl", [128, L * 4], F32)
        cl2 = P.alloc("cl2", [128, L * 4], F32)
        wabd = P.alloc("wabd", [128, L * 2 * 4, 128], BF16)
        hstate = P.alloc("hstate", [128, 4], F32)
        self.ring = [P.alloc("ring%d" % i, [128, SLOT_BYTES // 2], BF16) for i in range(RING_SLOTS)]
        rbase = abase + P.ptr
        R = Region(nc, trk, rbase, ARENA - P.ptr)
        self.ps = []
        for i in range(8):
            t = nc.alloc_psum_tensor("ps%d" % i, [128, 512], F32)
            trk.reg(t, PSUM_BASE + i * 2048)
            self.ps.append(t)

        blocks = []
        bidx = {}

        def wv(w, l):
            return w[l].rearrange("(kc p) n -> p kc n", p=128)

        for l in range(L):
            for c in range(4):
                bidx[(l, 'in', c)] = len(blocks)
                blocks.append(wv(w_in, l)[:, :, c * 512:(c + 1) * 512])
            for c in range(2):
                bidx[(l, 'out', c)] = len(blocks)
                blocks.append(wv(w_out, l)[:, :, c * 512:(c + 1) * 512])
            for nm, w in (('k', w_k), ('v', w_v), ('q', w_q), ('o', w_o)):
                for c in range(2):
                    bidx[(l, nm, c)] = len(blocks)
                    blocks.append(wv(w, l)[:, :, c * 512:(c + 1) * 512])
            for g in range(3):
                ncols = 1024 if g < 2 else 768
                c0 = g * 1024
                for hb in range(2):
                    lo = c0 + hb * 512
                    hi = min(c0 + ncols, lo + 512)
                    bidx[(l, 'gate', g, hb)] = len(blocks)
                    blocks.append(wv(w_gate, l)[:, :, lo:hi])
                    bidx[(l, 'up', g, hb)] = len(blocks)
                    blocks.append(wv(w_up, l)[:, :, lo:hi])
                nch = ncols // 128
                for hb in range(2):
                    k0 = hb * 4
                    k1 = min(nch, k0 + 4)
                    bidx[(l, 'down', g, hb)] = len(blocks)
                    blocks.append(w_down[l, c0 + k0 * 128:c0 + k1 * 128, :].rearrange("(kc p) n -> p kc n", p=128))
        self.ring_plan(blocks)

        A = self.act
        xv = xT.rearrange("(kc p) n -> p kc n", p=128)
        yv = yT.rearrange("(kc p) n -> p kc n", p=128)

        def pvc(off, n=1):
            return pv[:, off:off + n]

        with nc.Block():
            trk.dma('sp', pv[:], pvd[:, :], 'ld_pv')
            xtiles = [(ti, c0, min(512, T - c0)) for ti, c0 in enumerate(range(0, T, 512))]
            ti, c0, n = xtiles[0]
            trk.dma('sp', X[:, :, c0:c0 + n], xv[:, :, c0:c0 + n], 'ld_x%d' % ti)
            self.memset(identf[:], 0.0, eng='dve')
            trk.op('pool', [identf[:]], [identf[:]], lambda: nc.gpsimd.affine_select(
                out=identf[:], in_=identf[:], pattern=[[-1, 128]], compare_op=ALU.not_equal, fill=1.0,
                base=0, channel_multiplier=1))
            self.copy(ident[:], identf[:])
            self.memset(self.c1024[:], 1.0 / 1024.0)
            self.memset(c512[:], 1.0 / 512.0)
            self.memset(ones[:], 1.0)
            R.reset()
            R.ptr = 71936
            wst = R.alloc("wst", [128, L * 2 * 4, 128], F32)
            self.memset(wst[:], 0.0)
            for l in range(L):
                for gi, wsrc in enumerate((rg_wa, rg_wx)):
                    for hh in range(8):
                        j, e = hh // 2, hh % 2
                        trk.dma('sp', wst[e * 64:(e + 1) * 64, (l * 2 + gi) * 4 + j, e * 64:(e + 1) * 64],
                                wsrc[l, hh, :, :], 'ld_w%d' % ((l * 2 + gi) % 2))
            self.deferred = lambda: self.copy(wabd[:], wst[:])
            for ti, c0, n in xtiles[1:]:
                trk.dma('sp', X[:, :, c0:c0 + n], xv[:, :, c0:c0 + n], 'ld_x%d' % ti)
            for l in range(L):
                for j in range(4):
                    trk.dma('sp', o_scfh[l, :, j], cfs[l, :, j, :, 4:30], 'st_h')
            spt = R.alloc("spt", [128, L * 4], F32)
            A(spt[:], pvc(PV_LAM, 8), AF.Exp, scale=-1.0)
            A(spt[:], spt[:], AF.Ln, bias=1.0)
            self.ts(cl[:], spt[:], -8.0, None, ALU.mult)
            self.ts(cl2[:], spt[:], -16.0, None, ALU.mult)

            try:
                self.ck('P')
                for l in range(L):
                    self.layer(l, locals())
                if not getattr(self, 'final_done', False):
                    self.final_norm(locals())
            except _Stop:
                pass
            for k, v in trk.val.items():
                if k not in trk.E and v > 0:
                    nc.sync.wait_ge(trk.sems[k], v)
        return nc

    def build_diag(self, l, cfD, rgD, identf, pvc, do_cf=True, do_rg=True):
        MUL = ALU.mult
        for j in range(4):
            if do_rg:
                for k in range(4):
                    self.ts(rgD[:, j * 4 + k, :], identf[:], pvc(PV_RGW + (l * 4 + j) * 4 + k), None, MUL)
            if do_cf:
                for k in range(31):
                    if k % 3 == 2:
                        self.act(cfD[:, j * 31 + k, :], identf[:], AF.Copy,
                                 scale=pvc(PV_CFW + (l * 4 + j) * 31 + k))
                    else:
                        self.ts(cfD[:, j * 31 + k, :], identf[:], pvc(PV_CFW + (l * 4 + j) * 31 + k), None, MUL)

    def layer(self, l, env):
        nc, trk = self.nc, self.trk
        R = env['R']; X = env['X']; pv = env['pv']; bidx = env['bidx']
        ident, identf, c512, ones = env['ident'], env['identf'], env['c512'], env['ones']
        cl, cl2, wabd, hstate = env['cl'], env['cl2'], env['wabd'], env['hstate']
        dr_ = env['dr']
        A = self.act
        MUL, ADD, SUB = ALU.mult, ALU.add, ALU.subtract

        def pvc(off, n=1):
            return pv[:, off:off + n]

        R.reset()
        cfD = R.alloc("cfD", [128, 4 * 31, 128], BF16)
        if l in self.cf_handle:
            cfD = self.cf_handle[l]
        rgD = R.alloc("rgD", [128, 4 * 4, 128], BF16)
        NA = 256
        ycat = R.alloc("a_ycat", [128, KC, NA], BF16)
        sq = R.alloc("a_sq", [128, KC, NA], BF16)
        rstd = R.alloc("a_rstd", [128, NA], F32)
        xn = R.alloc("a_xn", [128, KC, NA], BF16)
        o_x = R.ptr
        xrb0 = R.alloc("a_xrb0", [128, 4, 3 + NA], BF16)
        cb0 = R.alloc("a_cb0", [128, 4, 30 + NA], BF16)
        o_end = R.ptr
        R.ptr = o_x
        csb = R.alloc("a_csb", [128, 4, 16, 34], BF16)
        assert R.ptr <= o_end
        R.ptr = o_end
        xrb1 = R.alloc("a_xrb1", [128, 4, 3 + NA], BF16)
        cb1 = R.alloc("a_cb1", [128, 4, 30 + NA], BF16)
        xrb2 = [xrb0, xrb1]
        cb2 = [cb0, cb1]
        gg2 = [R.alloc("a_gg%d" % i, [128, 4, NA], BF16) for i in range(2)]
        sg = R.alloc("a_sg", [128, NA], F32)
        xc2 = [R.alloc("a_xc%d" % i, [128, NA], F32) for i in range(2)]
        xcb = R.alloc("a_xcb", [128, NA], BF16)
        r_ = R.alloc("a_r", [128, NA], F32)
        i_ = R.alloc("a_i", [128, NA], F32)
        a_ = R.alloc("a_a", [128, NA], F32)
        m_ = R.alloc("a_m", [128, NA], F32)
        b_ = R.alloc("a_b", [128, NA], F32)
        h_ = R.alloc("a_h", [128, NA], F32)
        ccf = R.alloc("a_ccf", [128, 4, NA], F32)
        ccb = R.alloc("a_ccb", [128, 4, NA], BF16)
        sqb = R.alloc("a_sqb", [128, 4, NA], BF16)
        mean = R.alloc("a_mean", [128, NA], F32)
        var = R.alloc("a_var", [128, NA], F32)
        rstc = R.alloc("a_rstc", [128, NA], F32)
        o_dd = R.ptr
        dd = R.alloc("a_dd", [128, NA], F32)
        dd2 = [dd, sg]
        prg = R.alloc("a_prg", [128, 4, 3], F32)
        pcf = R.alloc("a_pcf", [128, 4, 30], F32)
        xrs = R.alloc("a_xrs", [128, 4, 16, 7], BF16)
        o_keep = R.ptr
        R.ptr = o_dd
        xrsf = R.alloc("a_xrsf", [128, 4, 16, 3], F32)
        R.ptr = o_keep
        srg = R.alloc("a_srg", [128, 4, 16, 3], F32)
        cs4 = R.alloc("a_cs4", [128, 4, 16, 4], F32)
        h0s = R.alloc("a_h0s", [128, 4, 16], F32)
        shl = R.alloc("a_shl", [128, 4, 16], F32)
        tmp16 = R.alloc("a_tmp16", [128, 16], F32)
        nb2 = R.alloc("a_nb2", [128, 8], F32)

        self.build_diag(l, cfD, rgD, identf, pvc, do_cf=(l not in self.cf_prebuilt))
        if getattr(self, 'deferred', None) is not None:
            self.deferred()
            self.deferred = None
        self.ts(nb2[:, 0:4], pvc(PV_BA + l * 4, 4), -1.0, None, MUL)
        self.ts(nb2[:, 4:8], pvc(PV_BX + l * 4, 4), -1.0, None, MUL)
        self.memset(xrb0[:, :, 0:3], 0.0)
        self.memset(cb0[:, :, 0:30], 0.0)
        self.memset(hstate[:], 0.0)

        win = [self.wget(bidx[(l, 'in', c)]) for c in range(4)]
        wout = [self.wget(bidx[(l, 'out', c)]) for c in range(2)]
        tiles = [(c0, NA, False) for c0 in range(0, NPR, NA)] + [(NPR, NSM, True)]
        NT = len(tiles)

        def load_sample_state():
            trk.dma('sp', xrsf[:], dr_['rgs'][l], 'ld_st0')
            trk.dma('sp', h0s[:], dr_['h0'][l], 'ld_st1')
            for j in range(4):
                trk.dma('pool', csb[:, j, :, 0:30], dr_['cfs'][l, :, j], 'ld_st%d' % (2 + j))
            self.copy(xrs[:, :, :, 0:3], xrsf[:])

        def v3(ap):
            return ap.rearrange("p (b t) -> p b t", t=4)

        def front(ti):
            c0, N, smp = tiles[ti]
            sset = ti % 2
            xrb, cb, gg = xrb2[sset], cb2[sset], gg2[sset]
            last_p = (not smp) and (c0 + N == NPR)
            if smp:
                load_sample_state()
            elif ti >= 1:
                self.copy(xrb[:, :, 0:3], xrb2[1 - sset][:, :, NA:NA + 3])
                self.copy(cb[:, :, 0:30], cb2[1 - sset][:, :, NA:NA + 30])
            for kc in range(KC):
                A(sq[:, kc, 0:N], X[:, kc, c0:c0 + N], AF.Square)
            bkn = self.bank()
            self.mm(bkn[:, 0:N], [(self.c1024[:], sq[:, kc, 0:N]) for kc in range(KC)])
            A(rstd[:, 0:N], bkn[:, 0:N], AF.Ln, bias=EPS)
            A(rstd[:, 0:N], rstd[:, 0:N], AF.Exp, scale=-0.5)
            yield
            for kc in range(KC):
                self.stt(xn[:, kc, 0:N], X[:, kc, c0:c0 + N], pvc(PV_GMIX + l * 8 + kc), rstd[:, 0:N], MUL, MUL)
            yield

            def proj(blk, j):
                bk = self.bank()
                self.mm(bk[:, 0:N], [(win[blk][:, kc, j * 128:(j + 1) * 128], xn[:, kc, 0:N]) for kc in range(KC)])
                return bk
            yield
            for j in range(4):
                bk = proj(1, j)
                A(gg[:, j, 0:N], bk[:, 0:N], AF.Gelu_apprx_tanh)
            yield
            for j in range(4):
                bk = proj(0, j)
                zc = pvc(PV_ZERO)
                if smp:
                    self.ts(xrs[:, j, :, 3:7], v3(bk[:, 0:N]), zc, None, ADD)
                    self.ts(srg[:, j, :, :], v3(bk[:, 0:N])[:, :, 1:4], zc, None, ADD)
                else:
                    self.ts(xrb[:, j, 3:3 + N], bk[:, 0:N], zc, None, ADD)
                    if last_p:
                        self.ts(prg[:, j, :], bk[:, N - 3:N], zc, None, ADD)
            yield
            yield
            for half in range(1):
                for j in range(4):
                    bv = proj(2, j)
                    bg = proj(3, j)
                    A(sg[:, 0:N], bg[:, 0:N], AF.Sigmoid)
                    if smp:
                        self.tt(cs4[:, j, :, :], v3(bv[:, 0:N]), v3(sg[:, 0:N]), MUL)
                        self.copy(csb[:, j, :, 30:34], cs4[:, j, :, :])
                    else:
                        self.tt(cb[:, j, 30:30 + N], bv[:, 0:N], sg[:, 0:N], MUL)
                        if last_p:
                            self.tt(pcf[:, j, :], bv[:, N - 30:N], sg[:, N - 30:N], MUL)
                yield

        def back(ti):
            c0, N, smp = tiles[ti]
            sset = ti % 2
            xrb, cb, gg = xrb2[sset], cb2[sset], gg2[sset]

            def tail(j):
                xc = xc2[j % 2]
                self.tt(b_[:, 0:N], i_[:, 0:N], xc[:, 0:N], MUL)
                self.tt(b_[:, 0:N], b_[:, 0:N], m_[:, 0:N], MUL)
                if smp:
                    av = v3(a_[:, 0:N])
                    bvw = v3(b_[:, 0:N])
                    self.tt(tmp16[:], av[:, :, 0], h0s[:, j, :], MUL)
                    self.tt(bvw[:, :, 0], bvw[:, :, 0], tmp16[:], ADD)
                    self.memset(av[:, :, 0], 0.0)
                    self.scan(h_[:, 0:N], a_[:, 0:N], b_[:, 0:N], 0.0)
                    self.copy(shl[:, j, :], v3(h_[:, 0:N])[:, :, 3])
                else:
                    self.scan(h_[:, 0:N], a_[:, 0:N], b_[:, 0:N], hstate[:, j:j + 1])
                    self.copy(hstate[:, j:j + 1], h_[:, N - 1:N])
                self.tt(ycat[:, j, 0:N], h_[:, 0:N], gg[:, j, 0:N], MUL)

            for j in range(4):
                xc = xc2[j % 2]
                bk = self.bank()
                if smp:
                    self.mm(bk[:, 0:N], [(rgD[:, j * 4 + k, :], xrs[:, j, :, k:k + 4]) for k in range(4)])
                else:
                    self.mm(bk[:, 0:N], [(rgD[:, j * 4 + k, :], xrb[:, j, k:k + N]) for k in range(4)])
                bias = pvc(PV_RGB + l * 4 + j)
                self.ts(xcb[:, 0:N], bk[:, 0:N], bias, None, ADD)
                self.ts(xc[:, 0:N], bk[:, 0:N], bias, None, ADD)
                bc = self.bank()
                if smp:
                    self.mm(bc[:, 0:N], [(cfD[:, j * 31 + k, :], csb[:, j, :, k:k + 4]) for k in range(31)])
                else:
                    self.mm(bc[:, 0:N], [(cfD[:, j * 31 + k, :], cb[:, j, k:k + N]) for k in range(31)])
                ba = self.bank()
                self.mm(ba[:, 0:N], [(wabd[:, (l * 2 + 0) * 4 + j, :], xcb[:, 0:N])])
                bx = self.bank()
                self.mm(bx[:, 0:N], [(wabd[:, (l * 2 + 1) * 4 + j, :], xcb[:, 0:N])])
                cbias = pvc(PV_CFB + l * 4 + j)
                self.ts(ccf[:, j, 0:N], bc[:, 0:N], cbias, None, ADD)
                self.ts(ccb[:, j, 0:N], bc[:, 0:N], cbias, None, ADD)
                self.stt(sqb[:, j, 0:N], bc[:, 0:N], cbias, ccf[:, j, 0:N], ADD, MUL)
                if j == 3:
                    bm = self.bank()
                    self.mm(bm[:, 0:N], [(c512[:], ccb[:, jj, 0:N]) for jj in range(4)])
                    bq = self.bank()
                    self.mm(bq[:, 0:N], [(c512[:], sqb[:, jj, 0:N]) for jj in range(4)])
                    A(mean[:, 0:N], bm[:, 0:N], AF.Copy)
                    self.tt(var[:, 0:N], mean[:, 0:N], mean[:, 0:N], MUL)
                    self.tt(var[:, 0:N], bq[:, 0:N], var[:, 0:N], SUB)
                    A(rstc[:, 0:N], var[:, 0:N], AF.Ln, bias=EPS)
                    A(rstc[:, 0:N], rstc[:, 0:N], AF.Exp, scale=-0.5)
                if j >= 1:
                    tail(j - 1)
                A(r_[:, 0:N], ba[:, 0:N], AF.Exp, scale=-1.0, bias=nb2[:, j:j + 1])
                A(i_[:, 0:N], bx[:, 0:N], AF.Exp, scale=-1.0, bias=nb2[:, 4 + j:5 + j])
                A(r_[:, 0:N], r_[:, 0:N], AF.Ln, bias=1.0)
                A(i_[:, 0:N], i_[:, 0:N], AF.Ln, bias=1.0)
                A(r_[:, 0:N], r_[:, 0:N], AF.Exp, scale=-1.0)
                A(i_[:, 0:N], i_[:, 0:N], AF.Exp, scale=-1.0)
                A(a_[:, 0:N], r_[:, 0:N], AF.Exp, scale=cl[:, l * 4 + j:l * 4 + j + 1])
                A(m_[:, 0:N], r_[:, 0:N], AF.Exp, scale=cl2[:, l * 4 + j:l * 4 + j + 1])
                A(m_[:, 0:N], m_[:, 0:N], AF.Ln, scale=-1.0, bias=1.0000001)
                A(m_[:, 0:N], m_[:, 0:N], AF.Exp, scale=0.5)
                yield
            for j in range(4):
                dd = dd2[j % 2]
                self.tt(dd[:, 0:N], ccf[:, j, 0:N], mean[:, 0:N], SUB)
                self.tt(dd[:, 0:N], dd[:, 0:N], rstc[:, 0:N], MUL)
                A(ycat[:, 4 + j, 0:N], dd[:, 0:N], AF.Silu, bias=pvc(PV_LNB + l * 4 + j),
                  scale=pvc(PV_LNG + l * 4 + j))
            yield
            tail(3)
            yield
            wout_proj(ti)
            yield

        def wout_proj(ti):
            c0, N, smp = tiles[ti]
            for oc in range(KC):
                bk = self.bank()
                self.mm(bk[:, 0:N], [(wout[oc // 4][:, kc, (oc % 4) * 128:(oc % 4 + 1) * 128], ycat[:, kc, 0:N])
                                     for kc in range(KC)])
                self.tt(X[:, oc, c0:c0 + N], X[:, oc, c0:c0 + N], bk[:, 0:N], ADD)

        def drive(gens):
            gens = [g for g in gens if g is not None]
            while gens:
                alive = []
                for g in gens:
                    try:
                        next(g)
                        alive.append(g)
                    except StopIteration:
                        pass
                gens = alive

        drive([front(0)])
        for ti in range(NT):
            drive([front(ti + 1) if ti + 1 < NT else None, back(ti)])
            if ti == NT - 2:
                for c in range(4):
                    self.wrel(bidx[(l, 'in', c)])

        trk.dma('sp', dr_['o_ph'][l], hstate[:], 'st_a0')
        trk.dma('sp', dr_['o_prg'][l], prg[:], 'st_a1')
        trk.dma('sp', dr_['o_pcf'][l], pcf[:], 'st_a2')
        trk.dma('sp', dr_['o_sh'][l], shl[:], 'st_a3')
        trk.dma('sp', dr_['o_srg'][l], srg[:], 'st_a4')
        trk.dma('sp', dr_['o_scfn'][l], cs4[:], 'st_a5')
        for c in range(2):
            self.wrel(bidx[(l, 'out', c)])

        self.ck('A%d' % l)
        R.reset()
        NB = 512
        KpT = R.alloc("b_KpT", [128, KC, 256], BF16)
        Vp = R.alloc("b_Vp", [128, 2, D], BF16)
        Ks = [R.alloc("b_Ks%d" % i, [128, KC, 256], BF16) for i in range(2)]
        Vs = [R.alloc("b_Vs%d" % i, [128, 2, D], BF16) for i in range(2)]
        sq = R.alloc("b_sq", [128, KC, NB], BF16)
        rstd = R.alloc("b_rstd", [128, NB], F32)
        xn = R.alloc("b_xn", [128, KC, NB], BF16)
        sq_s = R.alloc("b_sqs", [128, KC, NSM], BF16)
        rstd_s = R.alloc("b_rstds", [128, NSM], F32)
        xn_s = R.alloc("b_xns", [128, KC, NSM], BF16)
        mark = R.ptr
        memf = R.alloc("b_memf", [128, KC, 256], F32)
        memn = R.alloc("b_memn", [128, KC, 256], BF16)
        msq = R.alloc("b_msq", [128, KC, 256], BF16)
        mrs = R.alloc("b_mrs", [128, 256], F32)
        kst = R.alloc("b_kst", [128, KC, 256], F32)
        vst = R.alloc("b_vst", [128, 2, D], F32)
        trk.dma('sp', memf[:], dr_['memT'].rearrange("(kc p) n -> p kc n", p=128), 'ld_mem')
        self.norm(lambda kc: memf[:, kc, :], lambda kc: pvc(PV_GMEM + l * 8 + kc),
                  lambda kc: memn[:, kc, :], 256, msq, mrs)
        self.norm(lambda kc: X[:, kc, NPR:NPR + NSM], lambda kc: pvc(PV_GATT + l * 8 + kc),
                  lambda kc: xn_s[:, kc, 0:NSM], NSM, sq_s, rstd_s)
        self.norm(lambda kc: X[:, kc, 0:NB], lambda kc: pvc(PV_GATT + l * 8 + kc),
                  lambda kc: xn[:, kc, 0:NB], NB, sq, rstd)
        self.ck('Bn%d' % l)
        wk = [self.wget(bidx[(l, 'k', c)]) for c in range(2)]
        self.ck('Bw%d' % l)
        for dc in range(KC):
            bk = self.bank()
            self.mm(bk[:, 0:256], [(wk[dc // 4][:, kc, (dc % 4) * 128:(dc % 4 + 1) * 128], memn[:, kc, :])
                                   for kc in range(KC)])
            A(KpT[:, dc, :], bk[:, 0:256], AF.Copy)
            A(kst[:, dc, :], bk[:, 0:256], AF.Copy)
        self.ck('Bk%d' % l)
        trk.dma('sp', dr_['o_pk'][l], kst[:], 'st_bk')
        for c in range(2):
            self.wrel(bidx[(l, 'k', c)])
        wvv = [self.wget(bidx[(l, 'v', c)]) for c in range(2)]
        for mt in range(2):
            for cbk in range(2):
                bk = self.bank()
                self.mm(bk[:, :], [(memn[:, kc, mt * 128:(mt + 1) * 128], wvv[cbk][:, kc, :]) for kc in range(KC)])
                A(Vp[:, mt, cbk * 512:(cbk + 1) * 512], bk[:, :], AF.Copy)
                A(vst[:, mt, cbk * 512:(cbk + 1) * 512], bk[:, :], AF.Copy)
        trk.dma('sp', dr_['o_pv'][l].rearrange("(j p) d -> p j d", p=128), vst[:], 'st_bv')
        for c in range(2):
            self.wrel(bidx[(l, 'v', c)])
        self.ck('Ba%d' % l)
        R.ptr = mark
        qT2 = [R.alloc("b_qT%d" % i, [128, KC, NB], BF16) for i in range(2)]
        eT2 = [R.alloc("b_eT%d" % i, [128, 2, NB], BF16) for i in range(2)]
        rs2 = [R.alloc("b_rs%d" % i, [128, NB], F32) for i in range(2)]
        oT = R.alloc("b_oT", [128, KC, NB], BF16)
        qTs = R.alloc("b_qTs", [128, KC, NSM], BF16)
        oTs = R.alloc("b_oTs", [128, KC, NSM], BF16)
        esb = [R.alloc("b_esb%d" % i, [128, 32], BF16) for i in range(2)]
        ssb = R.alloc("b_ssb", [128, 32], F32)
        rsb = R.alloc("b_rsb", [128, 16], F32)
        wq = [self.wget(bidx[(l, 'q', c)]) for c in range(2)]
        wo = [self.wget(bidx[(l, 'o', c)]) for c in range(2)]

        def qproj(c0, N, qdst, xn=xn, do_norm=True):
            if do_norm:
                self.norm(lambda kc: X[:, kc, c0:c0 + N], lambda kc: pvc(PV_GATT + l * 8 + kc),
                          lambda kc: xn[:, kc, 0:N], N, sq, rstd)
            yield
            for oc in range(KC):
                bk = self.bank()
                self.mm(bk[:, 0:N], [(wq[oc // 4][:, kc, (oc % 4) * 128:(oc % 4 + 1) * 128], xn[:, kc, 0:N])
                                     for kc in range(KC)])
                A(qdst[:, oc, 0:N], bk[:, 0:N], AF.Copy, scale=1.0 / 16.0)
                if oc == 3:
                    yield
            yield

        def oproj(c0, N, osrc):
            for oc in range(KC):
                bk = self.bank()
                self.mm(bk[:, 0:N], [(wo[oc // 4][:, kc, (oc % 4) * 128:(oc % 4 + 1) * 128], osrc[:, kc, 0:N])
                                     for kc in range(KC)])
                self.tt(X[:, oc, c0:c0 + N], X[:, oc, c0:c0 + N], bk[:, 0:N], ADD)

        def kv_load(b):
            trk.dma('pool', Ks[b % 2][:], dr_['KcT'][l, b].rearrange("(kc p) m -> p kc m", p=128), 'ld_k%d' % (b % 2))
            trk.dma('pool', Vs[b % 2][:], dr_['Vc'][l, b].rearrange("(j p) d -> p j d", p=128), 'ld_v%d' % (b % 2))

        def sample_batch(b):
            if b + 1 < 16:
                kv_load(b + 1)
            kb, vb, es = Ks[b % 2], Vs[b % 2], esb[b % 2]
            bsc = self.bank()
            groups = []
            for hh in range(4):
                for jm in range(2):
                    col = hh * 8 + jm * 4
                    groups.append((bsc[:, col:col + 4],
                                   [(kb[:, 2 * hh + dc, jm * 128:(jm + 1) * 128],
                                     qTs[:, 2 * hh + dc, b * 4:(b + 1) * 4]) for dc in range(2)]))
            self.mm_groups(groups)
            A(es[:, :], bsc[:, 0:32], AF.Exp)
            bs = self.bank()
            self.mm(bs[:, 0:32], [(ones[:], es[:, :])])
            A(ssb[:, :], bs[:, 0:32], AF.Copy)
            sv = ssb[:, :].rearrange("p (h j t) -> p h j t", j=2, t=4)
            self.tt(rsb[:, :].rearrange("p (h t) -> p h t", t=4), sv[:, :, 0, :], sv[:, :, 1, :], ADD)
            self.recip(rsb[:, :], rsb[:, :])
            bo = self.bank()
            groups = []
            for hh in range(4):
                for dc in range(2):
                    col = (hh * 2 + dc) * 4
                    groups.append((bo[:, col:col + 4],
                                   [(vb[:, jm, hh * 256 + dc * 128:hh * 256 + (dc + 1) * 128],
                                     es[:, hh * 8 + jm * 4:hh * 8 + jm * 4 + 4]) for jm in range(2)]))
            self.mm_groups(groups)
            for dc in range(2):
                ov = oTs[:, :, b * 4:(b + 1) * 4].rearrange("p (h dc) t -> p dc h t", dc=2)[:, dc]
                iv = bo[:, 0:32].rearrange("p (h dc t) -> p dc h t", dc=2, t=4)[:, dc]
                rv = rsb[:, :].rearrange("p (h t) -> p h t", t=4)
                self.tt(ov, iv, rv, MUL)

        def scores(hh, N, qT):
            eT = eT2[hh % 2]
            for jm in range(2):
                bk = self.bank()
                self.mm(bk[:, 0:N], [(KpT[:, 2 * hh + dc, jm * 128:(jm + 1) * 128], qT[:, 2 * hh + dc, 0:N])
                                     for dc in range(2)])
                A(eT[:, jm, 0:N], bk[:, 0:N], AF.Exp)

        def sum_pv(hh, N):
            eT, rs = eT2[hh % 2], rs2[hh % 2]
            bs = self.bank()
            self.mm(bs[:, 0:N], [(ones[:], eT[:, jm, 0:N]) for jm in range(2)])
            A(rs[:, 0:N], bs[:, 0:N], AF.Ln)
            A(rs[:, 0:N], rs[:, 0:N], AF.Exp, scale=-1.0)
            for dc in range(2):
                bo = self.bank()
                self.mm(bo[:, 0:N], [(Vp[:, jm, hh * 256 + dc * 128:hh * 256 + (dc + 1) * 128], eT[:, jm, 0:N])
                                     for jm in range(2)])
                self.tt(oT[:, 2 * hh + dc, 0:N], bo[:, 0:N], rs[:, 0:N], MUL)

        def drive(gens):
            gens = [g for g in gens if g is not None]
            while gens:
                alive = []
                for g in gens:
                    try:
                        next(g)
                        alive.append(g)
                    except StopIteration:
                        pass
                gens = alive

        ptiles = list(range(0, NPR, NB))

        def backB(ti):
            qT = qT2[ti % 2]
            scores(0, NB, qT)
            for hh in range(4):
                if hh + 1 < 4:
                    scores(hh + 1, NB, qT)
                sum_pv(hh, NB)
                sample_batch(ti * 4 + hh)
                yield
            oproj(ptiles[ti], NB, oT)
            yield

        kv_load(0)
        drive([qproj(NPR, NSM, qTs, xn=xn_s, do_norm=False)])
        drive([qproj(ptiles[0], NB, qT2[0], do_norm=False)])
        for ti in range(len(ptiles)):
            nxt = qproj(ptiles[ti + 1], NB, qT2[(ti + 1) % 2]) if ti + 1 < len(ptiles) else None
            drive([nxt, backB(ti)])
        oproj(NPR, NSM, oTs)
        for c in range(2):
            self.wrel(bidx[(l, 'q', c)])
        for c in range(2):
            self.wrel(bidx[(l, 'o', c)])

        self.ck('B%d' % l)
        R.reset()
        NC_ = 512
        o_xnf = R.ptr
        xnf = R.alloc("c_xn", [128, KC, T], BF16)
        hb = R.alloc("c_h", [128, 8, T], BF16)
        sq = R.alloc("c_sq", [128, KC, NC_], BF16)
        rstd = R.alloc("c_rstd", [128, NC_], F32)
        sgt = [R.alloc("c_sg%d" % i, [128, NC_], F32) for i in range(2)]
        tiles = [(c0, min(NC_, T - c0)) for c0 in range(0, T, NC_)]
        def cnorm(ti):
            c0, N = tiles[ti]
            self.norm(lambda kc: X[:, kc, c0:c0 + N], lambda kc: pvc(PV_GFFN + l * 8 + kc),
                      lambda kc: xnf[:, kc, c0:c0 + N], N, sq, rstd)
        cnorm(0)
        cnorm(1)
        normed = 2
        cnt = 0
        for g in range(3):
            nch = 8 if g < 2 else 6
            for hbk in range(2):
                k0 = hbk * 4
                k1 = min(nch, k0 + 4)
                if k1 <= k0:
                    continue
                wg = self.wget(bidx[(l, 'gate', g, hbk)])
                wu = self.wget(bidx[(l, 'up', g, hbk)])
                for fcl in range(k0, k1):
                    cc = (fcl - k0) * 128
                    for ti_, (c0, N) in enumerate(tiles):
                        if normed < len(tiles) and ti_ + 2 >= normed:
                            cnorm(normed)
                            normed += 1
                        bg = self.bank()
                        self.mm(bg[:, 0:N], [(wg[:, kc, cc:cc + 128], xnf[:, kc, c0:c0 + N]) for kc in range(KC)])
                        bu = self.bank()
                        self.mm(bu[:, 0:N], [(wu[:, kc, cc:cc + 128], xnf[:, kc, c0:c0 + N]) for kc in range(KC)])
                        st = sgt[cnt % 2]
                        cnt += 1
                        A(st[:, 0:N], bg[:, 0:N], AF.Silu)
                        self.tt(hb[:, fcl, c0:c0 + N], bu[:, 0:N], st[:, 0:N], MUL)
                self.wrel(bidx[(l, 'gate', g, hbk)])
                self.wrel(bidx[(l, 'up', g, hbk)])
            wd = [self.wget(bidx[(l, 'down', g, hbk)]) for hbk in range(2)]
            if g == 2 and l + 1 < L:
                o_keep = R.ptr
                R.ptr = 0
                cfD_n = R.alloc("cfD_n", [128, 4 * 31, 128], BF16)
                R.ptr = o_keep
                assert o_xnf == 0
                self.build_diag(l + 1, cfD_n, None, identf, pvc, do_cf=True, do_rg=False)
                self.cf_prebuilt.add(l + 1)
                self.cf_handle[l + 1] = cfD_n
            if l == L - 1 and g == 2:
                o_keep = R.ptr
                R.ptr = o_xnf
                ys = [R.alloc("f_y%d" % i, [128, KC, NC_], F32) for i in range(2)]
                R.ptr = o_keep
                yv = env['yv']
                for ti, (c0, N) in enumerate(tiles):
                    for oc in range(KC):
                        bk = self.bank()
                        self.mm(bk[:, 0:N], [(wd[kcl // 4][:, kcl % 4, oc * 128:(oc + 1) * 128],
                                              hb[:, kcl, c0:c0 + N]) for kcl in range(nch)])
                        self.tt(X[:, oc, c0:c0 + N], X[:, oc, c0:c0 + N], bk[:, 0:N], ADD)
                    y = ys[ti % 2]
                    self.norm(lambda kc: X[:, kc, c0:c0 + N], lambda kc: pvc(PV_GFIN + kc),
                              lambda kc: y[:, kc, 0:N], N, sq, rstd)
                    trk.dma('sp', yv[:, :, c0:c0 + N], y[:, :, 0:N], 'st_y%d' % (ti % 2))
                self.final_done = True
            else:
              for oc in range(KC):
                for (c0, N) in tiles:
                    bk = self.bank()
                    self.mm(bk[:, 0:N], [(wd[kcl // 4][:, kcl % 4, oc * 128:(oc + 1) * 128], hb[:, kcl, c0:c0 + N])
                                         for kcl in range(nch)])
                    self.tt(X[:, oc, c0:c0 + N], X[:, oc, c0:c0 + N], bk[:, 0:N], ADD)
            for hbk in range(2):
                self.wrel(bidx[(l, 'down', g, hbk)])

        self.ck('C%d' % l)

    def final_norm(self, env):
        trk = self.trk
        R = env['R']; X = env['X']; pv = env['pv']; yv = env['yv']
        R.reset()
        NF = 512
        sqs = [R.alloc("f_sq%d" % i, [128, KC, NF], BF16) for i in range(2)]
        rstds = [R.alloc("f_rstd%d" % i, [128, NF], F32) for i in range(2)]
        ys = [R.alloc("f_y%d" % i, [128, KC, NF], F32) for i in range(2)]
        for ti, c0 in enumerate(range(0, T, NF)):
            N = min(NF, T - c0)
            y = ys[ti % 2]
            self.norm(lambda kc: X[:, kc, c0:c0 + N], lambda kc: pv[:, PV_GFIN + kc:PV_GFIN + kc + 1],
                      lambda kc: y[:, kc, 0:N], N, sqs[ti % 2], rstds[ti % 2])
            trk.dma('sp', yv[:, :, c0:c0 + N], y[:, :, 0:N], 'st_y%d' % (ti % 2))


_CACHE = {}


def _get_nc():
    if 'nc' not in _CACHE:
        _CACHE['nc'] = Builder().build()
    return _CACHE['nc']


def _pack_pv(inp):
    pv = np.zeros((128, PV_N), np.float32)

    def feat(v):
        return np.ascontiguousarray(v.reshape(L, KC, 128).transpose(2, 0, 1)).reshape(128, L * KC)

    def chan(v):
        return np.ascontiguousarray(v.reshape(L, 4, 128).transpose(2, 0, 1)).reshape(128, L * 4)

    pv[:, PV_GMIX:PV_GMIX + 16] = feat(inp['norm_mix_g'])
    pv[:, PV_GATT:PV_GATT + 16] = feat(inp['norm_attn_g'])
    pv[:, PV_GFFN:PV_GFFN + 16] = feat(inp['norm_ffn_g'])
    pv[:, PV_GMEM:PV_GMEM + 16] = feat(inp['norm_mem_g'])
    pv[:, PV_GFIN:PV_GFIN + 8] = inp['norm_final_g'].reshape(KC, 128).T
    rgw = inp['rg_conv_w'].reshape(L, 4, 4, 128).transpose(3, 0, 2, 1)
    pv[:, PV_RGW:PV_RGW + 32] = np.ascontiguousarray(rgw).reshape(128, 32)
    pv[:, PV_RGB:PV_RGB + 8] = chan(inp['rg_conv_b'])
    pv[:, PV_BA:PV_BA + 8] = chan(inp['rg_ba'])
    pv[:, PV_BX:PV_BX + 8] = chan(inp['rg_bx'])
    pv[:, PV_LAM:PV_LAM + 8] = chan(inp['rg_lambda'])
    cfw = inp['cf_conv_w'].reshape(L, 31, 4, 128).transpose(3, 0, 2, 1)
    pv[:, PV_CFW:PV_CFW + 248] = np.ascontiguousarray(cfw).reshape(128, 248)
    pv[:, PV_CFB:PV_CFB + 8] = chan(inp['cf_conv_b'])
    pv[:, PV_LNG:PV_LNG + 8] = chan(inp['cf_ln_g'])
    pv[:, PV_LNB:PV_LNB + 8] = chan(inp['cf_ln_b'])
    return pv


def _prep(inp):
    inp = {k: np.asarray(v) for k, v in inp.items()}
    f32 = np.float32
    pv = _pack_pv(inp)
    shared = {
        'pv': pv,
        'rg_wa': np.ascontiguousarray(inp['rg_wa'], f32), 'rg_wx': np.ascontiguousarray(inp['rg_wx'], f32),
        'w_in': np.ascontiguousarray(inp['w_in'], f32), 'w_out': np.ascontiguousarray(inp['w_out'], f32),
        'w_q': np.ascontiguousarray(inp['w_q'], f32), 'w_k': np.ascontiguousarray(inp['w_k'], f32),
        'w_v': np.ascontiguousarray(inp['w_v'], f32), 'w_o': np.ascontiguousarray(inp['w_o'], f32),
        'w_gate': np.ascontiguousarray(inp['w_gate'], f32), 'w_up': np.ascontiguousarray(inp['w_up'], f32),
        'w_down': np.ascontiguousarray(inp['w_down'], f32),
    }
    in_maps = []
    for c in range(NCORES):
        sl = slice(16 * c, 16 * (c + 1))
        xs = inp['x_sample'][sl].reshape(NSM, D)
        xT = np.ascontiguousarray(np.concatenate([inp['x_prompt'][c].T, xs.T], axis=1), f32)
        memT = np.ascontiguousarray(inp['mem_prompt'][c].T, f32)
        KcT = np.ascontiguousarray(inp['cache_mem_k'][:, sl].reshape(L, 16, 256, D).transpose(0, 1, 3, 2), f32)
        Vc = np.ascontiguousarray(inp['cache_mem_v'][:, sl].reshape(L, 16, 256, D), f32)
        h0 = np.ascontiguousarray(inp['state_rglru_h'][:, sl].reshape(L, 16, 4, 128).transpose(0, 3, 2, 1), f32)
        rgs = np.ascontiguousarray(
            inp['state_rglru_conv'][:, sl].reshape(L, 16, 3, 4, 128).transpose(0, 4, 3, 1, 2), f32)
        cfs = np.ascontiguousarray(
            inp['state_conf_conv'][:, sl].reshape(L, 16, 30, 4, 128).transpose(0, 4, 3, 1, 2), f32)
        m = dict(shared)
        m.update({'xT': xT, 'memT': memT, 'KcT': KcT, 'Vc': Vc, 'h0': h0, 'rgs': rgs, 'cfs': cfs})
        in_maps.append(m)
    return in_maps


def kernel(**inp):
    nc = _get_nc()
    in_maps = _prep(inp)
    res = run_bass_kernel_spmd(nc, in_maps, core_ids=list(range(NCORES)))
    return _post(res.results)


def _post(rs):
    f32 = np.float32
    B = NCORES
    y_prompt = np.empty((B, NPR, D), f32)
    y_sample = np.empty((128, 4, D), f32)
    p_h = np.empty((L, B, 512), f32)
    p_rg = np.empty((L, B, 3, 512), f32)
    p_cf = np.empty((L, B, 30, 512), f32)
    p_mk = np.empty((L, B, 256, 4, 256), f32)
    p_mv = np.empty((L, B, 256, 4, 256), f32)
    s_h = np.empty((L, 128, 512), f32)
    s_rg = np.empty((L, 128, 3, 512), f32)
    s_cf = np.empty((L, 128, 30, 512), f32)
    for c in range(NCORES):
        r = rs[c]
        sl = slice(16 * c, 16 * (c + 1))
        yT = r['yT']
        y_prompt[c] = yT[:, :NPR].T
        y_sample[sl] = yT[:, NPR:].T.reshape(16, 4, D)
        p_h[:, c] = r['o_ph'].transpose(0, 2, 1).reshape(L, 512)
        p_rg[:, c] = r['o_prg'].transpose(0, 3, 2, 1).reshape(L, 3, 512)
        p_cf[:, c] = r['o_pcf'].transpose(0, 3, 2, 1).reshape(L, 30, 512)
        p_mk[:, c] = r['o_pk'].transpose(0, 3, 2, 1).reshape(L, 256, 4, 256)
        p_mv[:, c] = r['o_pv'].reshape(L, 256, 4, 256)
        s_h[:, sl] = r['o_sh'].transpose(0, 3, 2, 1).reshape(L, 16, 512)
        s_rg[:, sl] = r['o_srg'].transpose(0, 3, 4, 2, 1).reshape(L, 16, 3, 512)
        scf = np.concatenate([r['o_scfh'], r['o_scfn']], axis=4)
        s_cf[:, sl] = scf.transpose(0, 3, 4, 2, 1).reshape(L, 16, 30, 512)
    return (y_prompt, y_sample, p_h, p_rg, p_cf, p_mk, p_mv, s_h, s_rg, s_cf)
```

```python
import itertools
import numpy as np
import concourse.bass as bass
import concourse.mybir as mybir
from concourse.bass_utils import run_bass_kernel_spmd

F32 = mybir.dt.float32
BF16 = mybir.dt.bfloat16
U8 = mybir.dt.uint8
AF = mybir.ActivationFunctionType
ALU = mybir.AluOpType

NCORES = 8
L = 2
D = 1024
KC = 8
NPR = 2048
NSM = 64
T = NPR + NSM
DFF = 2816
EPS = 1e-6
GRAN = 128
PSUM_BASE = 1 << 24
RING_SLOTS = 6
SLOT_BYTES = 8192

PV_GMIX, PV_GATT, PV_GFFN, PV_GMEM, PV_GFIN = 0, 16, 32, 48, 64
PV_RGW, PV_RGB, PV_BA, PV_BX, PV_LAM = 72, 104, 112, 120, 128
PV_CFW, PV_CFB, PV_LNG, PV_LNB = 136, 384, 392, 400
PV_ZERO = 408
PV_N = 416


def _esz(dt):
    return 4 if dt == F32 else (2 if dt == BF16 else 1)


class Trk:
    def __init__(self, nc):
        self.nc = nc
        self.E = {'pe': nc.tensor, 'act': nc.scalar, 'dve': nc.vector, 'pool': nc.gpsimd, 'sp': nc.sync}
        self.sems = {}
        self.val = {}
        self.waited = {e: {} for e in self.E}
        self.lastw = {}
        self.rd = {}
        self.base = {}

    def reg(self, handle, addr):
        self.base[handle.name] = addr

    def sem(self, key):
        if key not in self.sems:
            self.sems[key] = self.nc.alloc_semaphore('s_' + key)
            self.val[key] = 0
        return self.sems[key]

    def grans(self, aps):
        out = set()
        for ap in aps:
            nm = ap.tensor.name
            if nm not in self.base:
                continue
            base = self.base[nm]
            es = _esz(ap.dtype)
            pat = ap.ap
            rowstride = pat[0][0]
            off = int(ap.offset)
            if rowstride > 0:
                off = off % rowstride
            free = pat[1:]
            if len(free) == 0:
                rngs = [(off, off + 1)]
            else:
                ist, icnt = free[-1]
                ilen = (icnt - 1) * abs(ist) + 1
                outer = free[:-1]
                rngs = []
                for idx in itertools.product(*[range(c) for (_, c) in outer]):
                    st = off + sum(i * s for i, (s, _) in zip(idx, outer))
                    rngs.append((st, st + ilen))
            for lo, hi in rngs:
                blo = base + lo * es
                bhi = base + hi * es
                for g in range(blo // GRAN, (bhi - 1) // GRAN + 1):
                    out.add(g)
        return out

    def _deps(self, rg, wg):
        need = {}
        for g in rg:
            w = self.lastw.get(g)
            if w is not None and need.get(w[0], 0) < w[1]:
                need[w[0]] = w[1]
        for g in wg:
            w = self.lastw.get(g)
            if w is not None and need.get(w[0], 0) < w[1]:
                need[w[0]] = w[1]
            r = self.rd.get(g)
            if r:
                for k, v in r.items():
                    if need.get(k, 0) < v:
                        need[k] = v
        return need

    def _commit(self, rg, wg, key, val):
        for g in rg:
            d = self.rd.get(g)
            if d is None:
                self.rd[g] = {key: val}
            else:
                d[key] = val
        for g in wg:
            self.lastw[g] = (key, val)
            self.rd[g] = None

    def _wait(self, e, need):
        eng = self.E[e]
        wd = self.waited[e]
        for k, v in need.items():
            if e == 'pe' and k == 'pe':
                continue
            if k not in self.E:
                v = self.val[k]
            if wd.get(k, 0) < v:
                eng.wait_ge(self.sems[k], v)
                wd[k] = v

    def op(self, e, reads, writes, fn):
        rg = self.grans(reads)
        wg = self.grans(writes)
        self._wait(e, self._deps(rg, wg))
        ins = fn()
        s = self.sem(e)
        self.val[e] += 1
        ins.then_inc(s, 1)
        self._commit(rg, wg, e, self.val[e])

    def dma(self, q, out, in_, key):
        reads = [in_] if in_.tensor.name in self.base else []
        writes = [out] if out.tensor.name in self.base else []
        rg = self.grans(reads)
        wg = self.grans(writes)
        self._wait(q, self._deps(rg, wg))
        s = self.sem(key)
        ins = self.E[q].dma_start(out=out, in_=in_)
        self.val[key] += 16
        ins.then_inc(s, 16)
        self._commit(rg, wg, key, self.val[key])


class Region:
    def __init__(self, nc, trk, base, size):
        self.nc, self.trk, self.base, self.size = nc, trk, base, size
        self.ptr = 0

    def reset(self):
        self.ptr = 0

    def alloc(self, name, shape, dt):
        n = 1
        for s in shape[1:]:
            n *= s
        nbytes = n * _esz(dt)
        nbytes = (nbytes + GRAN - 1) // GRAN * GRAN
        assert self.ptr + nbytes <= self.size, (name, self.ptr, nbytes, self.size)
        addr = self.base + self.ptr
        t = self.nc.alloc_sbuf_tensor_at(name, list(shape), dt, offset=addr)
        self.trk.reg(t, addr)
        self.ptr += nbytes
        return t


class _Stop(Exception):
    pass


_STOP = None


class Builder:
    def ck(self, name):
        if _STOP == name:
            raise _Stop()

    def __init__(self):
        self.nc = bass.Bass("TRN2", target_bir_lowering=False)
        self.trk = Trk(self.nc)
        self.pb = 0
        self.cf_prebuilt = set()
        self.cf_handle = {}

    def bank(self):
        self.pb = (self.pb + 1) % 8
        return self.ps[self.pb]

    def mm(self, out, pairs, extra_reads=()):
        reads = []
        for a, b in pairs:
            reads.append(a)
            reads.append(b)
        n = len(pairs)

        def fn():
            ins = None
            for i, (a, b) in enumerate(pairs):
                ins = self.nc.tensor.matmul(out, lhsT=a, rhs=b, start=(i == 0), stop=(i == n - 1))
            return ins
        self.trk.op('pe', reads, [out], fn)

    def mm_groups(self, groups):
        reads, writes = [], []
        for out, pairs in groups:
            writes.append(out)
            for a, b in pairs:
                reads.append(a)
                reads.append(b)

        def fn():
            ins = None
            for out, pairs in groups:
                n = len(pairs)
                for i, (a, b) in enumerate(pairs):
                    ins = self.nc.tensor.matmul(out, lhsT=a, rhs=b, start=(i == 0), stop=(i == n - 1))
            return ins
        self.trk.op('pe', reads, writes, fn)

    def act(self, out, in_, func, bias=None, scale=None):
        reads = [in_]
        kw = {}
        if bias is not None:
            kw['bias'] = bias
            if not isinstance(bias, (int, float)):
                reads.append(bias)
        if scale is not None:
            kw['scale'] = scale
            if not isinstance(scale, (int, float)):
                reads.append(scale)
        self.trk.op('act', reads, [out], lambda: self.nc.scalar.activation(out=out, in_=in_, func=func, **kw))

    def tt(self, out, in0, in1, op, eng='dve'):
        e = self.trk.E[eng]
        self.trk.op(eng, [in0, in1], [out], lambda: e.tensor_tensor(out=out, in0=in0, in1=in1, op=op))

    def ts(self, out, in0, s1, s2, op0, op1=None, eng='dve'):
        e = self.trk.E[eng]
        reads = [in0]
        if not isinstance(s1, (int, float)):
            reads.append(s1)
        if s2 is not None and not isinstance(s2, (int, float)):
            reads.append(s2)
        if op1 is None:
            self.trk.op(eng, reads, [out], lambda: e.tensor_scalar(out=out, in0=in0, scalar1=s1, scalar2=None, op0=op0))
        else:
            self.trk.op(eng, reads, [out], lambda: e.tensor_scalar(out=out, in0=in0, scalar1=s1, scalar2=s2, op0=op0, op1=op1))

    def stt(self, out, in0, scalar, in1, op0, op1):
        reads = [in0, in1]
        if not isinstance(scalar, (int, float)):
            reads.append(scalar)
        self.trk.op('dve', reads, [out], lambda: self.nc.vector.scalar_tensor_tensor(
            out=out, in0=in0, scalar=scalar, in1=in1, op0=op0, op1=op1))

    def scan(self, out, d0, d1, initial):
        reads = [d0, d1]
        if not isinstance(initial, (int, float)):
            reads.append(initial)
        self.trk.op('dve', reads, [out], lambda: self.nc.vector.tensor_tensor_scan(
            out=out, data0=d0, data1=d1, initial=initial, op0=ALU.mult, op1=ALU.add))

    def recip(self, out, in_):
        self.trk.op('dve', [in_], [out], lambda: self.nc.vector.reciprocal(out=out, in_=in_))

    def copy(self, out, in_, eng='dve'):
        e = self.trk.E[eng]
        self.trk.op(eng, [in_], [out], lambda: e.tensor_copy(out=out, in_=in_))

    def memset(self, out, v, eng='dve'):
        e = self.trk.E[eng]
        self.trk.op(eng, [], [out], lambda: e.memset(out, v))

    def ring_plan(self, blocks):
        self.blocks = blocks
        self.blk_view = [None] * len(blocks)
        self.blk_emitted = 0
        self.blk_released = [False] * len(blocks)

    def _ring_emit(self, i):
        ap = self.blocks[i]
        a, b = ap.shape[1], ap.shape[2]
        slot = i % RING_SLOTS
        v = self.ring[slot][:, 0:a * b].rearrange("p (a b) -> p a b", a=a)
        self.trk.dma('pool', v, ap, 'ring%d' % slot)
        self.blk_view[i] = v

    def _ring_pump(self, upto=None):
        while self.blk_emitted < len(self.blocks):
            j = self.blk_emitted
            if upto is not None and j <= upto:
                pass
            elif j >= RING_SLOTS and not self.blk_released[j - RING_SLOTS]:
                break
            if j >= RING_SLOTS:
                assert self.blk_released[j - RING_SLOTS], "ring overflow: block %d" % j
            self._ring_emit(j)
            self.blk_emitted += 1

    def wget(self, i):
        self._ring_pump(upto=i)
        return self.blk_view[i]

    def wrel(self, i):
        self.blk_released[i] = True
        self._ring_pump()

    def norm(self, xin, gvec, out_fn, N, sq, rstd):
        for kc in range(KC):
            self.act(sq[:, kc, 0:N], xin(kc), AF.Square)
        bk = self.bank()
        self.mm(bk[:, 0:N], [(self.c1024[:], sq[:, kc, 0:N]) for kc in range(KC)])
        self.act(rstd[:, 0:N], bk[:, 0:N], AF.Ln, bias=EPS)
        self.act(rstd[:, 0:N], rstd[:, 0:N], AF.Exp, scale=-0.5)
        for kc in range(KC):
            self.stt(out_fn(kc), xin(kc), gvec(kc), rstd[:, 0:N], ALU.mult, ALU.mult)

    def build(self):
        nc = self.nc
        trk = self.trk
        dr = {}

        def din(name, shape):
            dr[name] = nc.dram_tensor(name, list(shape), F32, kind="ExternalInput").ap()
            return dr[name]

        def dout(name, shape):
            dr[name] = nc.dram_tensor(name, list(shape), F32, kind="ExternalOutput").ap()
            return dr[name]

        xT = din("xT", [D, T])
        memT = din("memT", [D, 256])
        KcT = din("KcT", [L, 16, D, 256])
        Vc = din("Vc", [L, 16, 256, D])
        h0 = din("h0", [L, 128, 4, 16])
        rgs = din("rgs", [L, 128, 4, 16, 3])
        cfs = din("cfs", [L, 128, 4, 16, 30])
        pvd = din("pv", [128, PV_N])
        rg_wa = din("rg_wa", [L, 8, 64, 64])
        rg_wx = din("rg_wx", [L, 8, 64, 64])
        w_in = din("w_in", [L, D, 2048])
        w_out = din("w_out", [L, D, D])
        w_q = din("w_q", [L, D, D])
        w_k = din("w_k", [L, D, D])
        w_v = din("w_v", [L, D, D])
        w_o = din("w_o", [L, D, D])
        w_gate = din("w_gate", [L, D, DFF])
        w_up = din("w_up", [L, D, DFF])
        w_down = din("w_down", [L, DFF, D])

        yT = dout("yT", [D, T])
        o_ph = dout("o_ph", [L, 128, 4])
        o_prg = dout("o_prg", [L, 128, 4, 3])
        o_pcf = dout("o_pcf", [L, 128, 4, 30])
        o_pk = dout("o_pk", [L, 128, 8, 256])
        o_pv = dout("o_pv", [L, 256, D])
        o_sh = dout("o_sh", [L, 128, 4, 16])
        o_srg = dout("o_srg", [L, 128, 4, 16, 3])
        o_scfh = dout("o_scfh", [L, 128, 4, 16, 26])
        o_scfn = dout("o_scfn", [L, 128, 4, 16, 4])

        ARENA = 212800
        arena = nc.alloc_sbuf_tensor("arena", [128, ARENA], U8)
        abase = nc.lookup_mloc(arena).addr
        P = Region(nc, trk, abase, ARENA)
        X = P.alloc("X", [128, KC, T], F32)
        identf = P.alloc("identf", [128, 128], F32)
        ident = P.alloc("ident", [128, 128], BF16)
        self.c1024 = P.alloc("c1024", [128, 128], BF16)
        c512 = P.alloc("c512", [128, 128], BF16)
        ones = P.alloc("ones", [128, 128], BF16)
        pv = P.alloc("pvs", [128, PV_N], F32)
        cl = P.alloc("cl", [128, L * 4], F32)
        cl2 = P.alloc("cl2", [128, L * 4], F32)
        wabd = P.alloc("wabd", [128, L * 2 * 4, 128], BF16)
        hstate = P.alloc("hstate", [128, 4], F32)
        self.ring = [P.alloc("ring%d" % i, [128, SLOT_BYTES // 2], BF16) for i in range(RING_SLOTS)]
        rbase = abase + P.ptr
        R = Region(nc, trk, rbase, ARENA - P.ptr)
        self.ps = []
        for i in range(8):
            t = nc.alloc_psum_tensor("ps%d" % i, [128, 512], F32)
            trk.reg(t, PSUM_BASE + i * 2048)
            self.ps.append(t)

        blocks = []
        bidx = {}

        def wv(w, l):
            return w[l].rearrange("(kc p) n -> p kc n", p=128)

        for l in range(L):
            for c in range(4):
                bidx[(l, 'in', c)] = len(blocks)
                blocks.append(wv(w_in, l)[:, :, c * 512:(c + 1) * 512])
            for c in range(2):
                bidx[(l, 'out', c)] = len(blocks)
                blocks.append(wv(w_out, l)[:, :, c * 512:(c + 1) * 512])
            for nm, w in (('k', w_k), ('v', w_v), ('q', w_q), ('o', w_o)):
                for c in range(2):
                    bidx[(l, nm, c)] = len(blocks)
                    blocks.append(wv(w, l)[:, :, c * 512:(c + 1) * 512])
            for g in range(3):
                ncols = 1024 if g < 2 else 768
                c0 = g * 1024
                for hb in range(2):
                    lo = c0 + hb * 512
                    hi = min(c0 + ncols, lo + 512)
                    bidx[(l, 'gate', g, hb)] = len(blocks)
                    blocks.append(wv(w_gate, l)[:, :, lo:hi])
                    bidx[(l, 'up', g, hb)] = len(blocks)
                    blocks.append(wv(w_up, l)[:, :, lo:hi])
                nch = ncols // 128
                for hb in range(2):
                    k0 = hb * 4
                    k1 = min(nch, k0 + 4)
                    bidx[(l, 'down', g, hb)] = len(blocks)
                    blocks.append(w_down[l, c0 + k0 * 128:c0 + k1 * 128, :].rearrange("(kc p) n -> p kc n", p=128))
        self.ring_plan(blocks)

        A = self.act
        xv = xT.rearrange("(kc p) n -> p kc n", p=128)
        yv = yT.rearrange("(kc p) n -> p kc n", p=128)

        def pvc(off, n=1):
            return pv[:, off:off + n]

        with nc.Block():
            trk.dma('sp', pv[:], pvd[:, :], 'ld_pv')
            xtiles = [(ti, c0, min(512, T - c0)) for ti, c0 in enumerate(range(0, T, 512))]
            ti, c0, n = xtiles[0]
            trk.dma('sp', X[:, :, c0:c0 + n], xv[:, :, c0:c0 + n], 'ld_x%d' % ti)
            self.memset(identf[:], 0.0, eng='dve')
            trk.op('pool', [identf[:]], [identf[:]], lambda: nc.gpsimd.affine_select(
                out=identf[:], in_=identf[:], pattern=[[-1, 128]], compare_op=ALU.not_equal, fill=1.0,
                base=0, channel_multiplier=1))
            self.copy(ident[:], identf[:])
            self.memset(self.c1024[:], 1.0 / 1024.0)
            self.memset(c512[:], 1.0 / 512.0)
            self.memset(ones[:], 1.0)
            R.reset()
            R.ptr = 71936
            wst = R.alloc("wst", [128, L * 2 * 4, 128], F32)
            self.memset(wst[:], 0.0)
            for l in range(L):
                for gi, wsrc in enumerate((rg_wa, rg_wx)):
                    for hh in range(8):
                        j, e = hh // 2, hh % 2
                        trk.dma('sp', wst[e * 64:(e + 1) * 64, (l * 2 + gi) * 4 + j, e * 64:(e + 1) * 64],
                                wsrc[l, hh, :, :], 'ld_w%d' % ((l * 2 + gi) % 2))
            self.deferred = lambda: self.copy(wabd[:], wst[:])
            for ti, c0, n in xtiles[1:]:
                trk.dma('sp', X[:, :, c0:c0 + n], xv[:, :, c0:c0 + n], 'ld_x%d' % ti)
            for l in range(L):
                for j in range(4):
                    trk.dma('sp', o_scfh[l, :, j], cfs[l, :, j, :, 4:30], 'st_h')
            spt = R.alloc("spt", [128, L * 4], F32)
            A(spt[:], pvc(PV_LAM, 8), AF.Exp, scale=-1.0)
            A(spt[:], spt[:], AF.Ln, bias=1.0)
            self.ts(cl[:], spt[:], -8.0, None, ALU.mult)
            self.ts(cl2[:], spt[:], -16.0, None, ALU.mult)

            try:
                self.ck('P')
                for l in range(L):
                    self.layer(l, locals())
                if not getattr(self, 'final_done', False):
                    self.final_norm(locals())
            except _Stop:
                pass
            for k, v in trk.val.items():
                if k not in trk.E and v > 0:
                    nc.sync.wait_ge(trk.sems[k], v)
        return nc

    def build_diag(self, l, cfD, rgD, identf, pvc, do_cf=True, do_rg=True):
        MUL = ALU.mult
        for j in range(4):
            if do_rg:
                for k in range(4):
                    self.ts(rgD[:, j * 4 + k, :], identf[:], pvc(PV_RGW + (l * 4 + j) * 4 + k), None, MUL)
            if do_cf:
                for k in range(31):
                    if k % 3 == 2:
                        self.act(cfD[:, j * 31 + k, :], identf[:], AF.Copy,
                                 scale=pvc(PV_CFW + (l * 4 + j) * 31 + k))
                    else:
                        self.ts(cfD[:, j * 31 + k, :], identf[:], pvc(PV_CFW + (l * 4 + j) * 31 + k), None, MUL)

    def layer(self, l, env):
        nc, trk = self.nc, self.trk
        R = env['R']; X = env['X']; pv = env['pv']; bidx = env['bidx']
        ident, identf, c512, ones = env['ident'], env['identf'], env['c512'], env['ones']
        cl, cl2, wabd, hstate = env['cl'], env['cl2'], env['wabd'], env['hstate']
        dr_ = env['dr']
        A = self.act
        MUL, ADD, SUB = ALU.mult, ALU.add, ALU.subtract

        def pvc(off, n=1):
            return pv[:, off:off + n]

        R.reset()
        cfD = R.alloc("cfD", [128, 4 * 31, 128], BF16)
        if l in self.cf_handle:
            cfD = self.cf_handle[l]
        rgD = R.alloc("rgD", [128, 4 * 4, 128], BF16)
        NA = 256
        ycat = R.alloc("a_ycat", [128, KC, NA], BF16)
        sq = R.alloc("a_sq", [128, KC, NA], BF16)
        rstd = R.alloc("a_rstd", [128, NA], F32)
        xn = R.alloc("a_xn", [128, KC, NA], BF16)
        o_x = R.ptr
        xrb0 = R.alloc("a_xrb0", [128, 4, 3 + NA], BF16)
        cb0 = R.alloc("a_cb0", [128, 4, 30 + NA], BF16)
        o_end = R.ptr
        R.ptr = o_x
        csb = R.alloc("a_csb", [128, 4, 16, 34], BF16)
        assert R.ptr <= o_end
        R.ptr = o_end
        xrb1 = R.alloc("a_xrb1", [128, 4, 3 + NA], BF16)
        cb1 = R.alloc("a_cb1", [128, 4, 30 + NA], BF16)
        xrb2 = [xrb0, xrb1]
        cb2 = [cb0, cb1]
        gg2 = [R.alloc("a_gg%d" % i, [128, 4, NA], BF16) for i in range(2)]
        sg = R.alloc("a_sg", [128, NA], F32)
        xc2 = [R.alloc("a_xc%d" % i, [128, NA], F32) for i in range(2)]
        xcb = R.alloc("a_xcb", [128, NA], BF16)
        r_ = R.alloc("a_r", [128, NA], F32)
        i_ = R.alloc("a_i", [128, NA], F32)
        a_ = R.alloc("a_a", [128, NA], F32)
        m_ = R.alloc("a_m", [128, NA], F32)
        b_ = R.alloc("a_b", [128, NA], F32)
        h_ = R.alloc("a_h", [128, NA], F32)
        ccf = R.alloc("a_ccf", [128, 4, NA], F32)
        ccb = R.alloc("a_ccb", [128, 4, NA], BF16)
        sqb = R.alloc("a_sqb", [128, 4, NA], BF16)
        mean = R.alloc("a_mean", [128, NA], F32)
        var = R.alloc("a_var", [128, NA], F32)
        rstc = R.alloc("a_rstc", [128, NA], F32)
        o_dd = R.ptr
        dd = R.alloc("a_dd", [128, NA], F32)
        dd2 = [dd, sg]
        prg = R.alloc("a_prg", [128, 4, 3], F32)
        pcf = R.alloc("a_pcf", [128, 4, 30], F32)
        xrs = R.alloc("a_xrs", [128, 4, 16, 7], BF16)
        o_keep = R.ptr
        R.ptr = o_dd
        xrsf = R.alloc("a_xrsf", [128, 4, 16, 3], F32)
        R.ptr = o_keep
        srg = R.alloc("a_srg", [128, 4, 16, 3], F32)
        cs4 = R.alloc("a_cs4", [128, 4, 16, 4], F32)
        h0s = R.alloc("a_h0s", [128, 4, 16], F32)
        shl = R.alloc("a_shl", [128, 4, 16], F32)
        tmp16 = R.alloc("a_tmp16", [128, 16], F32)
        nb2 = R.alloc("a_nb2", [128, 8], F32)

        self.build_diag(l, cfD, rgD, identf, pvc, do_cf=(l not in self.cf_prebuilt))
        if getattr(self, 'deferred', None) is not None:
            self.deferred()
            self.deferred = None
        self.ts(nb2[:, 0:4], pvc(PV_BA + l * 4, 4), -1.0, None, MUL)
        self.ts(nb2[:, 4:8], pvc(PV_BX + l * 4, 4), -1.0, None, MUL)
        self.memset(xrb0[:, :, 0:3], 0.0)
        self.memset(cb0[:, :, 0:30], 0.0)
        self.memset(hstate[:], 0.0)

        win = [self.wget(bidx[(l, 'in', c)]) for c in range(4)]
        wout = [self.wget(bidx[(l, 'out', c)]) for c in range(2)]
        tiles = [(c0, NA, False) for c0 in range(0, NPR, NA)] + [(NPR, NSM, True)]
        NT = len(tiles)

        def load_sample_state():
            trk.dma('sp', xrsf[:], dr_['rgs'][l], 'ld_st0')
            trk.dma('sp', h0s[:], dr_['h0'][l], 'ld_st1')
            for j in range(4):
                trk.dma('pool', csb[:, j, :, 0:30], dr_['cfs'][l, :, j], 'ld_st%d' % (2 + j))
            self.copy(xrs[:, :, :, 0:3], xrsf[:])

        def v3(ap):
            return ap.rearrange("p (b t) -> p b t", t=4)

        def front(ti):
            c0, N, smp = tiles[ti]
            sset = ti % 2
            xrb, cb, gg = xrb2[sset], cb2[sset], gg2[sset]
            last_p = (not smp) and (c0 + N == NPR)
            if smp:
                load_sample_state()
            elif ti >= 1:
                self.copy(xrb[:, :, 0:3], xrb2[1 - sset][:, :, NA:NA + 3])
                self.copy(cb[:, :, 0:30], cb2[1 - sset][:, :, NA:NA + 30])
            for kc in range(KC):
                A(sq[:, kc, 0:N], X[:, kc, c0:c0 + N], AF.Square)
            bkn = self.bank()
            self.mm(bkn[:, 0:N], [(self.c1024[:], sq[:, kc, 0:N]) for kc in range(KC)])
            A(rstd[:, 0:N], bkn[:, 0:N], AF.Ln, bias=EPS)
            A(rstd[:, 0:N], rstd[:, 0:N], AF.Exp, scale=-0.5)
            yield
            for kc in range(KC):
                self.stt(xn[:, kc, 0:N], X[:, kc, c0:c0 + N], pvc(PV_GMIX + l * 8 + kc), rstd[:, 0:N], MUL, MUL)
            yield

            def proj(blk, j):
                bk = self.bank()
                self.mm(bk[:, 0:N], [(win[blk][:, kc, j * 128:(j + 1) * 128], xn[:, kc, 0:N]) for kc in range(KC)])
                return bk
            yield
            for j in range(4):
                bk = proj(1, j)
                A(gg[:, j, 0:N], bk[:, 0:N], AF.Gelu_apprx_tanh)
            yield
            for j in range(4):
                bk = proj(0, j)
                zc = pvc(PV_ZERO)
                if smp:
                    self.ts(xrs[:, j, :, 3:7], v3(bk[:, 0:N]), zc, None, ADD)
                    self.ts(srg[:, j, :, :], v3(bk[:, 0:N])[:, :, 1:4], zc, None, ADD)
                else:
                    self.ts(xrb[:, j, 3:3 + N], bk[:, 0:N], zc, None, ADD)
                    if last_p:
                        self.ts(prg[:, j, :], bk[:, N - 3:N], zc, None, ADD)
            yield
            yield
            for half in range(1):
                for j in range(4):
                    bv = proj(2, j)
                    bg = proj(3, j)
                    A(sg[:, 0:N], bg[:, 0:N], AF.Sigmoid)
                    if smp:
                        self.tt(cs4[:, j, :, :], v3(bv[:, 0:N]), v3(sg[:, 0:N]), MUL)
                        self.copy(csb[:, j, :, 30:34], cs4[:, j, :, :])
                    else:
                        self.tt(cb[:, j, 30:30 + N], bv[:, 0:N], sg[:, 0:N], MUL)
                        if last_p:
                            self.tt(pcf[:, j, :], bv[:, N - 30:N], sg[:, N - 30:N], MUL)
                yield

        def back(ti):
            c0, N, smp = tiles[ti]
            sset = ti % 2
            xrb, cb, gg = xrb2[sset], cb2[sset], gg2[sset]

            def tail(j):
                xc = xc2[j % 2]
                self.tt(b_[:, 0:N], i_[:, 0:N], xc[:, 0:N], MUL)
                self.tt(b_[:, 0:N], b_[:, 0:N], m_[:, 0:N], MUL)
                if smp:
                    av = v3(a_[:, 0:N])
                    bvw = v3(b_[:, 0:N])
                    self.tt(tmp16[:], av[:, :, 0], h0s[:, j, :], MUL)
                    self.tt(bvw[:, :, 0], bvw[:, :, 0], tmp16[:], ADD)
                    self.memset(av[:, :, 0], 0.0)
                    self.scan(h_[:, 0:N], a_[:, 0:N], b_[:, 0:N], 0.0)
                    self.copy(shl[:, j, :], v3(h_[:, 0:N])[:, :, 3])
                else:
                    self.scan(h_[:, 0:N], a_[:, 0:N], b_[:, 0:N], hstate[:, j:j + 1])
                    self.copy(hstate[:, j:j + 1], h_[:, N - 1:N])
                self.tt(ycat[:, j, 0:N], h_[:, 0:N], gg[:, j, 0:N], MUL)

            for j in range(4):
                xc = xc2[j % 2]
                bk = self.bank()
                if smp:
                    self.mm(bk[:, 0:N], [(rgD[:, j * 4 + k, :], xrs[:, j, :, k:k + 4]) for k in range(4)])
                else:
                    self.mm(bk[:, 0:N], [(rgD[:, j * 4 + k, :], xrb[:, j, k:k + N]) for k in range(4)])
                bias = pvc(PV_RGB + l * 4 + j)
                self.ts(xcb[:, 0:N], bk[:, 0:N], bias, None, ADD)
                self.ts(xc[:, 0:N], bk[:, 0:N], bias, None, ADD)
                bc = self.bank()
                if smp:
                    self.mm(bc[:, 0:N], [(cfD[:, j * 31 + k, :], csb[:, j, :, k:k + 4]) for k in range(31)])
                else:
                    self.mm(bc[:, 0:N], [(cfD[:, j * 31 + k, :], cb[:, j, k:k + N]) for k in range(31)])
                ba = self.bank()
                self.mm(ba[:, 0:N], [(wabd[:, (l * 2 + 0) * 4 + j, :], xcb[:, 0:N])])
                bx = self.bank()
                self.mm(bx[:, 0:N], [(wabd[:, (l * 2 + 1) * 4 + j, :], xcb[:, 0:N])])
                cbias = pvc(PV_CFB + l * 4 + j)
                self.ts(ccf[:, j, 0:N], bc[:, 0:N], cbias, None, ADD)
                self.ts(ccb[:, j, 0:N], bc[:, 0:N], cbias, None, ADD)
                self.stt(sqb[:, j, 0:N], bc[:, 0:N], cbias, ccf[:, j, 0:N], ADD, MUL)
                if j == 3:
                    bm = self.bank()
                    self.mm(bm[:, 0:N], [(c512[:], ccb[:, jj, 0:N]) for jj in range(4)])
                    bq = self.bank()
                    self.mm(bq[:, 0:N], [(c512[:], sqb[:, jj, 0:N]) for jj in range(4)])
                    A(mean[:, 0:N], bm[:, 0:N], AF.Copy)
                    self.tt(var[:, 0:N], mean[:, 0:N], mean[:, 0:N], MUL)
                    self.tt(var[:, 0:N], bq[:, 0:N], var[:, 0:N], SUB)
                    A(rstc[:, 0:N], var[:, 0:N], AF.Ln, bias=EPS)
                    A(rstc[:, 0:N], rstc[:, 0:N], AF.Exp, scale=-0.5)
                if j >= 1:
                    tail(j - 1)
                A(r_[:, 0:N], ba[:, 0:N], AF.Exp, scale=-1.0, bias=nb2[:, j:j + 1])
                A(i_[:, 0:N], bx[:, 0:N], AF.Exp, scale=-1.0, bias=nb2[:, 4 + j:5 + j])
                A(r_[:, 0:N], r_[:, 0:N], AF.Ln, bias=1.0)
                A(i_[:, 0:N], i_[:, 0:N], AF.Ln, bias=1.0)
                A(r_[:, 0:N], r_[:, 0:N], AF.Exp, scale=-1.0)
                A(i_[:, 0:N], i_[:, 0:N], AF.Exp, scale=-1.0)
                A(a_[:, 0:N], r_[:, 0:N], AF.Exp, scale=cl[:, l * 4 + j:l * 4 + j + 1])
                A(m_[:, 0:N], r_[:, 0:N], AF.Exp, scale=cl2[:, l * 4 + j:l * 4 + j + 1])
                A(m_[:, 0:N], m_[:, 0:N], AF.Ln, scale=-1.0, bias=1.0000001)
                A(m_[:, 0:N], m_[:, 0:N], AF.Exp, scale=0.5)
                yield
            for j in range(4):
                dd = dd2[j % 2]
                self.tt(dd[:, 0:N], ccf[:, j, 0:N], mean[:, 0:N], SUB)
                self.tt(dd[:, 0:N], dd[:, 0:N], rstc[:, 0:N], MUL)
                A(ycat[:, 4 + j, 0:N], dd[:, 0:N], AF.Silu, bias=pvc(PV_LNB + l * 4 + j),
                  scale=pvc(PV_LNG + l * 4 + j))
            yield
            tail(3)
            yield
            wout_proj(ti)
            yield

        def wout_proj(ti):
            c0, N, smp = tiles[ti]
            for oc in range(KC):
                bk = self.bank()
                self.mm(bk[:, 0:N], [(wout[oc // 4][:, kc, (oc % 4) * 128:(oc % 4 + 1) * 128], ycat[:, kc, 0:N])
                                     for kc in range(KC)])
                self.tt(X[:, oc, c0:c0 + N], X[:, oc, c0:c0 + N], bk[:, 0:N], ADD)

        def drive(gens):
            gens = [g for g in gens if g is not None]
            while gens:
                alive = []
                for g in gens:
                    try:
                        next(g)
                        alive.append(g)
                    except StopIteration:
                        pass
                gens = alive

        drive([front(0)])
        for ti in range(NT):
            drive([front(ti + 1) if ti + 1 < NT else None, back(ti)])
            if ti == NT - 2:
                for c in range(4):
                    self.wrel(bidx[(l, 'in', c)])

        trk.dma('sp', dr_['o_ph'][l], hstate[:], 'st_a0')
        trk.dma('sp', dr_['o_prg'][l], prg[:], 'st_a1')
        trk.dma('sp', dr_['o_pcf'][l], pcf[:], 'st_a2')
        trk.dma('sp', dr_['o_sh'][l], shl[:], 'st_a3')
        trk.dma('sp', dr_['o_srg'][l], srg[:], 'st_a4')
        trk.dma('sp', dr_['o_scfn'][l], cs4[:], 'st_a5')
        for c in range(2):
            self.wrel(bidx[(l, 'out', c)])

        self.ck('A%d' % l)
        R.reset()
        NB = 512
        KpT = R.alloc("b_KpT", [128, KC, 256], BF16)
        Vp = R.alloc("b_Vp", [128, 2, D], BF16)
        Ks = [R.alloc("b_Ks%d" % i, [128, KC, 256], BF16) for i in range(2)]
        Vs = [R.alloc("b_Vs%d" % i, [128, 2, D], BF16) for i in range(2)]
        sq = R.alloc("b_sq", [128, KC, NB], BF16)
        rstd = R.alloc("b_rstd", [128, NB], F32)
        xn = R.alloc("b_xn", [128, KC, NB], BF16)
        sq_s = R.alloc("b_sqs", [128, KC, NSM], BF16)
        rstd_s = R.alloc("b_rstds", [128, NSM], F32)
        xn_s = R.alloc("b_xns", [128, KC, NSM], BF16)
        mark = R.ptr
        memf = R.alloc("b_memf", [128, KC, 256], F32)
        memn = R.alloc("b_memn", [128, KC, 256], BF16)
        msq = R.alloc("b_msq", [128, KC, 256], BF16)
        mrs = R.alloc("b_mrs", [128, 256], F32)
        kst = R.alloc("b_kst", [128, KC, 256], F32)
        vst = R.alloc("b_vst", [128, 2, D], F32)
        trk.dma('sp', memf[:], dr_['memT'].rearrange("(kc p) n -> p kc n", p=128), 'ld_mem')
        self.norm(lambda kc: memf[:, kc, :], lambda kc: pvc(PV_GMEM + l * 8 + kc),
                  lambda kc: memn[:, kc, :], 256, msq, mrs)
        self.norm(lambda kc: X[:, kc, NPR:NPR + NSM], lambda kc: pvc(PV_GATT + l * 8 + kc),
                  lambda kc: xn_s[:, kc, 0:NSM], NSM, sq_s, rstd_s)
        self.norm(lambda kc: X[:, kc, 0:NB], lambda kc: pvc(PV_GATT + l * 8 + kc),
                  lambda kc: xn[:, kc, 0:NB], NB, sq, rstd)
        self.ck('Bn%d' % l)
        wk = [self.wget(bidx[(l, 'k', c)]) for c in range(2)]
        self.ck('Bw%d' % l)
        for dc in range(KC):
            bk = self.bank()
            self.mm(bk[:, 0:256], [(wk[dc // 4][:, kc, (dc % 4) * 128:(dc % 4 + 1) * 128], memn[:, kc, :])
                                   for kc in range(KC)])
            A(KpT[:, dc, :], bk[:, 0:256], AF.Copy)
            A(kst[:, dc, :], bk[:, 0:256], AF.Copy)
        self.ck('Bk%d' % l)
        trk.dma('sp', dr_['o_pk'][l], kst[:], 'st_bk')
        for c in range(2):
            self.wrel(bidx[(l, 'k', c)])
        wvv = [self.wget(bidx[(l, 'v', c)]) for c in range(2)]
        for mt in range(2):
            for cbk in range(2):
                bk = self.bank()
                self.mm(bk[:, :], [(memn[:, kc, mt * 128:(mt + 1) * 128], wvv[cbk][:, kc, :]) for kc in range(KC)])
                A(Vp[:, mt, cbk * 512:(cbk + 1) * 512], bk[:, :], AF.Copy)
                A(vst[:, mt, cbk * 512:(cbk + 1) * 512], bk[:, :], AF.Copy)
        trk.dma('sp', dr_['o_pv'][l].rearrange("(j p) d -> p j d", p=128), vst[:], 'st_bv')
        for c in range(2):
            self.wrel(bidx[(l, 'v', c)])
        self.ck('Ba%d' % l)
        R.ptr = mark
        qT2 = [R.alloc("b_qT%d" % i, [128, KC, NB], BF16) for i in range(2)]
        eT2 = [R.alloc("b_eT%d" % i, [128, 2, NB], BF16) for i in range(2)]
        rs2 = [R.alloc("b_rs%d" % i, [128, NB], F32) for i in range(2)]
        oT = R.alloc("b_oT", [128, KC, NB], BF16)
        qTs = R.alloc("b_qTs", [128, KC, NSM], BF16)
        oTs = R.alloc("b_oTs", [128, KC, NSM], BF16)
        esb = [R.alloc("b_esb%d" % i, [128, 32], BF16) for i in range(2)]
        ssb = R.alloc("b_ssb", [128, 32], F32)
        rsb = R.alloc("b_rsb", [128, 16], F32)
        wq = [self.wget(bidx[(l, 'q', c)]) for c in range(2)]
        wo = [self.wget(bidx[(l, 'o', c)]) for c in range(2)]

        def qproj(c0, N, qdst, xn=xn, do_norm=True):
            if do_norm:
                self.norm(lambda kc: X[:, kc, c0:c0 + N], lambda kc: pvc(PV_GATT + l * 8 + kc),
                          lambda kc: xn[:, kc, 0:N], N, sq, rstd)
            yield
            for oc in range(KC):
                bk = self.bank()
                self.mm(bk[:, 0:N], [(wq[oc // 4][:, kc, (oc % 4) * 128:(oc % 4 + 1) * 128], xn[:, kc, 0:N])
                                     for kc in range(KC)])
                self.ts(qdst[:, oc, 0:N], bk[:, 0:N], pvc(PV_ZERO), None, ADD)
                if oc == 3:
                    yield
            yield

        def oproj(c0, N, osrc):
            for oc in range(KC):
                bk = self.bank()
                self.mm(bk[:, 0:N], [(wo[oc // 4][:, kc, (oc % 4) * 128:(oc % 4 + 1) * 128], osrc[:, kc, 0:N])
                                     for kc in range(KC)])
                self.tt(X[:, oc, c0:c0 + N], X[:, oc, c0:c0 + N], bk[:, 0:N], ADD)

        def kv_load(b):
            trk.dma('pool', Ks[b % 2][:], dr_['KcT'][l, b].rearrange("(kc p) m -> p kc m", p=128), 'ld_k%d' % (b % 2))
            trk.dma('pool', Vs[b % 2][:], dr_['Vc'][l, b].rearrange("(j p) d -> p j d", p=128), 'ld_v%d' % (b % 2))

        def sample_batch(b):
            if b + 1 < 16:
                kv_load(b + 1)
            kb, vb, es = Ks[b % 2], Vs[b % 2], esb[b % 2]
            bsc = self.bank()
            groups = []
            for hh in range(4):
                for jm in range(2):
                    col = hh * 8 + jm * 4
                    groups.append((bsc[:, col:col + 4],
                                   [(kb[:, 2 * hh + dc, jm * 128:(jm + 1) * 128],
                                     qTs[:, 2 * hh + dc, b * 4:(b + 1) * 4]) for dc in range(2)]))
            self.mm_groups(groups)
            A(es[:, :], bsc[:, 0:32], AF.Exp, scale=1.0 / 16.0)
            bs = self.bank()
            self.mm(bs[:, 0:32], [(ones[:], es[:, :])])
            A(ssb[:, :], bs[:, 0:32], AF.Copy)
            sv = ssb[:, :].rearrange("p (h j t) -> p h j t", j=2, t=4)
            self.tt(rsb[:, :].rearrange("p (h t) -> p h t", t=4), sv[:, :, 0, :], sv[:, :, 1, :], ADD)
            self.recip(rsb[:, :], rsb[:, :])
            bo = self.bank()
            groups = []
            for hh in range(4):
                for dc in range(2):
                    col = (hh * 2 + dc) * 4
                    groups.append((bo[:, col:col + 4],
                                   [(vb[:, jm, hh * 256 + dc * 128:hh * 256 + (dc + 1) * 128],
                                     es[:, hh * 8 + jm * 4:hh * 8 + jm * 4 + 4]) for jm in range(2)]))
            self.mm_groups(groups)
            for dc in range(2):
                ov = oTs[:, :, b * 4:(b + 1) * 4].rearrange("p (h dc) t -> p dc h t", dc=2)[:, dc]
                iv = bo[:, 0:32].rearrange("p (h dc t) -> p dc h t", dc=2, t=4)[:, dc]
                rv = rsb[:, :].rearrange("p (h t) -> p h t", t=4)
                self.tt(ov, iv, rv, MUL)

        def scores(hh, N, qT):
            eT = eT2[hh % 2]
            for jm in range(2):
                bk = self.bank()
                self.mm(bk[:, 0:N], [(KpT[:, 2 * hh + dc, jm * 128:(jm + 1) * 128], qT[:, 2 * hh + dc, 0:N])
                                     for dc in range(2)])
                A(eT[:, jm, 0:N], bk[:, 0:N], AF.Exp, scale=1.0 / 16.0)

        def sum_pv(hh, N):
            eT, rs = eT2[hh % 2], rs2[hh % 2]
            bs = self.bank()
            self.mm(bs[:, 0:N], [(ones[:], eT[:, jm, 0:N]) for jm in range(2)])
            A(rs[:, 0:N], bs[:, 0:N], AF.Ln)
            A(rs[:, 0:N], rs[:, 0:N], AF.Exp, scale=-1.0)
            for dc in range(2):
                bo = self.bank()
                self.mm(bo[:, 0:N], [(Vp[:, jm, hh * 256 + dc * 128:hh * 256 + (dc + 1) * 128], eT[:, jm, 0:N])
                                     for jm in range(2)])
                self.tt(oT[:, 2 * hh + dc, 0:N], bo[:, 0:N], rs[:, 0:N], MUL)

        def drive(gens):
            gens = [g for g in gens if g is not None]
            while gens:
                alive = []
                for g in gens:
                    try:
                        next(g)
                        alive.append(g)
                    except StopIteration:
                        pass
                gens = alive

        ptiles = list(range(0, NPR, NB))

        def backB(ti):
            qT = qT2[ti % 2]
            scores(0, NB, qT)
            for hh in range(4):
                if hh + 1 < 4:
                    scores(hh + 1, NB, qT)
                sum_pv(hh, NB)
                sample_batch(ti * 4 + hh)
                yield
            oproj(ptiles[ti], NB, oT)
            yield

        kv_load(0)
        drive([qproj(NPR, NSM, qTs, xn=xn_s, do_norm=False)])
        drive([qproj(ptiles[0], NB, qT2[0], do_norm=False)])
        for ti in range(len(ptiles)):
            nxt = qproj(ptiles[ti + 1], NB, qT2[(ti + 1) % 2]) if ti + 1 < len(ptiles) else None
            drive([nxt, backB(ti)])
        oproj(NPR, NSM, oTs)
        for c in range(2):
            self.wrel(bidx[(l, 'q', c)])
        for c in range(2):
            self.wrel(bidx[(l, 'o', c)])

        self.ck('B%d' % l)
        R.reset()
        NC_ = 512
        o_xnf = R.ptr
        xnf = R.alloc("c_xn", [128, KC, T], BF16)
        hb = R.alloc("c_h", [128, 8, T], BF16)
        sq = R.alloc("c_sq", [128, KC, NC_], BF16)
        rstd = R.alloc("c_rstd", [128, NC_], F32)
        sgt = [R.alloc("c_sg%d" % i, [128, NC_], F32) for i in range(2)]
        tiles = [(c0, min(NC_, T - c0)) for c0 in range(0, T, NC_)]
        def cnorm(ti):
            c0, N = tiles[ti]
            self.norm(lambda kc: X[:, kc, c0:c0 + N], lambda kc: pvc(PV_GFFN + l * 8 + kc),
                      lambda kc: xnf[:, kc, c0:c0 + N], N, sq, rstd)
        cnorm(0)
        cnorm(1)
        normed = 2
        cnt = 0
        for g in range(3):
            nch = 8 if g < 2 else 6
            for hbk in range(2):
                k0 = hbk * 4
                k1 = min(nch, k0 + 4)
                if k1 <= k0:
                    continue
                wg = self.wget(bidx[(l, 'gate', g, hbk)])
                wu = self.wget(bidx[(l, 'up', g, hbk)])
                for fcl in range(k0, k1):
                    cc = (fcl - k0) * 128
                    for ti_, (c0, N) in enumerate(tiles):
                        if normed < len(tiles) and ti_ + 2 >= normed:
                            cnorm(normed)
                            normed += 1
                        bg = self.bank()
                        self.mm(bg[:, 0:N], [(wg[:, kc, cc:cc + 128], xnf[:, kc, c0:c0 + N]) for kc in range(KC)])
                        bu = self.bank()
                        self.mm(bu[:, 0:N], [(wu[:, kc, cc:cc + 128], xnf[:, kc, c0:c0 + N]) for kc in range(KC)])
                        st = sgt[cnt % 2]
                        cnt += 1
                        A(st[:, 0:N], bg[:, 0:N], AF.Silu)
                        self.tt(hb[:, fcl, c0:c0 + N], bu[:, 0:N], st[:, 0:N], MUL)
                self.wrel(bidx[(l, 'gate', g, hbk)])
                self.wrel(bidx[(l, 'up', g, hbk)])
            wd = [self.wget(bidx[(l, 'down', g, hbk)]) for hbk in range(2)]
            if g == 2 and l + 1 < L:
                o_keep = R.ptr
                R.ptr = 0
                cfD_n = R.alloc("cfD_n", [128, 4 * 31, 128], BF16)
                R.ptr = o_keep
                assert o_xnf == 0
                self.build_diag(l + 1, cfD_n, None, identf, pvc, do_cf=True, do_rg=False)
                self.cf_prebuilt.add(l + 1)
                self.cf_handle[l + 1] = cfD_n
            if l == L - 1 and g == 2:
                o_keep = R.ptr
                R.ptr = o_xnf
                ys = [R.alloc("f_y%d" % i, [128, KC, NC_], F32) for i in range(2)]
                R.ptr = o_keep
                yv = env['yv']
                for ti, (c0, N) in enumerate(tiles):
                    for oc in range(KC):
                        bk = self.bank()
                        self.mm(bk[:, 0:N], [(wd[kcl // 4][:, kcl % 4, oc * 128:(oc + 1) * 128],
                                              hb[:, kcl, c0:c0 + N]) for kcl in range(nch)])
                        self.tt(X[:, oc, c0:c0 + N], X[:, oc, c0:c0 + N], bk[:, 0:N], ADD)
                    y = ys[ti % 2]
                    self.norm(lambda kc: X[:, kc, c0:c0 + N], lambda kc: pvc(PV_GFIN + kc),
                              lambda kc: y[:, kc, 0:N], N, sq, rstd)
                    trk.dma('sp', yv[:, :, c0:c0 + N], y[:, :, 0:N], 'st_y%d' % (ti % 2))
                self.final_done = True
            else:
              for oc in range(KC):
                for (c0, N) in tiles:
                    bk = self.bank()
                    self.mm(bk[:, 0:N], [(wd[kcl // 4][:, kcl % 4, oc * 128:(oc + 1) * 128], hb[:, kcl, c0:c0 + N])
                                         for kcl in range(nch)])
                    self.tt(X[:, oc, c0:c0 + N], X[:, oc, c0:c0 + N], bk[:, 0:N], ADD)
            for hbk in range(2):
                self.wrel(bidx[(l, 'down', g, hbk)])

        self.ck('C%d' % l)

    def final_norm(self, env):
        trk = self.trk
        R = env['R']; X = env['X']; pv = env['pv']; yv = env['yv']
        R.reset()
        NF = 512
        sqs = [R.alloc("f_sq%d" % i, [128, KC, NF], BF16) for i in range(2)]
        rstds = [R.alloc("f_rstd%d" % i, [128, NF], F32) for i in range(2)]
        ys = [R.alloc("f_y%d" % i, [128, KC, NF], F32) for i in range(2)]
        for ti, c0 in enumerate(range(0, T, NF)):
            N = min(NF, T - c0)
            y = ys[ti % 2]
            self.norm(lambda kc: X[:, kc, c0:c0 + N], lambda kc: pv[:, PV_GFIN + kc:PV_GFIN + kc + 1],
                      lambda kc: y[:, kc, 0:N], N, sqs[ti % 2], rstds[ti % 2])
            trk.dma('sp', yv[:, :, c0:c0 + N], y[:, :, 0:N], 'st_y%d' % (ti % 2))


_CACHE = {}


def _get_nc():
    if 'nc' not in _CACHE:
        _CACHE['nc'] = Builder().build()
    return _CACHE['nc']


def _pack_pv(inp):
    pv = np.zeros((128, PV_N), np.float32)

    def feat(v):
        return np.ascontiguousarray(v.reshape(L, KC, 128).transpose(2, 0, 1)).reshape(128, L * KC)

    def chan(v):
        return np.ascontiguousarray(v.reshape(L, 4, 128).transpose(2, 0, 1)).reshape(128, L * 4)

    pv[:, PV_GMIX:PV_GMIX + 16] = feat(inp['norm_mix_g'])
    pv[:, PV_GATT:PV_GATT + 16] = feat(inp['norm_attn_g'])
    pv[:, PV_GFFN:PV_GFFN + 16] = feat(inp['norm_ffn_g'])
    pv[:, PV_GMEM:PV_GMEM + 16] = feat(inp['norm_mem_g'])
    pv[:, PV_GFIN:PV_GFIN + 8] = inp['norm_final_g'].reshape(KC, 128).T
    rgw = inp['rg_conv_w'].reshape(L, 4, 4, 128).transpose(3, 0, 2, 1)
    pv[:, PV_RGW:PV_RGW + 32] = np.ascontiguousarray(rgw).reshape(128, 32)
    pv[:, PV_RGB:PV_RGB + 8] = chan(inp['rg_conv_b'])
    pv[:, PV_BA:PV_BA + 8] = chan(inp['rg_ba'])
    pv[:, PV_BX:PV_BX + 8] = chan(inp['rg_bx'])
    pv[:, PV_LAM:PV_LAM + 8] = chan(inp['rg_lambda'])
    cfw = inp['cf_conv_w'].reshape(L, 31, 4, 128).transpose(3, 0, 2, 1)
    pv[:, PV_CFW:PV_CFW + 248] = np.ascontiguousarray(cfw).reshape(128, 248)
    pv[:, PV_CFB:PV_CFB + 8] = chan(inp['cf_conv_b'])
    pv[:, PV_LNG:PV_LNG + 8] = chan(inp['cf_ln_g'])
    pv[:, PV_LNB:PV_LNB + 8] = chan(inp['cf_ln_b'])
    return pv


def _prep(inp):
    inp = {k: np.asarray(v) for k, v in inp.items()}
    f32 = np.float32
    pv = _pack_pv(inp)
    shared = {
        'pv': pv,
        'rg_wa': np.ascontiguousarray(inp['rg_wa'], f32), 'rg_wx': np.ascontiguousarray(inp['rg_wx'], f32),
        'w_in': np.ascontiguousarray(inp['w_in'], f32), 'w_out': np.ascontiguousarray(inp['w_out'], f32),
        'w_q': np.ascontiguousarray(inp['w_q'], f32), 'w_k': np.ascontiguousarray(inp['w_k'], f32),
        'w_v': np.ascontiguousarray(inp['w_v'], f32), 'w_o': np.ascontiguousarray(inp['w_o'], f32),
        'w_gate': np.ascontiguousarray(inp['w_gate'], f32), 'w_up': np.ascontiguousarray(inp['w_up'], f32),
        'w_down': np.ascontiguousarray(inp['w_down'], f32),
    }
    in_maps = []
    for c in range(NCORES):
        sl = slice(16 * c, 16 * (c + 1))
        xs = inp['x_sample'][sl].reshape(NSM, D)
        xT = np.ascontiguousarray(np.concatenate([inp['x_prompt'][c].T, xs.T], axis=1), f32)
        memT = np.ascontiguousarray(inp['mem_prompt'][c].T, f32)
        KcT = np.ascontiguousarray(inp['cache_mem_k'][:, sl].reshape(L, 16, 256, D).transpose(0, 1, 3, 2), f32)
        Vc = np.ascontiguousarray(inp['cache_mem_v'][:, sl].reshape(L, 16, 256, D), f32)
        h0 = np.ascontiguousarray(inp['state_rglru_h'][:, sl].reshape(L, 16, 4, 128).transpose(0, 3, 2, 1), f32)
        rgs = np.ascontiguousarray(
            inp['state_rglru_conv'][:, sl].reshape(L, 16, 3, 4, 128).transpose(0, 4, 3, 1, 2), f32)
        cfs = np.ascontiguousarray(
            inp['state_conf_conv'][:, sl].reshape(L, 16, 30, 4, 128).transpose(0, 4, 3, 1, 2), f32)
        m = dict(shared)
        m.update({'xT': xT, 'memT': memT, 'KcT': KcT, 'Vc': Vc, 'h0': h0, 'rgs': rgs, 'cfs': cfs})
        in_maps.append(m)
    return in_maps


def kernel(**inp):
    nc = _get_nc()
    in_maps = _prep(inp)
    res = run_bass_kernel_spmd(nc, in_maps, core_ids=list(range(NCORES)))
    return _post(res.results)


def _post(rs):
    f32 = np.float32
    B = NCORES
    y_prompt = np.empty((B, NPR, D), f32)
    y_sample = np.empty((128, 4, D), f32)
    p_h = np.empty((L, B, 512), f32)
    p_rg = np.empty((L, B, 3, 512), f32)
    p_cf = np.empty((L, B, 30, 512), f32)
    p_mk = np.empty((L, B, 256, 4, 256), f32)
    p_mv = np.empty((L, B, 256, 4, 256), f32)
    s_h = np.empty((L, 128, 512), f32)
    s_rg = np.empty((L, 128, 3, 512), f32)
    s_cf = np.empty((L, 128, 30, 512), f32)
    for c in range(NCORES):
        r = rs[c]
        sl = slice(16 * c, 16 * (c + 1))
        yT = r['yT']
        y_prompt[c] = yT[:, :NPR].T
        y_sample[sl] = yT[:, NPR:].T.reshape(16, 4, D)
        p_h[:, c] = r['o_ph'].transpose(0, 2, 1).reshape(L, 512)
        p_rg[:, c] = r['o_prg'].transpose(0, 3, 2, 1).reshape(L, 3, 512)
        p_cf[:, c] = r['o_pcf'].transpose(0, 3, 2, 1).reshape(L, 30, 512)
        p_mk[:, c] = r['o_pk'].transpose(0, 3, 2, 1).reshape(L, 256, 4, 256)
        p_mv[:, c] = r['o_pv'].reshape(L, 256, 4, 256)
        s_h[:, sl] = r['o_sh'].transpose(0, 3, 2, 1).reshape(L, 16, 512)
        s_rg[:, sl] = r['o_srg'].transpose(0, 3, 4, 2, 1).reshape(L, 16, 3, 512)
        scf = np.concatenate([r['o_scfh'], r['o_scfn']], axis=4)
        s_cf[:, sl] = scf.transpose(0, 3, 4, 2, 1).reshape(L, 16, 30, 512)
    return (y_prompt, y_sample, p_h, p_rg, p_cf, p_mk, p_mv, s_h, s_rg, s_cf)
```
